# Optimizing a Trainium2 kernel written in Bass

```python
import jax, jax.numpy as jnp
from jax import lax
import numpy as np

D_MODEL = 1024
BATCH = 1
SEQ = 16384
DEPTH = 2

HEAD_DIM = 64
ROPE_DIM = HEAD_DIM // 4
ROPE_THETA = 500000.0
RMS_EPS = 1e-6
Q_BLOCK = 128
NEG_INF = -1e30
POS_BIG = 1e30

MOBA_HEADS = 6
MOBA_BLOCK = 256
MOBA_TOPK = 3

NSA_HEADS = 4
NSA_CMP_LEN = 32
NSA_CMP_STRIDE = 16
NSA_CMP_HIDDEN = 4 * HEAD_DIM
NSA_SLC_BLOCK = 64
NSA_SLC_TOPN = 16
NSA_WINDOW = 512

FOX_HEADS = 6
N_BRANCH = 3

MOBA_W = MOBA_HEADS * HEAD_DIM
NSA_W = NSA_HEADS * HEAD_DIM
FOX_W = FOX_HEADS * HEAD_DIM
IN_SPLITS = (MOBA_W,) * 4 + (NSA_W,) + (HEAD_DIM,) * 6 + (3 * NSA_HEADS, NSA_W) + (FOX_W,) * 3 + (FOX_HEADS, FOX_W, N_BRANCH * D_MODEL)
IN_COLS = sum(IN_SPLITS)
ATTN_SCALE = HEAD_DIM ** -0.5

kernel_name = "hybrid_moba_nsa_fox_gated"


def rms_norm(x, g):
    xf = x.astype(jnp.float32)
    y = xf * lax.rsqrt(jnp.mean(xf * xf, axis=-1, keepdims=True) + RMS_EPS)
    return (y * g.astype(jnp.float32)).astype(x.dtype)


def rope_tables(pos):
    inv = ROPE_THETA ** (-jnp.arange(0, ROPE_DIM, 2, dtype=jnp.float32) / ROPE_DIM)
    ang = pos.astype(jnp.float32)[:, None] * inv[None, :]
    return jnp.cos(ang), jnp.sin(ang)


def apply_partial_rope(x, cos, sin):
    half = ROPE_DIM // 2
    xr = x[..., :ROPE_DIM].astype(jnp.float32)
    x1, x2 = xr[..., :half], xr[..., half:]
    rot = jnp.concatenate([x1 * cos - x2 * sin, x2 * cos + x1 * sin], axis=-1)
    return jnp.concatenate([rot.astype(x.dtype), x[..., ROPE_DIM:]], axis=-1)


def masked_softmax(logits, mask):
    p = jax.nn.softmax(jnp.where(mask, logits.astype(jnp.float32), NEG_INF), axis=-1)
    return p * mask


def split_heads(t, n):
    B, S, _ = t.shape
    return t.reshape(B, S, n, HEAD_DIM).transpose(0, 2, 1, 3)


def merge_heads(t):
    B, H, S, dh = t.shape
    return t.transpose(0, 2, 1, 3).reshape(B, S, H * dh)


def sweep(fn, S):
    out = lax.map(fn, jnp.arange(S // Q_BLOCK))
    nq, B, H, Q, dh = out.shape
    return out.transpose(1, 2, 0, 3, 4).reshape(B, H, nq * Q, dh)


def moba_attention(q, k, v):
    B, H, S, dh = q.shape
    nb = -(-S // MOBA_BLOCK)
    pad = nb * MOBA_BLOCK - S
    kp = jnp.pad(k, ((0, 0), (0, 0), (0, pad), (0, 0)))
    vp = jnp.pad(v, ((0, 0), (0, 0), (0, pad), (0, 0)))
    kb = kp.reshape(B, H, nb, MOBA_BLOCK, dh)
    vb = vp.reshape(B, H, nb, MOBA_BLOCK, dh)
    kmean = jnp.mean(kb.astype(jnp.float32), axis=3).astype(k.dtype)
    n_sel = max(1, min(MOBA_TOPK, nb - 1))
    bi = jnp.arange(B)[:, None, None, None]
    hi = jnp.arange(H)[None, :, None, None]
    blk_ids = jnp.arange(nb)
    inner = jnp.arange(MOBA_BLOCK)

    def chunk(c):
        c0 = c * Q_BLOCK
        qc = lax.dynamic_slice_in_dim(q, c0, Q_BLOCK, axis=2)
        t = c0 + jnp.arange(Q_BLOCK)
        cur = t // MOBA_BLOCK
        gate = jnp.einsum('bhqd,bhnd->bhqn', qc, kmean).astype(jnp.float32)
        gate = jnp.where(blk_ids[None, :] < cur[:, None], gate, NEG_INF)
        _, idx = lax.top_k(gate, n_sel)
        sel_ok = jnp.arange(n_sel)[None, :] < cur[:, None]
        ks = kb[bi, hi, idx]
        vs = vb[bi, hi, idx]
        s_sel = jnp.einsum('bhqd,bhqnkd->bhqnk', qc, ks) * ATTN_SCALE
        m_sel = jnp.broadcast_to(sel_ok[:, :, None], (Q_BLOCK, n_sel, MOBA_BLOCK))
        own0 = (c0 // MOBA_BLOCK) * MOBA_BLOCK
        ko = lax.dynamic_slice_in_dim(kp, own0, MOBA_BLOCK, axis=2)
        vo = lax.dynamic_slice_in_dim(vp, own0, MOBA_BLOCK, axis=2)
        s_own = jnp.einsum('bhqd,bhkd->bhqk', qc, ko) * ATTN_SCALE
        m_own = (own0 + inner)[None, :] <= t[:, None]
        n_k = n_sel * MOBA_BLOCK
        logits = jnp.concatenate([s_sel.reshape(B, H, Q_BLOCK, n_k), s_own], axis=-1)
        mask = jnp.concatenate([m_sel.reshape(Q_BLOCK, n_k), m_own], axis=-1)
        p = masked_softmax(logits, mask).astype(v.dtype)
        p_sel = p[..., :n_k].reshape(B, H, Q_BLOCK, n_sel, MOBA_BLOCK)
        return (jnp.einsum('bhqnk,bhqnkd->bhqd', p_sel, vs)
                + jnp.einsum('bhqk,bhkd->bhqd', p[..., n_k:], vo))

    return sweep(chunk, S)


def nsa_compress(t, gidx, pe, w1, w2):
    B = t.shape[0]
    blocks = t[:, gidx] + pe
    flat = blocks.reshape(B, gidx.shape[0], NSA_CMP_LEN * HEAD_DIM)
    return jax.nn.silu(flat @ w1) @ w2


def nsa_attention(q, k_cmp, v_cmp, cmp_end, k_slc, v_slc, k_win, v_win, gates):
    B, H, S, dh = q.shape
    ns = S // NSA_SLC_BLOCK
    n_top = min(NSA_SLC_TOPN, ns)
    cmp_start = cmp_end - (NSA_CMP_LEN - 1)
    slc_start = jnp.arange(ns) * NSA_SLC_BLOCK
    overlap = ((cmp_start[:, None] <= slc_start[None, :] + NSA_SLC_BLOCK - 1)
               & (cmp_end[:, None] >= slc_start[None, :])).astype(jnp.float32)
    ksb = k_slc.reshape(B, ns, NSA_SLC_BLOCK, dh)
    vsb = v_slc.reshape(B, ns, NSA_SLC_BLOCK, dh)
    kwp = jnp.pad(k_win, ((0, 0), (NSA_WINDOW, 0), (0, 0)))
    vwp = jnp.pad(v_win, ((0, 0), (NSA_WINDOW, 0), (0, 0)))
    bi = jnp.arange(B)[:, None, None]
    blk = jnp.arange(ns)
    inner = jnp.arange(NSA_SLC_BLOCK)

    def chunk(c):
        c0 = c * Q_BLOCK
        qc = lax.dynamic_slice_in_dim(q, c0, Q_BLOCK, axis=2)
        t = c0 + jnp.arange(Q_BLOCK)
        s_c = jnp.einsum('bhqd,bnd->bhqn', qc, k_cmp) * ATTN_SCALE
        m_c = cmp_end[None, :] <= t[:, None]
        p_c = masked_softmax(s_c, m_c)
        o_c = jnp.einsum('bhqn,bnd->bhqd', p_c.astype(v_cmp.dtype), v_cmp)
        imp = jnp.einsum('bhqn,nm->bqm', p_c, overlap)
        cur = t // NSA_SLC_BLOCK
        score = jnp.where(blk[None, :] <= cur[:, None], imp, NEG_INF)
        forced = (blk[None, :] == 0) | (blk[None, :] == cur[:, None]) | (blk[None, :] == cur[:, None] - 1)
        score = jnp.where(forced, POS_BIG, score)
        _, idx = lax.top_k(score, n_top)
        ok = jnp.arange(n_top)[None, :] < (cur + 1)[:, None]
        ks = ksb[bi, idx]
        vs = vsb[bi, idx]
        s_s = jnp.einsum('bhqd,bqnkd->bhqnk', qc, ks) * ATTN_SCALE
        kpos = idx[..., None] * NSA_SLC_BLOCK + inner
        m_s = (kpos <= t[None, :, None, None]) & ok[None, :, :, None]
        n_k = n_top * NSA_SLC_BLOCK
        p_s = masked_softmax(s_s.reshape(B, H, Q_BLOCK, n_k), m_s.reshape(B, 1, Q_BLOCK, n_k))
        o_s = jnp.einsum('bhqnk,bqnkd->bhqd', p_s.reshape(B, H, Q_BLOCK, n_top, NSA_SLC_BLOCK).astype(v_slc.dtype), vs)
        kw = lax.dynamic_slice_in_dim(kwp, c0, NSA_WINDOW + Q_BLOCK, axis=1)
        vw = lax.dynamic_slice_in_dim(vwp, c0, NSA_WINDOW + Q_BLOCK, axis=1)
        wpos = c0 - NSA_WINDOW + jnp.arange(NSA_WINDOW + Q_BLOCK)
        diff = t[:, None] - wpos[None, :]
        m_w = (diff >= 0) & (diff < NSA_WINDOW) & (wpos[None, :] >= 0)
        s_w = jnp.einsum('bhqd,bkd->bhqk', qc, kw) * ATTN_SCALE
        o_w = jnp.einsum('bhqk,bkd->bhqd', masked_softmax(s_w, m_w).astype(v_win.dtype), vw)
        gc = lax.dynamic_slice_in_dim(gates, c0, Q_BLOCK, axis=2)
        return gc[..., 0:1] * o_c + gc[..., 1:2] * o_s + gc[..., 2:3] * o_w

    return sweep(chunk, S)


def fox_attention(q, k, v, log_f):
    B, H, S, dh = q.shape
    csum = jnp.cumsum(log_f, axis=-1)
    kpos = jnp.arange(S)

    def chunk(c):
        c0 = c * Q_BLOCK
        qc = lax.dynamic_slice_in_dim(q, c0, Q_BLOCK, axis=2)
        cq = lax.dynamic_slice_in_dim(csum, c0, Q_BLOCK, axis=2)
        t = c0 + jnp.arange(Q_BLOCK)
        s = (jnp.einsum('bhqd,bhkd->bhqk', qc, k).astype(jnp.float32) * ATTN_SCALE
             + cq[..., None] - csum[:, :, None, :])
        p = masked_softmax(s, kpos[None, :] <= t[:, None])
        return jnp.einsum('bhqk,bhkd->bhqd', p.astype(v.dtype), v)

    return sweep(chunk, S)


def hybrid_layer(x, norm_g, w_in, b_f, b_gate, moba_qk_g, nsa_q_g, nsa_k_g, fox_qk_g,
                 cmp_pe, cmp_w1, cmp_w2, w_up_moba, w_up_nsa, w_up_fox, w_out):
    B, S, _ = x.shape
    h = rms_norm(x, norm_g)
    proj = h @ w_in
    offsets = np.cumsum(IN_SPLITS)[:-1].tolist()
    (mq, mk, mv, mz, nq, kc, vc, ksl, vsl, kw, vw, ng, nz,
     fq, fk, fv, ff, fz, gl) = jnp.split(proj, offsets, axis=-1)
    pos = jnp.arange(S)
    cos, sin = rope_tables(pos)

    q = apply_partial_rope(rms_norm(split_heads(mq, MOBA_HEADS), moba_qk_g[0]), cos, sin)
    k = apply_partial_rope(rms_norm(split_heads(mk, MOBA_HEADS), moba_qk_g[1]), cos, sin)
    o_moba = merge_heads(moba_attention(q, k, split_heads(mv, MOBA_HEADS)))
    y_moba = (o_moba * jax.nn.silu(mz)) @ w_up_moba

    q = apply_partial_rope(rms_norm(split_heads(nq, NSA_HEADS), nsa_q_g), cos, sin)
    nc = (S - NSA_CMP_LEN) // NSA_CMP_STRIDE + 1
    starts = jnp.arange(nc) * NSA_CMP_STRIDE
    gidx = starts[:, None] + jnp.arange(NSA_CMP_LEN)[None, :]
    cmp_end = starts + NSA_CMP_LEN - 1
    k_cmp = nsa_compress(kc, gidx, cmp_pe[0], cmp_w1[0], cmp_w2[0])
    v_cmp = nsa_compress(vc, gidx, cmp_pe[1], cmp_w1[1], cmp_w2[1])
    cos_c, sin_c = rope_tables(cmp_end)
    k_cmp = apply_partial_rope(rms_norm(k_cmp, nsa_k_g[0]), cos_c, sin_c)
    ksl = apply_partial_rope(rms_norm(ksl, nsa_k_g[1]), cos, sin)
    kw = apply_partial_rope(rms_norm(kw, nsa_k_g[2]), cos, sin)
    gates = jax.nn.sigmoid(ng.reshape(B, S, NSA_HEADS, 3).transpose(0, 2, 1, 3))
    o_nsa = merge_heads(nsa_attention(q, k_cmp, v_cmp, cmp_end, ksl, vsl, kw, vw, gates))
    y_nsa = (o_nsa * jax.nn.silu(nz)) @ w_up_nsa

    q = rms_norm(split_heads(fq, FOX_HEADS), fox_qk_g[0])
    k = rms_norm(split_heads(fk, FOX_HEADS), fox_qk_g[1])
    log_f = jax.nn.log_sigmoid((ff + b_f).astype(jnp.float32)).transpose(0, 2, 1)
    o_fox = merge_heads(fox_attention(q, k, split_heads(fv, FOX_HEADS), log_f))
    y_fox = (o_fox * jax.nn.silu(fz)) @ w_up_fox

    g = jax.nn.sigmoid(gl + b_gate).reshape(B, S, N_BRANCH, D_MODEL)
    merged = g[:, :, 0] * y_moba + g[:, :, 1] * y_nsa + g[:, :, 2] * y_fox
    return x + merged @ w_out


def setup_inputs(seed: int = 0) -> dict:
    key = jax.random.key(seed)
    ks = jax.random.split(key, 16)
    f32 = jnp.float32

    def nrm(k, shape, scale):
        return jax.random.normal(k, shape, f32) * scale

    cmp_in = NSA_CMP_LEN * HEAD_DIM
    return {
        'x': nrm(ks[0], (BATCH, SEQ, D_MODEL), 1.0),
        'norm_g': 1.0 + nrm(ks[1], (DEPTH, D_MODEL), 0.02),
        'w_in': nrm(ks[2], (DEPTH, D_MODEL, IN_COLS), D_MODEL ** -0.5),
        'b_f': 3.0 + nrm(ks[3], (DEPTH, FOX_HEADS), 0.5),
        'b_gate': nrm(ks[4], (DEPTH, N_BRANCH * D_MODEL), 0.01),
        'moba_qk_g': 1.0 + nrm(ks[5], (DEPTH, 2, HEAD_DIM), 0.02),
        'nsa_q_g': 1.0 + nrm(ks[6], (DEPTH, HEAD_DIM), 0.02),
        'nsa_k_g': 1.0 + nrm(ks[7], (DEPTH, 3, HEAD_DIM), 0.02),
        'fox_qk_g': 1.0 + nrm(ks[8], (DEPTH, 2, HEAD_DIM), 0.02),
        'cmp_pe': nrm(ks[9], (DEPTH, 2, NSA_CMP_LEN, HEAD_DIM), 0.1),
        'cmp_w1': nrm(ks[10], (DEPTH, 2, cmp_in, NSA_CMP_HIDDEN), cmp_in ** -0.5),
        'cmp_w2': nrm(ks[11], (DEPTH, 2, NSA_CMP_HIDDEN, HEAD_DIM), NSA_CMP_HIDDEN ** -0.5),
        'w_up_moba': nrm(ks[12], (DEPTH, MOBA_W, D_MODEL), MOBA_W ** -0.5),
        'w_up_nsa': nrm(ks[13], (DEPTH, NSA_W, D_MODEL), NSA_W ** -0.5),
        'w_up_fox': nrm(ks[14], (DEPTH, FOX_W, D_MODEL), FOX_W ** -0.5),
        'w_out': nrm(ks[15], (DEPTH, D_MODEL, D_MODEL), D_MODEL ** -0.5),
    }


def reference(x, norm_g, w_in, b_f, b_gate, moba_qk_g, nsa_q_g, nsa_k_g, fox_qk_g,
              cmp_pe, cmp_w1, cmp_w2, w_up_moba, w_up_nsa, w_up_fox, w_out):
    for l in range(DEPTH):
        x = hybrid_layer(x, norm_g[l], w_in[l], b_f[l], b_gate[l], moba_qk_g[l], nsa_q_g[l],
                         nsa_k_g[l], fox_qk_g[l], cmp_pe[l], cmp_w1[l], cmp_w2[l],
                         w_up_moba[l], w_up_nsa[l], w_up_fox[l], w_out[l])
    return x
```

```python
import numpy as np
import ml_dtypes
from contextlib import ExitStack
import concourse.bass as bass
import concourse.mybir as mybir
from concourse.bass_utils import run_bass_kernel_spmd

F32 = mybir.dt.float32
BF16 = mybir.dt.bfloat16
AF = mybir.ActivationFunctionType
ALU = mybir.AluOpType
AX = mybir.AxisListType
NPBF = ml_dtypes.bfloat16

NCORES = 8
S_LEN = 16384
DM = 1024
HD = 64
EPS = 1e-6
SCALE = 0.125
NEGM = -30000.0


class Buf:
    def __init__(self, t, name):
        self.t = t
        self.name = name
        self.w = None
        self.r = []
        self.dsem = None
        self.dcnt = 0

    def __getitem__(self, k):
        return self.t[k]


class Sched:
    ENG = ("pe", "act", "dve", "pool", "sp")

    def __init__(self, nc, stack):
        self.nc = nc
        self.stack = stack
        self.sem_stack = stack
        self.ops = {e: [] for e in self.ENG}
        self.sem = {e: stack.enter_context(nc.semaphore("S_" + e)) for e in self.ENG}
        self.seq = {e: 0 for e in self.ENG}
        self.known = {e: {} for e in self.ENG}
        self.semobjs = {}
        self.dma_tokens = []
        self.nb = 0

    def sb(self, shape, dt, name=None):
        self.nb += 1
        name = (name or "sb") + f"_{self.nb}"
        t = self.stack.enter_context(self.nc.sbuf_tensor(name, list(shape), dt))
        return Buf(t, name)

    def ps(self, shape, dt, name=None):
        self.nb += 1
        name = (name or "ps") + f"_{self.nb}"
        t = self.stack.enter_context(self.nc.psum_tensor(name, list(shape), dt))
        return Buf(t, name)

    def _dsem(self, b):
        if b.dsem is None:
            b.dsem = self.sem_stack.enter_context(self.nc.semaphore("D_" + b.name))
        return b.dsem

    def _waits(self, eng, reads, writes):
        toks = []
        for b in list(reads) + list(writes):
            if b.w is not None:
                toks.append(b.w)
        for b in writes:
            toks.extend(b.r)
        best = {}
        for (s, v) in toks:
            k = id(s)
            self.semobjs[k] = s
            if v > best.get(k, 0):
                best[k] = v
        kn = self.known[eng]
        for k, v in best.items():
            if eng == "pe" and self.semobjs[k] is self.sem["pe"]:
                continue
            if kn.get(k, 0) >= v:
                continue
            kn[k] = v
            self.ops[eng].append(("wait", self.semobjs[k], v))

    def op(self, eng, fn, reads=(), writes=(), signal=True):
        self._waits(eng, reads, writes)
        tok = (self.sem[eng], self.seq[eng] + 1)
        if signal:
            self.seq[eng] += 1
        self.ops[eng].append(("op", fn, self.sem[eng] if signal else None, 1))
        for b in reads:
            b.r.append(tok)
        for b in writes:
            b.w = tok
            b.r = []
        return tok

    def dma(self, q, out_ap, in_ap, reads=(), writes=(), **kw):
        self._waits(q, reads, writes)
        owner = (list(writes) + list(reads))[0]
        s = self._dsem(owner)
        owner.dcnt += 16
        tok = (s, owner.dcnt)
        self.ops[q].append(("op", (lambda e, o=out_ap, i=in_ap, kw=kw: e.dma_start(out=o, in_=i, **kw)), s, 16))
        for b in reads:
            b.r.append(tok)
        for b in writes:
            b.w = tok
            b.r = []
        self.dma_tokens.append(tok)
        return tok

    def barrier(self):
        best = {}
        for (s_, v) in self.dma_tokens:
            k = id(s_)
            self.semobjs[k] = s_
            best[k] = max(best.get(k, 0), v)
        for e in self.ENG:
            if self.seq[e] > 0:
                k = id(self.sem[e])
                self.semobjs[k] = self.sem[e]
                best[k] = self.seq[e]
        for e in self.ENG:
            kn = self.known[e]
            for k, v in best.items():
                if self.semobjs[k] is self.sem[e]:
                    continue
                if kn.get(k, 0) >= v:
                    continue
                kn[k] = v
                self.ops[e].append(("wait", self.semobjs[k], v))

    def finish(self):
        nc = self.nc
        best = {}
        for (s, v) in self.dma_tokens:
            k = id(s)
            self.semobjs[k] = s
            best[k] = max(best.get(k, 0), v)
        for e in self.ENG:
            if e != "sp" and self.seq[e] > 0:
                k = id(self.sem[e])
                self.semobjs[k] = self.sem[e]
                best[k] = self.seq[e]
        for k, v in best.items():
            self.ops["sp"].append(("wait", self.semobjs[k], v))
        self._emit_block()

    def finish_part(self):
        self._emit_block()
        self.ops = {e: [] for e in self.ENG}

    def _emit_block(self):
        nc = self.nc
        ops = self.ops

        def replay(e, lst):
            for it in lst:
                if it[0] == "wait":
                    e.wait_ge(it[1], it[2])
                else:
                    ins = it[1](e)
                    if it[2] is not None:
                        ins.then_inc(it[2], it[3])

        with nc.Block() as block:
            @block.tensor
            def _(e):
                replay(e, ops["pe"])

            @block.scalar
            def _(e):
                replay(e, ops["act"])

            @block.vector
            def _(e):
                replay(e, ops["dve"])

            @block.gpsimd
            def _(e):
                replay(e, ops["pool"])

            @block.sync
            def _(e):
                replay(e, ops["sp"])


class Ring:
    def __init__(self, bufs):
        self.bufs = bufs
        self.i = 0

    def next(self):
        b = self.bufs[self.i % len(self.bufs)]
        self.i += 1
        return b


def dram_in(nc, name, shape, dt):
    return nc.dram_tensor(name, list(shape), dt, kind="ExternalInput").ap()


def dram_out(nc, name, shape, dt):
    return nc.dram_tensor(name, list(shape), dt, kind="ExternalOutput").ap()


TOK = S_LEN // NCORES
NTT = TOK // 128
P1_TILES = [("A", 512), ("A", 512), ("A", 128), ("B", 512), ("B", 256), ("C", 512), ("C", 512), ("F", 18)]
P1_NCOLS = sum(w for _, w in P1_TILES)
P5_TILES = [("D", 512), ("D", 512)] + [("E", 512)] * 6
P5_NCOLS = 4096


def col_ranges():
    sp = [384] * 4 + [256] + [64] * 6 + [12, 256] + [384] * 3 + [6, 384, 3072]
    names = ["mq", "mk", "mv", "mz", "nq", "kc", "vc", "ksl", "vsl", "kw", "vw", "ng", "nz", "fq", "fk", "fv", "ff", "fz", "gl"]
    off = np.concatenate([[0], np.cumsum(sp)])
    return {n: np.arange(off[i], off[i + 1]) for i, n in enumerate(names)}


def p1_col_perm():
    rng = col_ranges()
    return np.concatenate([rng[n] for n in ["mq", "mk", "nq", "ksl", "kw", "fq", "fk", "mv", "fv", "kc", "vc", "vsl", "vw", "ng", "ff"]])


def p5_col_perm():
    rng = col_ranges()
    return np.concatenate([rng[n] for n in ["mz", "nz", "fz", "gl"]])


def emit_xg(S, x, gnorm, ID, EPSC, PST):
    XG = S.sb([128, 8, TOK], BF16, "XG")
    RSTD = S.sb([128, NTT], F32, "RSTD")
    GREP = S.sb([128, DM], F32, "GREP")
    S.dma("pool", GREP[:], gnorm.partition_broadcast(128), writes=[GREP])
    XS = Ring([S.sb([128, DM], F32, "XS") for _ in range(2)])
    XB = Ring([S.sb([128, DM], BF16, "XB") for _ in range(2)])
    JUNK = S.sb([128, DM], BF16, "JUNK")
    SSQ = S.sb([128, NTT], F32, "SSQ")
    for tt in range(NTT):
        b = XS.next(); xb = XB.next()
        S.dma("sp", b[:], x[tt * 128:(tt + 1) * 128, :], writes=[b])
        S.op("act", lambda e, b=b, tt=tt: e.activation(out=JUNK[:], in_=b[:], func=AF.Square, accum_out=SSQ[:, tt:tt + 1]), reads=[b], writes=[JUNK, SSQ])
        S.op("pool", lambda e, b=b, xb=xb: e.tensor_tensor(out=xb[:], in0=b[:], in1=GREP[:], op=ALU.mult), reads=[b, GREP], writes=[xb])
        for c in range(8):
            pt = PST.next()
            S.op("pe", lambda e, pt=pt, xb=xb, c=c: e.matmul(pt[:], lhsT=xb[:, c * 128:(c + 1) * 128], rhs=ID[:], start=True, stop=True), reads=[xb, ID], writes=[pt])
            if c % 2 == 0:
                S.op("act", lambda e, pt=pt, c=c, tt=tt: e.activation(out=XG[:, c, tt * 128:(tt + 1) * 128], in_=pt[:], func=AF.Copy), reads=[pt], writes=[XG])
            else:
                S.op("dve", lambda e, pt=pt, c=c, tt=tt: e.tensor_copy(out=XG[:, c, tt * 128:(tt + 1) * 128], in_=pt[:]), reads=[pt], writes=[XG])
    S.op("act", lambda e: e.activation(out=SSQ[:], in_=SSQ[:], func=AF.Sqrt, bias=EPSC[:], scale=1.0 / DM), reads=[SSQ, EPSC], writes=[SSQ])
    S.op("dve", lambda e: e.reciprocal(out=RSTD[:], in_=SSQ[:]), reads=[SSQ], writes=[RSTD])
    return XG, RSTD


def proj_loop(S, XG, RSTD, w, tiles, sinks, EPSC, BIAS=None, GAINS=None, CS=None, kmean=None, nwbuf=2):
    WF = Ring([S.sb([128, 8, 512], F32, "WF") for _ in range(nwbuf)])
    WB = Ring([S.sb([128, 8, 512], BF16, "WB") for _ in range(2)])
    PS = Ring([S.ps([128, 512], F32, "PS") for _ in range(4)])
    kinds = set(k for k, _ in tiles)
    OF = Ring([S.sb([128, 512], F32, "OF") for _ in range(3)]) if kinds & {"D", "E", "F"} else None
    TMP = Ring([S.sb([128, 512], F32, "TMP") for _ in range(2)]) if kinds & {"E", "F"} else None
    if kinds & {"A", "B", "C"}:
        Y = Ring([S.sb([128, 512], F32, "Y") for _ in range(2)])
        SQ = Ring([S.sb([128, 512], F32, "SQ") for _ in range(2)])
        RS = Ring([S.sb([128, 16], F32, "RS") for _ in range(2)])
        RT = Ring([S.sb([128, 4, 8, 8], F32, "RT") for _ in range(2)])
        OBF = Ring([S.sb([128, 512], BF16, "OBF") for _ in range(3)])
    c0 = 0
    kcol = {k: 0 for k in "ABCDEF"}
    for ti, (kind, wd) in enumerate(tiles):
        wf = WF.next()
        wb = WB.next()
        S.dma("sp" if ti % 2 == 0 else "pool", wf[:, :, 0:wd], w[:, c0:c0 + wd].rearrange("(c p) n -> p c n", p=128), writes=[wf])
        for half in range(2):
            eng = "pool" if half == 0 else "dve"
            S.op(eng, lambda e, wf=wf, wb=wb, half=half, wd=wd: e.tensor_copy(out=wb[:, 4 * half:4 * half + 4, 0:wd], in_=wf[:, 4 * half:4 * half + 4, 0:wd]),
                 reads=[wf], writes=[wb])
        k0 = kcol[kind]
        for tt in range(NTT):
            ps = PS.next()
            for c in range(8):
                S.op("pe", lambda e, ps=ps, wb=wb, c=c, tt=tt, wd=wd: e.matmul(ps[:, 0:wd], lhsT=XG[:, c, tt * 128:(tt + 1) * 128], rhs=wb[:, c, 0:wd],
                                                                               start=(c == 0), stop=(c == 7)),
                     reads=[XG, wb], writes=[ps], signal=(c == 7))
            rs_t = RSTD[:, tt:tt + 1]
            rows = slice(tt * 128, (tt + 1) * 128)
            if kind == "C":
                o = OBF.next()
                S.op("act", lambda e, o=o, ps=ps, rs_t=rs_t, wd=wd: e.activation(out=o[:, 0:wd], in_=ps[:, 0:wd], func=AF.Copy, scale=rs_t),
                     reads=[ps, RSTD], writes=[o])
                S.dma("sp", sinks["C"][rows, k0:k0 + wd], o[:, 0:wd], reads=[o])
            elif kind == "D":
                o = OF.next()
                S.op("act", lambda e, o=o, ps=ps, rs_t=rs_t, wd=wd: e.activation(out=o[:, 0:wd], in_=ps[:, 0:wd], func=AF.Silu, scale=rs_t),
                     reads=[ps, RSTD], writes=[o])
                S.dma("sp", sinks["D"][rows, k0:k0 + wd], o[:, 0:wd], reads=[o])
            elif kind == "E":
                t = TMP.next()
                o = OF.next()
                S.op("dve", lambda e, t=t, ps=ps, rs_t=rs_t, k0=k0, wd=wd: e.scalar_tensor_tensor(out=t[:, 0:wd], in0=ps[:, 0:wd], scalar=rs_t, in1=BIAS[:, k0:k0 + wd],
                                                                                               op0=ALU.mult, op1=ALU.add),
                     reads=[ps, RSTD, BIAS], writes=[t])
                S.op("act", lambda e, o=o, t=t, wd=wd: e.activation(out=o[:, 0:wd], in_=t[:, 0:wd], func=AF.Sigmoid), reads=[t], writes=[o])
                S.dma("sp", sinks["E"][rows, k0:k0 + wd], o[:, 0:wd], reads=[o])
            elif kind == "F":
                t = TMP.next()
                o = OF.next()
                S.op("act", lambda e, o=o, ps=ps, rs_t=rs_t: e.activation(out=o[:, 0:12], in_=ps[:, 0:12], func=AF.Sigmoid, scale=rs_t),
                     reads=[ps, RSTD], writes=[o])
                S.op("dve", lambda e, t=t, ps=ps, rs_t=rs_t: e.scalar_tensor_tensor(out=t[:, 0:6], in0=ps[:, 12:18], scalar=rs_t, in1=BIAS[:, 0:6],
                                                                                 op0=ALU.mult, op1=ALU.add),
                     reads=[ps, RSTD, BIAS], writes=[t])
                S.op("act", lambda e, t=t: e.activation(out=t[:, 8:14], in_=t[:, 0:6], func=AF.Exp, scale=-1.0), reads=[t], writes=[t])
                S.op("act", lambda e, t=t: e.activation(out=t[:, 16:22], in_=t[:, 8:14], func=AF.Ln, bias=1.0), reads=[t], writes=[t])
                S.op("dve", lambda e, t=t, o=o: e.tensor_scalar(out=o[:, 12:18], in0=t[:, 16:22], scalar1=-1.0, scalar2=None, op0=ALU.mult),
                     reads=[t], writes=[o])
                S.dma("sp", sinks["F"][rows, :], o[:, 0:18], reads=[o])
            else:
                nh = wd // 64
                y = Y.next(); sq = SQ.next(); rs = RS.next(); o = OBF.next()
                goff = k0 if kind == "A" else 1152 + k0
                S.op("act", lambda e, y=y, ps=ps, rs_t=rs_t, wd=wd: e.activation(out=y[:, 0:wd], in_=ps[:, 0:wd], func=AF.Copy, scale=rs_t),
                     reads=[ps, RSTD], writes=[y])
                S.op("pool", lambda e, y=y, sq=sq, wd=wd: e.tensor_tensor(out=sq[:, 0:wd], in0=y[:, 0:wd], in1=y[:, 0:wd], op=ALU.mult),
                     reads=[y], writes=[sq])
                S.op("dve", lambda e, sq=sq, rs=rs, nh=nh, wd=wd: e.tensor_reduce(out=rs[:, 0:nh], in_=sq[:, 0:wd].rearrange("p (h d) -> p h d", d=64), axis=AX.X, op=ALU.add),
                     reads=[sq], writes=[rs])
                S.op("act", lambda e, rs=rs, nh=nh: e.activation(out=rs[:, 0:nh], in_=rs[:, 0:nh], func=AF.Sqrt, bias=EPSC[:], scale=1.0 / 64), reads=[rs, EPSC], writes=[rs])
                S.op("dve", lambda e, rs=rs, nh=nh: e.reciprocal(out=rs[:, 0:nh], in_=rs[:, 0:nh]), reads=[rs], writes=[rs])
                S.op("dve", lambda e, y=y, rs=rs, nh=nh, wd=wd: e.tensor_tensor(out=y[:, 0:wd].rearrange("p (h d) -> p h d", d=64), in0=y[:, 0:wd].rearrange("p (h d) -> p h d", d=64),
                                                                              in1=rs[:, 0:nh].unsqueeze(2).to_broadcast([128, nh, 64]), op=ALU.mult),
                     reads=[y, rs], writes=[y])
                if kind == "B":
                    S.op("pool", lambda e, y=y, o=o, goff=goff, wd=wd: e.tensor_tensor(out=o[:, 0:wd], in0=y[:, 0:wd], in1=GAINS[:, goff:goff + wd], op=ALU.mult),
                         reads=[y, GAINS], writes=[o])
                    S.dma("sp", sinks["B"][rows, k0:k0 + wd], o[:, 0:wd], reads=[o])
                else:
                    rt = RT.next()
                    S.op("pool", lambda e, y=y, goff=goff, wd=wd: e.tensor_tensor(out=y[:, 0:wd], in0=y[:, 0:wd], in1=GAINS[:, goff:goff + wd], op=ALU.mult),
                         reads=[y, GAINS], writes=[y])
                    yv = y[:, 0:wd].rearrange("p (h d) -> p h d", d=64)
                    ov = o[:, 0:wd].rearrange("p (h d) -> p h d", d=64)
                    cosb = CS[:, 0, tt, :].unsqueeze(1).to_broadcast([128, nh, 8])
                    sinb = CS[:, 1, tt, :].unsqueeze(1).to_broadcast([128, nh, 8])
                    S.op("act", lambda e, o=o, y=y, wd=wd: e.activation(out=o[:, 0:wd], in_=y[:, 0:wd], func=AF.Copy), reads=[y], writes=[o])
                    S.op("dve", lambda e, rt=rt, yv=yv, cosb=cosb, nh=nh: e.tensor_tensor(out=rt[:, 0, 0:nh, :], in0=yv[:, :, 0:8], in1=cosb, op=ALU.mult), reads=[y, CS], writes=[rt])
                    S.op("dve", lambda e, rt=rt, yv=yv, sinb=sinb, nh=nh: e.tensor_tensor(out=rt[:, 1, 0:nh, :], in0=yv[:, :, 8:16], in1=sinb, op=ALU.mult), reads=[y, CS], writes=[rt])
                    S.op("dve", lambda e, rt=rt, yv=yv, cosb=cosb, nh=nh: e.tensor_tensor(out=rt[:, 2, 0:nh, :], in0=yv[:, :, 8:16], in1=cosb, op=ALU.mult), reads=[y, CS], writes=[rt])
                    S.op("dve", lambda e, rt=rt, yv=yv, sinb=sinb, nh=nh: e.tensor_tensor(out=rt[:, 3, 0:nh, :], in0=yv[:, :, 0:8], in1=sinb, op=ALU.mult), reads=[y, CS], writes=[rt])
                    S.op("dve", lambda e, rt=rt, ov=ov, nh=nh: e.tensor_tensor(out=ov[:, :, 0:8], in0=rt[:, 0, 0:nh, :], in1=rt[:, 1, 0:nh, :], op=ALU.subtract), reads=[rt], writes=[o])
                    S.op("dve", lambda e, rt=rt, ov=ov, nh=nh: e.tensor_tensor(out=ov[:, :, 8:16], in0=rt[:, 2, 0:nh, :], in1=rt[:, 3, 0:nh, :], op=ALU.add), reads=[rt], writes=[o])
                    if kmean is not None:
                        KMP, C256 = kmean
                        for hc in range(0, wd, 64):
                            gc = k0 + hc
                            if 384 <= gc < 768:
                                h = (gc - 384) // 64
                                S.op("pe", lambda e, o=o, hc=hc, h=h, tt=tt: e.matmul(KMP[:, h * NTT + tt:h * NTT + tt + 1], lhsT=o[:, hc:hc + 64], rhs=C256[:, 0:1], start=True, stop=True),
                                     reads=[o, C256], writes=[KMP])
                    S.dma("sp", sinks["A"][rows, k0:k0 + wd], o[:, 0:wd], reads=[o])
        kcol[kind] += wd
        c0 += wd


def build_p1():
    nc = bass.Bass("TRN2", target_bir_lowering=False)
    x = dram_in(nc, "x", [TOK, DM], F32)
    gnorm = dram_in(nc, "gnorm", [1, DM], F32)
    w = dram_in(nc, "w", [DM, P1_NCOLS], F32)
    cs = dram_in(nc, "cs", [128, 2, NTT, 8], F32)
    gains = dram_in(nc, "gains", [1, 1920], F32)
    bias = dram_in(nc, "bias", [1, 6], F32)
    ident = dram_in(nc, "ident", [128, 128], BF16)
    oA = dram_out(nc, "oA", [TOK, 1152], BF16)
    oB = dram_out(nc, "oB", [TOK, 768], BF16)
    oC = dram_out(nc, "oC", [TOK, 1024], BF16)
    oF = dram_out(nc, "oF", [TOK, 18], F32)
    okm = dram_out(nc, "okm", [64, 48], F32)
    with ExitStack() as st:
        S = Sched(nc, st)
        EPSC = S.sb([128, 1], F32, "EPSC")
        S.op("dve", lambda e: e.memset(EPSC[:], EPS), writes=[EPSC])
        C256 = S.sb([128, 1], BF16, "C256")
        S.op("dve", lambda e: e.memset(C256[:], 1.0 / 256), writes=[C256])
        ID = S.sb([128, 128], BF16, "ID")
        S.dma("sp", ID[:], ident, writes=[ID])
        CS = S.sb([128, 2, NTT, 8], F32, "CS")
        GAINS = S.sb([128, 1920], F32, "GAINS")
        BIAS = S.sb([128, 6], F32, "BIAS")
        S.dma("sp", CS[:], cs, writes=[CS])
        S.dma("pool", GAINS[:], gains.partition_broadcast(128), writes=[GAINS])
        S.dma("pool", BIAS[:], bias.partition_broadcast(128), writes=[BIAS])
        PST = Ring([S.ps([128, 128], F32, "PST") for _ in range(2)])
        KMP = S.ps([64, 96], F32, "KMP")
        XG, RSTD = emit_xg(S, x, gnorm, ID, EPSC, PST)
        proj_loop(S, XG, RSTD, w, P1_TILES, {"A": oA, "B": oB, "C": oC, "F": oF}, EPSC, BIAS=BIAS, GAINS=GAINS, CS=CS, kmean=(KMP, C256))
        KMS = S.sb([64, 96], F32, "KMS")
        KM8 = S.sb([64, 48], F32, "KM8")
        S.op("act", lambda e: e.activation(out=KMS[:], in_=KMP[:], func=AF.Copy), reads=[KMP], writes=[KMS])
        kv = KMS[:].rearrange("p (h b t) -> p h b t", h=6, t=2)
        S.op("dve", lambda e: e.tensor_tensor(out=KM8[:].rearrange("p (h b) -> p h b", h=6), in0=kv[:, :, :, 0], in1=kv[:, :, :, 1], op=ALU.add), reads=[KMS], writes=[KM8])
        S.dma("sp", okm, KM8[:], reads=[KM8])
        S.finish()
    return nc


def rope_cs(pos):
    inv = (500000.0 ** (-np.arange(0, 16, 2, dtype=np.float32) / np.float32(16))).astype(np.float32)
    ang = pos.astype(np.float32)[:, None] * inv[None, :]
    return np.cos(ang).astype(np.float32), np.sin(ang).astype(np.float32)


def run_p1(xl, norm_g, w_in, b_f, b_gate, moba_qk_g, nsa_q_g, nsa_k_g, fox_qk_g):
    nc = get_nc("p1", build_p1)
    wr = np.ascontiguousarray(w_in[:, p1_col_perm()])
    gains = np.concatenate([np.tile(moba_qk_g[0], 6), np.tile(moba_qk_g[1], 6), np.tile(nsa_q_g, 4), nsa_k_g[1], nsa_k_g[2],
                            np.tile(fox_qk_g[0], 6), np.tile(fox_qk_g[1], 6)])[None, :].astype(np.float32)
    in_maps = []
    for c in range(NCORES):
        xs = xl[c * TOK:(c + 1) * TOK]
        cos, sin = rope_cs(np.arange(c * TOK, (c + 1) * TOK))
        cs = np.stack([cos.reshape(NTT, 128, 8).transpose(1, 0, 2), sin.reshape(NTT, 128, 8).transpose(1, 0, 2)], axis=1)
        in_maps.append({"x": np.ascontiguousarray(xs), "gnorm": np.ascontiguousarray(norm_g[None, :]), "w": wr,
                        "cs": np.ascontiguousarray(cs), "gains": gains, "bias": np.ascontiguousarray(b_f[None, :]), "ident": IDENT})
    res = run(nc, in_maps)
    out = {}
    for k in ["oA", "oB", "oC", "oF"]:
        out[k] = np.concatenate([r[k] for r in res], axis=0)
    out["kmT"] = np.ascontiguousarray(np.concatenate([r["okm"].reshape(64, 6, 8) for r in res], axis=2).reshape(64, 384))
    return out


def build_attn(jobs, tag):
    nc = bass.Bass("TRN2", target_bir_lowering=False)
    ident = dram_in(nc, "ident", [128, 128], BF16)
    ins = []
    for j, jb in enumerate(jobs):
        ins.append(dict(
            ka=dram_in(nc, f"ka{j}", [jb["Kc"], jb["NK"]], BF16),
            va=dram_in(nc, f"va{j}", [128, (jb["NK"] // 128) * jb["W"]], BF16),
            qa=dram_in(nc, f"qa{j}", [jb["Kc"], jb["nvar"] * jb["NQ"]], BF16),
            mk=dram_in(nc, f"mk{j}", [128, jb["nmask"] * 512], BF16),
            o=dram_out(nc, f"o{j}", [jb["NQ"], jb["W"]], F32)))
    mNK = max(jb["NK"] for jb in jobs)
    mVA = max((jb["NK"] // 128) * jb["W"] for jb in jobs)
    mQA = max(jb["nvar"] * jb["NQ"] for jb in jobs)
    mMK = max(jb["nmask"] for jb in jobs)
    nset = min(2, len(jobs))
    with ExitStack() as st:
        S = Sched(nc, st)
        ID = S.sb([128, 128], BF16, "ID")
        S.dma("sp", ID[:], ident, writes=[ID])
        sets = [dict(KA=S.sb([128, mNK], BF16, "KA"), VA=S.sb([128, mVA], BF16, "VA"), QA=S.sb([128, mQA], BF16, "QA"),
                     MK=S.sb([128, mMK * 512], BF16, "MK")) for _ in range(nset)]
        PSS = Ring([S.ps([128, 512], F32, "PSS") for _ in range(3)])
        ACC = [S.ps([128, 512], F32, "ACC") for _ in range(4)]
        PT = Ring([S.sb([128, 512], BF16, "PT") for _ in range(3)])
        OB = Ring([S.sb([128, 328], F32, "OB") for _ in range(4)])
        for j, jb in enumerate(jobs):
            sset = sets[j % nset]
            KA, VA, QA, MK = sset["KA"], sset["VA"], sset["QA"], sset["MK"]
            Kc, W, NQ = jb["Kc"], jb["W"], jb["NQ"]
            io = ins[j]
            S.dma("sp", KA[0:Kc, 0:jb["NK"]], io["ka"], writes=[KA])
            S.dma("sp", QA[0:Kc, 0:jb["nvar"] * NQ], io["qa"], writes=[QA])
            S.dma("sp", VA[:, 0:(jb["NK"] // 128) * W], io["va"], writes=[VA])
            S.dma("sp", MK[:, 0:jb["nmask"] * 512], io["mk"], writes=[MK])
            for lg, pairs in enumerate(jb["sched"]):
                npairs = len(pairs)
                pend = None

                def emit_pv(pt, kt, first, last, lg=lg):
                    for i in range(4):
                        S.op("pe", lambda e, pt=pt, kt=kt, i=i, first=first, last=last, W=W, VA=VA: e.matmul(
                            ACC[i][:, 0:W], lhsT=pt[:, i * 128:(i + 1) * 128], rhs=VA[:, kt * W:(kt + 1) * W], start=first, stop=last),
                            reads=[pt, VA], writes=[ACC[i]], signal=(i == 3))
                    if last:
                        for i in range(4):
                            ob = OB.next()
                            S.op("dve", lambda e, ob=ob, i=i, W=W: e.tensor_copy(out=ob[:, 0:W], in_=ACC[i][:, 0:W]), reads=[ACC[i]], writes=[ob])
                            r0 = (lg * 4 + i) * 128
                            S.dma("pool", io["o"][r0:r0 + 128, :], ob[:, 0:W], reads=[ob])

                for pi, (kt, var, midx) in enumerate(pairs):
                    ps = PSS.next()
                    q0 = var * NQ + lg * 512
                    S.op("pe", lambda e, ps=ps, kt=kt, q0=q0, midx=midx, KA=KA, QA=QA, Kc=Kc: e.matmul(
                        ps[:], lhsT=KA[0:Kc, kt * 128:(kt + 1) * 128], rhs=QA[0:Kc, q0:q0 + 512], start=True, stop=(midx is None)),
                        reads=[KA, QA], writes=[ps], signal=(midx is None))
                    if midx is not None:
                        S.op("pe", lambda e, ps=ps, midx=midx, MK=MK: e.matmul(
                            ps[:], lhsT=ID[:], rhs=MK[:, midx * 512:(midx + 1) * 512], start=False, stop=True),
                            reads=[ID, MK], writes=[ps], signal=True)
                    pt = PT.next()
                    S.op("act", lambda e, pt=pt, ps=ps: e.activation(out=pt[:], in_=ps[:], func=AF.Exp, scale=SCALE), reads=[ps], writes=[pt])
                    if pend is not None:
                        emit_pv(*pend)
                    pend = (pt, kt, pi == 0, pi == npairs - 1)
                emit_pv(*pend)
        S.finish()
    return nc


def build_p2():
    nc = bass.Bass("TRN2", target_bir_lowering=False)
    lf = dram_in(nc, "lf", [6, S_LEN], F32)
    flk = dram_in(nc, "flk", [128, 16, 128], BF16)
    flv = dram_in(nc, "flv", [128, 16, 128], BF16)
    w1 = dram_in(nc, "w1", [128, 2, 16, 256], F32)
    w2 = dram_in(nc, "w2", [128, 2, 2, 64], F32)
    pe = dram_in(nc, "pe", [128, 2, 16], F32)
    kgain = dram_in(nc, "kgain", [1, 64], F32)
    csc = dram_in(nc, "csc", [128, 2, 8], F32)
    qT = dram_in(nc, "qT", [64, 6 * TOK], BF16)
    kmT = dram_in(nc, "kmT", [64, 384], F32)
    gm = dram_in(nc, "gm", [128, 3, NTT, 64], F32)
    csplit = dram_out(nc, "csplit", [6, 6, S_LEN], BF16)
    kcmp = dram_out(nc, "kcmp", [128, 64], BF16)
    vcmp = dram_out(nc, "vcmp", [128, 64], BF16)
    mb = dram_out(nc, "mb", [TOK, 384], BF16)
    with ExitStack() as st:
        S = Sched(nc, st)
        EPSC = S.sb([128, 1], F32, "EPSC")
        S.op("dve", lambda e: e.memset(EPSC[:], EPS), writes=[EPSC])
        CH = 512
        ONES = S.sb([6, CH], F32, "ONES")
        S.op("dve", lambda e: e.memset(ONES[:], 1.0), writes=[ONES])
        LFr = Ring([S.sb([6, CH], F32, "LF") for _ in range(2)])
        Cr = Ring([S.sb([6, CH], F32, "C8") for _ in range(2)])
        OCr = Ring([S.sb([6, 6, CH], BF16, "OC") for _ in range(2)])
        Hf = S.sb([6, CH], F32, "Hf")
        R1 = S.sb([6, CH], F32, "R1")
        prev = None
        for ci in range(S_LEN // CH):
            l = LFr.next(); c8 = Cr.next(); oc = OCr.next()
            S.dma("sp", l[:], lf[:, ci * CH:(ci + 1) * CH], writes=[l])
            S.op("dve", lambda e, l=l: e.tensor_scalar(out=l[:], in0=l[:], scalar1=8.0, scalar2=None, op0=ALU.mult), reads=[l], writes=[l])
            if prev is None:
                S.op("dve", lambda e, l=l, c8=c8: e.tensor_tensor_scan(out=c8[:], data0=ONES[:], data1=l[:], initial=0.0, op0=ALU.mult, op1=ALU.add),
                     reads=[ONES, l], writes=[c8])
            else:
                S.op("dve", lambda e, l=l, c8=c8, prev=prev: e.tensor_tensor_scan(out=c8[:], data0=ONES[:], data1=l[:], initial=prev[:, CH - 1:CH], op0=ALU.mult, op1=ALU.add),
                     reads=[ONES, l, prev], writes=[c8])
            prev = c8
            S.op("dve", lambda e, c8=c8, oc=oc: e.tensor_copy(out=oc[:, 0, :], in_=c8[:]), reads=[c8], writes=[oc])
            S.op("dve", lambda e, oc=oc: e.tensor_copy(out=Hf[:], in_=oc[:, 0, :]), reads=[oc], writes=[Hf])
            S.op("dve", lambda e, c8=c8: e.tensor_tensor(out=R1[:], in0=c8[:], in1=Hf[:], op=ALU.subtract), reads=[c8, Hf], writes=[R1])
            S.op("dve", lambda e, oc=oc: e.tensor_copy(out=oc[:, 1, :], in_=R1[:]), reads=[R1], writes=[oc])
            S.op("dve", lambda e, oc=oc: e.tensor_copy(out=Hf[:], in_=oc[:, 1, :]), reads=[oc], writes=[Hf])
            S.op("dve", lambda e: e.tensor_tensor(out=R1[:], in0=R1[:], in1=Hf[:], op=ALU.subtract), reads=[R1, Hf], writes=[R1])
            S.op("dve", lambda e, oc=oc: e.tensor_copy(out=oc[:, 2, :], in_=R1[:]), reads=[R1], writes=[oc])
            S.op("dve", lambda e, oc=oc: e.tensor_scalar(out=oc[:, 3:6, :], in0=oc[:, 0:3, :], scalar1=-1.0, scalar2=None, op0=ALU.mult), reads=[oc], writes=[oc])
            S.dma("sp", csplit[:, :, ci * CH:(ci + 1) * CH], oc[:], reads=[oc])
        W1F = S.sb([128, 2, 16, 256], F32, "W1F"); W1B = S.sb([128, 2, 16, 256], BF16, "W1B")
        W2F = S.sb([128, 2, 2, 64], F32, "W2F"); W2B = S.sb([128, 2, 2, 64], BF16, "W2B")
        PEF = S.sb([128, 2, 16], F32, "PEF"); PEB = S.sb([128, 2, 16], BF16, "PEB")
        FL = [S.sb([128, 16, 128], BF16, "FLK"), S.sb([128, 16, 128], BF16, "FLV")]
        KG = S.sb([128, 64], F32, "KG"); CSC = S.sb([128, 2, 8], F32, "CSC")
        S.dma("sp", W1F[:], w1, writes=[W1F]); S.dma("sp", W2F[:], w2, writes=[W2F]); S.dma("sp", PEF[:], pe, writes=[PEF])
        S.dma("sp", FL[0][:], flk, writes=[FL[0]]); S.dma("sp", FL[1][:], flv, writes=[FL[1]])
        S.dma("sp", KG[:], kgain.partition_broadcast(128), writes=[KG]); S.dma("sp", CSC[:], csc, writes=[CSC])
        S.op("dve", lambda e: e.tensor_copy(out=W1B[:], in_=W1F[:]), reads=[W1F], writes=[W1B])
        S.op("dve", lambda e: e.tensor_copy(out=W2B[:], in_=W2F[:]), reads=[W2F], writes=[W2B])
        S.op("dve", lambda e: e.tensor_copy(out=PEB[:], in_=PEF[:]), reads=[PEF], writes=[PEB])
        B1 = S.sb([128, 4], F32, "B1")
        HS = S.sb([128, 2, 128], BF16, "HS")
        PB = S.ps([128, 8], F32, "PB")
        PH = Ring([S.ps([128, 128], F32, "PH") for _ in range(2)])
        PO = S.ps([128, 64], F32, "PO")
        YC = S.sb([128, 64], F32, "YC"); SQC = S.sb([128, 64], F32, "SQC"); RSC = S.sb([128, 1], F32, "RSC")
        RTC = S.sb([128, 4, 8], F32, "RTC"); OKC = S.sb([128, 64], BF16, "OKC"); OVC = S.sb([128, 64], BF16, "OVC")
        for kv in range(2):
            for half in range(2):
                idx = kv * 2 + half
                for j in range(16):
                    S.op("pe", lambda e, kv=kv, half=half, j=j, idx=idx: e.matmul(PB[:, idx:idx + 1], lhsT=W1B[:, kv, j, half * 128:(half + 1) * 128], rhs=PEB[:, kv, j:j + 1],
                                                                                   start=(j == 0), stop=(j == 15)), reads=[W1B, PEB], writes=[PB], signal=(j == 15))
                S.op("dve", lambda e, idx=idx: e.tensor_copy(out=B1[:, idx:idx + 1], in_=PB[:, idx:idx + 1]), reads=[PB], writes=[B1])
                ph = PH.next()
                for j in range(16):
                    S.op("pe", lambda e, ph=ph, kv=kv, half=half, j=j: e.matmul(ph[:], lhsT=W1B[:, kv, j, half * 128:(half + 1) * 128], rhs=FL[kv][:, j, :],
                                                                                start=(j == 0), stop=(j == 15)), reads=[W1B, FL[kv]], writes=[ph], signal=(j == 15))
                S.op("act", lambda e, ph=ph, half=half, idx=idx: e.activation(out=HS[:, half, :], in_=ph[:], func=AF.Silu, bias=B1[:, idx:idx + 1]),
                     reads=[ph, B1], writes=[HS])
            for half in range(2):
                S.op("pe", lambda e, kv=kv, half=half: e.matmul(PO[:], lhsT=HS[:, half, :], rhs=W2B[:, kv, half, :], start=(half == 0), stop=(half == 1)),
                     reads=[HS, W2B], writes=[PO], signal=(half == 1))
            if kv == 1:
                S.op("act", lambda e: e.activation(out=OVC[:], in_=PO[:], func=AF.Copy), reads=[PO], writes=[OVC])
                S.dma("sp", vcmp, OVC[:], reads=[OVC])
            else:
                S.op("act", lambda e: e.activation(out=YC[:], in_=PO[:], func=AF.Copy), reads=[PO], writes=[YC])
                S.op("dve", lambda e: e.tensor_tensor(out=SQC[:], in0=YC[:], in1=YC[:], op=ALU.mult), reads=[YC], writes=[SQC])
                S.op("dve", lambda e: e.tensor_reduce(out=RSC[:], in_=SQC[:], axis=AX.X, op=ALU.add), reads=[SQC], writes=[RSC])
                S.op("act", lambda e: e.activation(out=RSC[:], in_=RSC[:], func=AF.Sqrt, bias=EPSC[:], scale=1.0 / 64), reads=[RSC, EPSC], writes=[RSC])
                S.op("dve", lambda e: e.reciprocal(out=RSC[:], in_=RSC[:]), reads=[RSC], writes=[RSC])
                S.op("dve", lambda e: e.scalar_tensor_tensor(out=YC[:], in0=YC[:], scalar=RSC[:, 0:1], in1=KG[:], op0=ALU.mult, op1=ALU.mult), reads=[YC, RSC, KG], writes=[YC])
                S.op("act", lambda e: e.activation(out=OKC[:], in_=YC[:], func=AF.Copy), reads=[YC], writes=[OKC])
                S.op("dve", lambda e: e.tensor_tensor(out=RTC[:, 0, :], in0=YC[:, 0:8], in1=CSC[:, 0, :], op=ALU.mult), reads=[YC, CSC], writes=[RTC])
                S.op("dve", lambda e: e.tensor_tensor(out=RTC[:, 1, :], in0=YC[:, 8:16], in1=CSC[:, 1, :], op=ALU.mult), reads=[YC, CSC], writes=[RTC])
                S.op("dve", lambda e: e.tensor_tensor(out=RTC[:, 2, :], in0=YC[:, 8:16], in1=CSC[:, 0, :], op=ALU.mult), reads=[YC, CSC], writes=[RTC])
                S.op("dve", lambda e: e.tensor_tensor(out=RTC[:, 3, :], in0=YC[:, 0:8], in1=CSC[:, 1, :], op=ALU.mult), reads=[YC, CSC], writes=[RTC])
                S.op("dve", lambda e: e.tensor_tensor(out=OKC[:, 0:8], in0=RTC[:, 0, :], in1=RTC[:, 1, :], op=ALU.subtract), reads=[RTC], writes=[OKC])
                S.op("dve", lambda e: e.tensor_tensor(out=OKC[:, 8:16], in0=RTC[:, 2, :], in1=RTC[:, 3, :], op=ALU.add), reads=[RTC], writes=[OKC])
                S.dma("sp", kcmp, OKC[:], reads=[OKC])
        KMF = S.sb([64, 384], F32, "KMF")
        KMB = S.sb([64, 384], BF16, "KMB")
        S.dma("sp", KMF[:], kmT, writes=[KMF])
        S.op("act", lambda e: e.activation(out=KMB[:], in_=KMF[:], func=AF.Copy), reads=[KMF], writes=[KMB])
        QT = S.sb([64, 6 * TOK], BF16, "QT")
        S.dma("sp", QT[:], qT, writes=[QT])
        GM = S.sb([128, 3, NTT, 64], F32, "GM")
        S.dma("sp", GM[:], gm, writes=[GM])
        PG = Ring([S.ps([128, 64], F32, "PG") for _ in range(2)])
        GS = Ring([S.sb([128, 64], F32, "GS") for _ in range(2)])
        M8 = Ring([S.sb([128, 8], F32, "M8") for _ in range(2)])
        T1 = Ring([S.sb([128, 64], F32, "T1") for _ in range(2)])
        MBO = Ring([S.sb([128, 384], BF16, "MBO") for _ in range(2)])
        for lt in range(NTT):
            mbo = MBO.next()
            for h in range(6):
                pg = PG.next(); gs = GS.next(); m8 = M8.next(); t1 = T1.next()
                S.op("pe", lambda e, pg=pg, h=h, lt=lt: e.matmul(pg[:], lhsT=QT[:, h * TOK + lt * 128:h * TOK + (lt + 1) * 128], rhs=KMB[:, h * 64:(h + 1) * 64], start=True, stop=True),
                     reads=[QT, KMB], writes=[pg])
                S.op("dve", lambda e, pg=pg, gs=gs, lt=lt: e.tensor_tensor(out=gs[:], in0=pg[:], in1=GM[:, 0, lt, :], op=ALU.add), reads=[pg, GM], writes=[gs])
                S.op("dve", lambda e, gs=gs, m8=m8: e.max(out=m8[:], in_=gs[:]), reads=[gs], writes=[m8])
                S.op("dve", lambda e, gs=gs, m8=m8, t1=t1, lt=lt: e.scalar_tensor_tensor(out=t1[:], in0=gs[:], scalar=m8[:, 2:3], in1=GM[:, 1, lt, :], op0=ALU.is_ge, op1=ALU.mult),
                     reads=[gs, m8, GM], writes=[t1])
                S.op("dve", lambda e, t1=t1, lt=lt: e.tensor_tensor(out=t1[:], in0=t1[:], in1=GM[:, 2, lt, :], op=ALU.add), reads=[t1, GM], writes=[t1])
                S.op("dve", lambda e, t1=t1, mbo=mbo, h=h: e.tensor_scalar(out=mbo[:, h * 64:(h + 1) * 64], in0=t1[:], scalar1=-NEGM, scalar2=NEGM, op0=ALU.mult, op1=ALU.add),
                     reads=[t1], writes=[mbo])
            S.dma("sp", mb[lt * 128:(lt + 1) * 128, :], mbo[:], reads=[mbo])
        S.finish()
    return nc


def build_p4():
    nc = bass.Bass("TRN2", target_bir_lowering=False)
    oc = dram_in(nc, "oc", [4, TOK, 257], F32)
    cm = dram_in(nc, "cm", [128, 3, NTT, 256], F32)
    mbs = dram_out(nc, "mbs", [TOK, 256], BF16)
    with ExitStack() as st:
        S = Sched(nc, st)
        CM = S.sb([128, 3, NTT, 256], F32, "CM")
        S.dma("sp", CM[:], cm, writes=[CM])
        OC = Ring([S.sb([128, 4, 257], F32, "OC") for _ in range(2)])
        RD = Ring([S.sb([128, 4], F32, "RD") for _ in range(2)])
        IMP = Ring([S.sb([128, 256], F32, "IMP") for _ in range(2)])
        RR = Ring([S.sb([128, 256], F32, "RR") for _ in range(2)])
        M8 = Ring([S.sb([128, 16], F32, "M8") for _ in range(2)])
        MBO = Ring([S.sb([128, 256], BF16, "MBO") for _ in range(2)])
        for lt in range(NTT):
            o = OC.next(); rd = RD.next(); imp = IMP.next(); rr = RR.next(); m8 = M8.next(); mbo = MBO.next()
            S.dma("sp", o[:], oc[:, lt * 128:(lt + 1) * 128, :].rearrange("h q w -> q h w"), writes=[o])
            S.op("dve", lambda e, o=o, rd=rd: e.tensor_scalar(out=rd[:], in0=o[:, :, 0], scalar1=1e-30, scalar2=None, op0=ALU.max), reads=[o], writes=[rd])
            S.op("dve", lambda e, rd=rd: e.reciprocal(out=rd[:], in_=rd[:]), reads=[rd], writes=[rd])
            S.op("dve", lambda e, o=o, rd=rd, imp=imp: e.tensor_scalar(out=imp[:], in0=o[:, 0, 1:257], scalar1=rd[:, 0:1], scalar2=None, op0=ALU.mult), reads=[o, rd], writes=[imp])
            for h in range(1, 4):
                S.op("dve", lambda e, o=o, rd=rd, imp=imp, h=h: e.scalar_tensor_tensor(out=imp[:], in0=o[:, h, 1:257], scalar=rd[:, h:h + 1], in1=imp[:], op0=ALU.mult, op1=ALU.add),
                     reads=[o, rd, imp], writes=[imp])
            S.op("dve", lambda e, imp=imp, lt=lt: e.tensor_tensor(out=imp[:], in0=imp[:], in1=CM[:, 0, lt, :], op=ALU.add), reads=[imp, CM], writes=[imp])
            S.op("dve", lambda e, imp=imp, m8=m8: e.max(out=m8[:, 0:8], in_=imp[:]), reads=[imp], writes=[m8])
            S.op("dve", lambda e, imp=imp, m8=m8, rr=rr: e.match_replace(out=rr[:], in_to_replace=m8[:, 0:8], in_values=imp[:], imm_value=-1e30), reads=[imp, m8], writes=[rr])
            S.op("dve", lambda e, rr=rr, m8=m8: e.max(out=m8[:, 8:16], in_=rr[:]), reads=[rr], writes=[m8])
            S.op("dve", lambda e, imp=imp, m8=m8, rr=rr, lt=lt: e.scalar_tensor_tensor(out=rr[:], in0=imp[:], scalar=m8[:, 12:13], in1=CM[:, 1, lt, :], op0=ALU.is_ge, op1=ALU.mult),
                 reads=[imp, m8, CM], writes=[rr])
            S.op("dve", lambda e, rr=rr, lt=lt: e.tensor_tensor(out=rr[:], in0=rr[:], in1=CM[:, 2, lt, :], op=ALU.add), reads=[rr, CM], writes=[rr])
            S.op("dve", lambda e, rr=rr, mbo=mbo: e.tensor_scalar(out=mbo[:], in0=rr[:], scalar1=-NEGM, scalar2=NEGM, op0=ALU.mult, op1=ALU.add), reads=[rr], writes=[mbo])
            S.dma("sp", mbs[lt * 128:(lt + 1) * 128, :], mbo[:], reads=[mbo])
        S.finish()
    return nc


def build_p5():
    nc = bass.Bass("TRN2", target_bir_lowering=False)
    xs = dram_in(nc, "xs", [TOK, DM], F32)
    gnorm = dram_in(nc, "gnorm", [1, DM], F32)
    wz = dram_in(nc, "wz", [DM, P5_NCOLS], F32)
    bgate = dram_in(nc, "bgate", [1, 3072], F32)
    ng = dram_in(nc, "ng", [TOK, 18], F32)
    om = dram_in(nc, "om", [TOK, 6 * 65], F32)
    ofx = dram_in(nc, "ofx", [TOK, 6 * 65], F32)
    on = dram_in(nc, "on", [TOK, 3, 4 * 65], F32)
    wup = dram_in(nc, "wup", [DM, DM], F32)
    wout = dram_in(nc, "wout", [DM, DM], F32)
    ident = dram_in(nc, "ident", [128, 128], BF16)
    out = dram_out(nc, "out", [TOK, DM], F32)
    zz = nc.dram_tensor("zz_scr", [TOK, 1024], F32).ap()
    gg = nc.dram_tensor("gg_scr", [TOK, 3072], F32).ap()
    with ExitStack() as st:
        S = Sched(nc, st)
        ID = S.sb([128, 128], BF16, "ID")
        S.dma("sp", ID[:], ident, writes=[ID])
        EPSC = S.sb([128, 1], F32, "EPSC")
        S.op("dve", lambda e: e.memset(EPSC[:], EPS), writes=[EPSC])
        PST = Ring([S.ps([128, 128], F32, "PST") for _ in range(2)])
        with ExitStack() as stA:
            SA = S
            old_stack = S.stack
            S.stack = stA
            BIAS = S.sb([128, 3072], F32, "BIAS")
            S.dma("pool", BIAS[:], bgate.partition_broadcast(128), writes=[BIAS])
            XG, RSTD = emit_xg(S, xs, gnorm, ID, EPSC, PST)
            proj_loop(S, XG, RSTD, wz, P5_TILES, {"D": zz, "E": gg}, EPSC, BIAS=BIAS, nwbuf=2)
            S.stack = old_stack
            S.barrier()
            S.finish_part()
        WUP = S.sb([128, 8, DM], BF16, "WUP"); WOUT = S.sb([128, 8, DM], BF16, "WOUT")
        WF = Ring([S.sb([128, 2, DM], F32, "WF") for _ in range(2)])
        for wi, (src, dst) in enumerate(((wup, WUP), (wout, WOUT))):
            for q4 in range(4):
                wf = WF.next()
                S.dma("sp", wf[:], src[q4 * 256:(q4 + 1) * 256, :].rearrange("(c p) n -> p c n", p=128), writes=[wf])
                S.op("pool", lambda e, wf=wf, dst=dst, q4=q4: e.tensor_copy(out=dst[:, 2 * q4:2 * q4 + 2, :], in_=wf[:]), reads=[wf], writes=[dst])
        X = Ring([S.sb([128, DM], F32, "X") for _ in range(2)])
        Z = Ring([S.sb([128, DM], F32, "Z") for _ in range(2)])
        G = Ring([S.sb([128, 3072], F32, "G") for _ in range(2)])
        NG = Ring([S.sb([128, 18], F32, "NG") for _ in range(2)])
        OM = Ring([S.sb([128, 6, 65], F32, "OM") for _ in range(2)])
        OFX = Ring([S.sb([128, 6, 65], F32, "OFX") for _ in range(2)])
        ON = Ring([S.sb([128, 3, 4, 65], F32, "ON") for _ in range(2)])
        RD = Ring([S.sb([128, 32], F32, "RD") for _ in range(2)])
        T32 = Ring([S.sb([128, DM], F32, "T32") for _ in range(2)])
        TN = Ring([S.sb([128, 256], F32, "TN") for _ in range(2)])
        TB = Ring([S.sb([128, DM], BF16, "TB") for _ in range(2)])
        TT = Ring([S.sb([128, 8, 128], BF16, "TT") for _ in range(2)])
        MG = Ring([S.sb([128, DM], F32, "MG") for _ in range(2)])
        TM = Ring([S.sb([128, 512], F32, "TM") for _ in range(2)])
        MGB = Ring([S.sb([128, DM], BF16, "MGB") for _ in range(2)])
        MT = Ring([S.sb([128, 8, 128], BF16, "MT") for _ in range(2)])
        OUT = Ring([S.sb([128, DM], F32, "OUT") for _ in range(2)])
        PSY = Ring([S.ps([128, 512], F32, "PSY") for _ in range(3)])
        PSO = Ring([S.ps([128, 512], F32, "PSO") for _ in range(2)])
        for tt in range(NTT):
            rows = slice(tt * 128, (tt + 1) * 128)
            x = X.next(); z = Z.next(); g = G.next(); ngt = NG.next(); o_m = OM.next(); o_f = OFX.next(); o_n = ON.next()
            rd = RD.next(); t32 = T32.next(); tn = TN.next(); tb = TB.next(); ttt = TT.next(); mg = MG.next(); mgb = MGB.next(); mt = MT.next(); ot = OUT.next()
            S.dma("sp", x[:], xs[rows, :], writes=[x]); S.dma("sp", z[:], zz[rows, :], writes=[z]); S.dma("sp", g[:], gg[rows, :], writes=[g])
            S.dma("sp", ngt[:], ng[rows, :], writes=[ngt])
            S.dma("sp", o_m[:], om[rows, :].rearrange("q (h w) -> q h w", w=65), writes=[o_m])
            S.dma("sp", o_f[:], ofx[rows, :].rearrange("q (h w) -> q h w", w=65), writes=[o_f])
            S.dma("sp", o_n[:], on[rows, :, :].rearrange("q j (h w) -> q j h w", w=65), writes=[o_n])
            S.op("dve", lambda e, rd=rd, o_m=o_m: e.reciprocal(out=rd[:, 0:6], in_=o_m[:, :, 64]), reads=[o_m], writes=[rd])
            S.op("dve", lambda e, rd=rd, o_f=o_f: e.reciprocal(out=rd[:, 6:12], in_=o_f[:, :, 64]), reads=[o_f], writes=[rd])
            S.op("dve", lambda e, rd=rd, o_m=o_m, t32=t32: e.tensor_tensor(out=t32[:, 0:384].rearrange("p (h d) -> p h d", d=64), in0=o_m[:, :, 0:64],
                                                                         in1=rd[:, 0:6].unsqueeze(2).to_broadcast([128, 6, 64]), op=ALU.mult), reads=[o_m, rd], writes=[t32])
            S.op("dve", lambda e, rd=rd, o_f=o_f, t32=t32: e.tensor_tensor(out=t32[:, 640:1024].rearrange("p (h d) -> p h d", d=64), in0=o_f[:, :, 0:64],
                                                                         in1=rd[:, 6:12].unsqueeze(2).to_broadcast([128, 6, 64]), op=ALU.mult), reads=[o_f, rd], writes=[t32])
            S.op("dve", lambda e, rd=rd, o_n=o_n: e.tensor_scalar(out=rd[:, 12:24].rearrange("p (j h) -> p j h", h=4), in0=o_n[:, :, :, 64], scalar1=1e-30, scalar2=None, op0=ALU.max),
                 reads=[o_n], writes=[rd])
            S.op("dve", lambda e, rd=rd: e.reciprocal(out=rd[:, 12:24], in_=rd[:, 12:24]), reads=[rd], writes=[rd])
            S.op("dve", lambda e, rd=rd, ngt=ngt: e.tensor_tensor(out=rd[:, 12:24].rearrange("p (j h) -> p j h", h=4), in0=rd[:, 12:24].rearrange("p (j h) -> p j h", h=4),
                                                                  in1=ngt[:, 0:12].rearrange("p (h j) -> p j h", j=3), op=ALU.mult), reads=[rd, ngt], writes=[rd])
            for j in range(3):
                dst = t32 if j == 0 else tn
                dv = (t32[:, 384:640] if j == 0 else tn[:, 0:256]).rearrange("p (h d) -> p h d", d=64)
                S.op("dve", lambda e, rd=rd, o_n=o_n, dv=dv, j=j: e.tensor_tensor(out=dv, in0=o_n[:, j, :, 0:64],
                                                                                  in1=rd[:, 12 + 4 * j:16 + 4 * j].unsqueeze(2).to_broadcast([128, 4, 64]), op=ALU.mult),
                     reads=[o_n, rd], writes=[dst])
                if j > 0:
                    S.op("pool", lambda e, t32=t32, tn=tn: e.tensor_tensor(out=t32[:, 384:640], in0=t32[:, 384:640], in1=tn[:, 0:256], op=ALU.add), reads=[t32, tn], writes=[t32])
            S.op("pool", lambda e, t32=t32, z=z, tb=tb: e.tensor_tensor(out=tb[:], in0=t32[:], in1=z[:], op=ALU.mult), reads=[t32, z], writes=[tb])
            for c in range(8):
                pt = PST.next()
                S.op("pe", lambda e, pt=pt, tb=tb, c=c: e.matmul(pt[:], lhsT=tb[:, c * 128:(c + 1) * 128], rhs=ID[:], start=True, stop=True), reads=[tb, ID], writes=[pt])
                S.op("act", lambda e, pt=pt, ttt=ttt, c=c: e.activation(out=ttt[:, c, :], in_=pt[:], func=AF.Copy), reads=[pt], writes=[ttt])
            for ct in range(2):
                cols = slice(ct * 512, (ct + 1) * 512)
                for b, chs in enumerate(((0, 1, 2), (3, 4), (5, 6, 7))):
                    py = PSY.next()
                    for k, ch in enumerate(chs):
                        S.op("pe", lambda e, py=py, ttt=ttt, ch=ch, cols=cols, k=k, n=len(chs): e.matmul(py[:], lhsT=ttt[:, ch, :], rhs=WUP[:, ch, cols], start=(k == 0), stop=(k == n - 1)),
                             reads=[ttt, WUP], writes=[py], signal=(k == len(chs) - 1))
                    gsl = slice(b * 1024 + ct * 512, b * 1024 + (ct + 1) * 512)
                    if b == 0:
                        S.op("dve", lambda e, py=py, g=g, mg=mg, cols=cols, gsl=gsl: e.tensor_tensor(out=mg[:, cols], in0=py[:], in1=g[:, gsl], op=ALU.mult), reads=[py, g], writes=[mg])
                    else:
                        tm = TM.next()
                        S.op("dve", lambda e, py=py, g=g, tm=tm, gsl=gsl: e.tensor_tensor(out=tm[:], in0=py[:], in1=g[:, gsl], op=ALU.mult), reads=[py, g], writes=[tm])
                        S.op("pool", lambda e, mg=mg, tm=tm, cols=cols: e.tensor_tensor(out=mg[:, cols], in0=mg[:, cols], in1=tm[:], op=ALU.add), reads=[mg, tm], writes=[mg])
            S.op("pool", lambda e, mg=mg, mgb=mgb: e.tensor_copy(out=mgb[:], in_=mg[:]), reads=[mg], writes=[mgb])
            for c in range(8):
                pt = PST.next()
                S.op("pe", lambda e, pt=pt, mgb=mgb, c=c: e.matmul(pt[:], lhsT=mgb[:, c * 128:(c + 1) * 128], rhs=ID[:], start=True, stop=True), reads=[mgb, ID], writes=[pt])
                S.op("act", lambda e, pt=pt, mt=mt, c=c: e.activation(out=mt[:, c, :], in_=pt[:], func=AF.Copy), reads=[pt], writes=[mt])
            for ct in range(2):
                cols = slice(ct * 512, (ct + 1) * 512)
                po = PSO.next()
                for c in range(8):
                    S.op("pe", lambda e, po=po, mt=mt, c=c, cols=cols: e.matmul(po[:], lhsT=mt[:, c, :], rhs=WOUT[:, c, cols], start=(c == 0), stop=(c == 7)),
                         reads=[mt, WOUT], writes=[po], signal=(c == 7))
                S.op("dve", lambda e, po=po, x=x, ot=ot, cols=cols: e.tensor_tensor(out=ot[:, cols], in0=po[:], in1=x[:, cols], op=ALU.add), reads=[po, x], writes=[ot])
            S.dma("pool", out[rows, :], ot[:], reads=[ot])
        S.finish()
    return nc


def bf(a):
    return np.ascontiguousarray(a).astype(NPBF) if a.dtype != NPBF else np.ascontiguousarray(a)


def mask_tile(fn):
    k = np.arange(128)[:, None]
    q = np.arange(512)[None, :]
    return np.where(fn(k, q), 0.0, NEGM).astype(np.float32)


def pack_masks(tiles):
    return bf(np.concatenate(tiles, axis=1))


M_CAUSAL = [mask_tile(lambda k, q, r=r: 128 * r + k <= q) for r in range(4)]
M_ZERO = np.zeros((128, 512), np.float32)
M_FULL = np.full((128, 512), NEGM, np.float32)
MASKS_PAR = [pack_masks(M_CAUSAL + [M_FULL] * 4), pack_masks([M_ZERO] * 4 + M_CAUSAL)]
MASKS_WIN = pack_masks([mask_tile(lambda k, q, r=r: (q - (128 * r + k) >= 0) & (q - (128 * r + k) < 512)) for r in range(-4, 4)])
MASKS_CMP = [pack_masks([mask_tile(lambda k, q, r=r0 - par: 16 * k + 31 + 512 * r <= q) for r0 in range(-4, 1)]) for par in range(2)]
IDENT = np.eye(128, dtype=np.float32).astype(NPBF)


def dense_sched(nvar):
    sched = []
    for i in range(16):
        sched.append([(kt, (kt // 32) if nvar == 4 else 0, (kt - 8 * i) if kt >= 8 * i else None) for kt in range(8 * i + 8)])
    return sched


def win_sched():
    return [[(8 * i + j, 0, j) for j in range(8)] for i in range(16)]


def cmp_sched():
    sched = []
    for i in range(16):
        prs = []
        for j in range(8):
            r0 = 4 * j - 2 * i
            if r0 >= 2:
                continue
            prs.append((j, 0, None if r0 <= -5 else r0 + 4))
        sched.append(prs)
    return sched


def va_pack(v, extra=None):
    nk = v.shape[0]
    cols = [v.astype(NPBF), np.ones((nk, 1), NPBF)]
    if extra is not None:
        cols.append(extra.astype(NPBF))
    va = np.concatenate(cols, axis=1)
    W = va.shape[1]
    return np.ascontiguousarray(va.reshape(nk // 128, 128, W).transpose(1, 0, 2).reshape(128, (nk // 128) * W))


def par_qidx(par):
    return np.concatenate([np.arange(1024 * i + 512 * par, 1024 * i + 512 * par + 512) for i in range(16)])


_NC_CACHE = {}


def get_nc(key, fn):
    if key not in _NC_CACHE:
        _NC_CACHE[key] = fn()
    return _NC_CACHE[key]


def run(nc, in_maps):
    res = run_bass_kernel_spmd(nc, in_maps, core_ids=list(range(NCORES)))
    return res.results


def layer_forward(xl, p):
    S = S_LEN
    o1 = run_p1(xl, p["norm_g"], p["w_in"], p["b_f"], p["b_gate"], p["moba_qk_g"], p["nsa_q_g"], p["nsa_k_g"], p["fox_qk_g"])
    oA, oB, oC, oF = (o1[k] for k in ("oA", "oB", "oC", "oF"))
    mq, mk, nq, ksl, kw = oA[:, 0:384], oA[:, 384:768], oA[:, 768:1024], oA[:, 1024:1088], oA[:, 1088:1152]
    fq, fk = oB[:, 0:384], oB[:, 384:768]
    mv, fv, kc, vc, vsl, vw = oC[:, 0:384], oC[:, 384:768], oC[:, 768:832], oC[:, 832:896], oC[:, 896:960], oC[:, 960:1024]
    lf = np.ascontiguousarray(oF[:, 12:18].T)
    gidx = (np.arange(1023) * 16)[:, None] + np.arange(32)[None, :]

    def flat_of(t):
        blocks = np.zeros((1024, 32, 64), NPBF)
        blocks[:1023] = t[gidx]
        return blocks.reshape(1024, 2048).T.reshape(16, 128, 1024)
    flk_all, flv_all = flat_of(kc), flat_of(vc)
    w1 = np.ascontiguousarray(p["cmp_w1"].reshape(2, 16, 128, 256).transpose(2, 0, 1, 3))
    w2 = np.ascontiguousarray(p["cmp_w2"].reshape(2, 2, 128, 64).transpose(2, 0, 1, 3))
    pe = np.ascontiguousarray(p["cmp_pe"].reshape(2, 16, 128).transpose(2, 0, 1))
    kgain = np.ascontiguousarray(p["nsa_k_g"][0][None, :])
    in_maps = []
    for c in range(NCORES):
        cosc, sinc = rope_cs(np.arange(128 * c, 128 * c + 128) * 16 + 31)
        qs = mq[c * TOK:(c + 1) * TOK]
        qT = np.ascontiguousarray(qs.reshape(TOK, 6, 64).transpose(2, 1, 0).reshape(64, 6 * TOK))
        gm = np.zeros((128, 3, NTT, 64), np.float32)
        for lt in range(NTT):
            cur = (16 * c + lt) // 2
            gm[:, 0, lt, cur:] = -1e30
            gm[:, 1, lt, :cur] = 1.0
            gm[:, 2, lt, cur] = 1.0
        in_maps.append({"lf": lf, "flk": np.ascontiguousarray(flk_all[:, :, 128 * c:128 * c + 128].transpose(1, 0, 2)),
                        "flv": np.ascontiguousarray(flv_all[:, :, 128 * c:128 * c + 128].transpose(1, 0, 2)),
                        "w1": w1, "w2": w2, "pe": pe, "kgain": kgain, "csc": np.ascontiguousarray(np.stack([cosc, sinc], axis=1)),
                        "qT": qT, "kmT": o1["kmT"], "gm": gm})
    r2 = run(get_nc("p2", build_p2), in_maps)
    csplit = r2[0]["csplit"]
    kcmp = np.concatenate([r["kcmp"] for r in r2], axis=0)
    vcmp = np.concatenate([r["vcmp"] for r in r2], axis=0)
    mb = np.concatenate([r["mb"] for r in r2], axis=0)
    keys = np.arange(S)
    E_moba = (keys[None, :] // 256 == np.arange(64)[:, None]).astype(NPBF)
    E_slc = ((keys[None, :] // 64) % 64 == np.arange(64)[:, None]).astype(NPBF)
    ones3 = np.ones((3, S), NPBF)

    def dense_job_inputs(branch, h, par):
        qi = par_qidx(par)
        KA = np.zeros((128, S), NPBF)
        QA = np.zeros((128, 8192), NPBF)
        if branch == "fox":
            KA[0:64] = fk[:, h * 64:(h + 1) * 64].T
            KA[64:67] = csplit[h, 3:6]
            KA[67:70] = ones3
            QA[0:64] = fq[qi, h * 64:(h + 1) * 64].T
            QA[64:67] = ones3[:, qi]
            QA[67:70] = csplit[h, 0:3][:, qi]
            VA = va_pack(fv[:, h * 64:(h + 1) * 64])
        else:
            KA[0:64] = mk[:, h * 64:(h + 1) * 64].T
            KA[64:128] = E_moba
            QA[0:64] = mq[qi, h * 64:(h + 1) * 64].T
            QA[64:128] = mb[qi, h * 64:(h + 1) * 64].T
            VA = va_pack(mv[:, h * 64:(h + 1) * 64])
        return KA, VA, QA, MASKS_PAR[par]
    djobs = [(br, h, par) for br in ("fox", "moba") for h in range(6) for par in range(2)]
    overlap = np.zeros((1024, 256), np.float32)
    for m in range(256):
        lo_, hi_ = max(4 * m - 1, 0), min(4 * m + 3, 1022)
        overlap[lo_:hi_ + 1, m] = 1.0
    kcmpT = np.ascontiguousarray(kcmp.T)
    va_cmp = va_pack(vcmp, overlap)
    kwT = np.ascontiguousarray(kw.T)
    va_win = va_pack(vw)
    spec_d = dict(Kc=128, NK=S, W=65, nvar=1, NQ=8192, nmask=8, sched=dense_sched(1))
    jobs3 = [spec_d, spec_d, spec_d, dict(Kc=64, NK=S, W=65, nvar=1, NQ=8192, nmask=8, sched=win_sched()),
             dict(Kc=64, NK=1024, W=321, nvar=1, NQ=8192, nmask=5, sched=cmp_sched())]
    kw_sh = [np.concatenate([np.zeros((64, 512), NPBF), kwT[:, :S - 512]], axis=1), kwT]
    vw_aug = np.concatenate([vw.astype(NPBF), np.ones((S, 1), NPBF)], axis=1)
    vw_sh = [np.concatenate([np.zeros((512, 65), NPBF), vw_aug[:S - 512]], axis=0), vw_aug]
    va_win = [np.ascontiguousarray(v.reshape(S // 128, 128, 65).transpose(1, 0, 2).reshape(128, (S // 128) * 65)) for v in vw_sh]
    in_maps = []
    for c in range(NCORES):
        m = {"ident": IDENT}
        for s_ in range(3):
            KA, VA, QA, MK = dense_job_inputs(*djobs[3 * c + s_])
            m[f"ka{s_}"], m[f"va{s_}"], m[f"qa{s_}"], m[f"mk{s_}"] = KA, VA, QA, MK
        hh, par = c % 4, c // 4
        nqT = np.ascontiguousarray(nq[par_qidx(par), hh * 64:(hh + 1) * 64].T)
        m["ka3"], m["va3"], m["qa3"], m["mk3"] = np.ascontiguousarray(kw_sh[par]), va_win[par], nqT, MASKS_WIN
        m["ka4"], m["va4"], m["qa4"], m["mk4"] = kcmpT, va_cmp, nqT, MASKS_CMP[par]
        in_maps.append(m)
    r3 = run(get_nc("attn3", lambda: build_attn(jobs3, "a3")), in_maps)
    o_fox = np.zeros((6, S, 65), np.float32)
    o_moba = np.zeros((6, S, 65), np.float32)
    for jid, (br, h, par) in enumerate(djobs):
        (o_fox if br == "fox" else o_moba)[h, par_qidx(par)] = r3[jid // 3][f"o{jid % 3}"]
    o_win = np.zeros((4, S, 65), np.float32)
    o_cmp = np.zeros((4, S, 321), np.float32)
    for c in range(NCORES):
        o_win[c % 4, par_qidx(c // 4)] = r3[c]["o3"]
        o_cmp[c % 4, par_qidx(c // 4)] = r3[c]["o4"]
    in_maps = []
    for c in range(NCORES):
        cm = np.zeros((128, 3, NTT, 256), np.float32)
        t = (c * TOK + np.arange(TOK)).reshape(NTT, 128).T
        cur = t // 64
        mm = np.arange(256)[None, None, :]
        cand = (mm >= 1) & (mm <= cur[:, :, None] - 2)
        forced = (mm == 0) | (mm == cur[:, :, None]) | (mm == cur[:, :, None] - 1)
        cm[:, 0] = np.where(cand, 0.0, -1e30)
        cm[:, 1] = cand
        cm[:, 2] = forced
        in_maps.append({"oc": np.ascontiguousarray(o_cmp[:, c * TOK:(c + 1) * TOK, 64:321]), "cm": cm})
    r4 = run(get_nc("p4", build_p4), in_maps)
    mbs = np.concatenate([r["mbs"] for r in r4], axis=0)
    spec_s = dict(Kc=128, NK=S, W=65, nvar=4, NQ=8192, nmask=8, sched=dense_sched(4))
    KAs = np.zeros((128, S), NPBF)
    KAs[0:64] = ksl.T
    KAs[64:128] = E_slc
    va_s = va_pack(vsl)
    in_maps = []
    for c in range(NCORES):
        h, par = c // 2, c % 2
        qi = par_qidx(par)
        QA = np.zeros((128, 4, 8192), NPBF)
        QA[0:64] = nq[qi, h * 64:(h + 1) * 64].T[:, None, :]
        QA[64:128] = mbs[qi].T.reshape(4, 64, 8192).transpose(1, 0, 2)
        in_maps.append({"ident": IDENT, "ka0": KAs, "va0": va_s, "qa0": QA.reshape(128, 4 * 8192), "mk0": MASKS_PAR[par]})
    r5 = run(get_nc("attn5", lambda: build_attn([spec_s], "a5")), in_maps)
    o_slc = np.zeros((4, S, 65), np.float32)
    for c in range(NCORES):
        o_slc[c // 2, par_qidx(c % 2)] = r5[c]["o0"]
    wz = np.ascontiguousarray(p["w_in"][:, p5_col_perm()])
    wup = np.ascontiguousarray(np.concatenate([p["w_up_moba"], p["w_up_nsa"], p["w_up_fox"]], axis=0))
    om = np.ascontiguousarray(o_moba.transpose(1, 0, 2).reshape(S, 6 * 65))
    ofx = np.ascontiguousarray(o_fox.transpose(1, 0, 2).reshape(S, 6 * 65))
    on = np.ascontiguousarray(np.stack([o_cmp[:, :, 0:65], o_slc, o_win], axis=0).transpose(2, 0, 1, 3).reshape(S, 3, 4 * 65))
    in_maps = []
    for c in range(NCORES):
        sl = slice(c * TOK, (c + 1) * TOK)
        in_maps.append({"xs": np.ascontiguousarray(xl[sl]), "gnorm": np.ascontiguousarray(p["norm_g"][None, :]), "wz": wz,
                        "bgate": np.ascontiguousarray(p["b_gate"][None, :]),
                        "ng": np.ascontiguousarray(oF[sl]), "om": om[sl], "ofx": ofx[sl], "on": on[sl], "wup": wup, "wout": p["w_out"], "ident": IDENT})
    r6 = run(get_nc("p5", build_p5), in_maps)
    dbg = dict(o_fox=o_fox, o_moba=o_moba, o_win=o_win, o_cmp=o_cmp, o_slc=o_slc, mb=mb, mbs=mbs, kcmp=kcmp, vcmp=vcmp, csplit=csplit)
    return np.concatenate([r["out"] for r in r6], axis=0), dbg


PARAM_KEYS = ["norm_g", "w_in", "b_f", "b_gate", "moba_qk_g", "nsa_q_g", "nsa_k_g", "fox_qk_g", "cmp_pe", "cmp_w1", "cmp_w2",
              "w_up_moba", "w_up_nsa", "w_up_fox", "w_out"]


def kernel(**inputs):
    x = np.asarray(inputs["x"], np.float32)
    xl = np.ascontiguousarray(x[0])
    for l in range(2):
        p = {k: np.ascontiguousarray(np.asarray(inputs[k], np.float32)[l]) for k in PARAM_KEYS}
        xl, _ = layer_forward(xl, p)
    return xl[None].astype(np.float32)
```

```python
import numpy as np
import ml_dtypes
from contextlib import ExitStack
import concourse.bass as bass
import concourse.mybir as mybir
from concourse.bass_utils import run_bass_kernel_spmd

F32 = mybir.dt.float32
BF16 = mybir.dt.bfloat16
AF = mybir.ActivationFunctionType
ALU = mybir.AluOpType
AX = mybir.AxisListType
NPBF = ml_dtypes.bfloat16

NCORES = 8
S_LEN = 16384
DM = 1024
HD = 64
EPS = 1e-6
SCALE = 0.125
NEGM = -30000.0


class Buf:
    def __init__(self, t, name):
        self.t = t
        self.name = name
        self.w = None
        self.r = []
        self.dsem = None
        self.dcnt = 0

    def __getitem__(self, k):
        return self.t[k]


class Sched:
    ENG = ("pe", "act", "dve", "pool", "sp")

    def __init__(self, nc, stack):
        self.nc = nc
        self.stack = stack
        self.sem_stack = stack
        self.ops = {e: [] for e in self.ENG}
        self.sem = {e: stack.enter_context(nc.semaphore("S_" + e)) for e in self.ENG}
        self.seq = {e: 0 for e in self.ENG}
        self.known = {e: {} for e in self.ENG}
        self.semobjs = {}
        self.dma_tokens = []
        self.nb = 0

    def sb(self, shape, dt, name=None):
        self.nb += 1
        name = (name or "sb") + f"_{self.nb}"
        t = self.stack.enter_context(self.nc.sbuf_tensor(name, list(shape), dt))
        return Buf(t, name)

    def ps(self, shape, dt, name=None):
        self.nb += 1
        name = (name or "ps") + f"_{self.nb}"
        t = self.stack.enter_context(self.nc.psum_tensor(name, list(shape), dt))
        return Buf(t, name)

    def _dsem(self, b):
        if b.dsem is None:
            b.dsem = self.sem_stack.enter_context(self.nc.semaphore("D_" + b.name))
        return b.dsem

    def _waits(self, eng, reads, writes):
        toks = []
        for b in list(reads) + list(writes):
            if b.w is not None:
                toks.append(b.w)
        for b in writes:
            toks.extend(b.r)
        best = {}
        for (s, v) in toks:
            k = id(s)
            self.semobjs[k] = s
            if v > best.get(k, 0):
                best[k] = v
        kn = self.known[eng]
        for k, v in best.items():
            if eng == "pe" and self.semobjs[k] is self.sem["pe"]:
                continue
            if kn.get(k, 0) >= v:
                continue
            kn[k] = v
            self.ops[eng].append(("wait", self.semobjs[k], v))

    def op(self, eng, fn, reads=(), writes=(), signal=True):
        self._waits(eng, reads, writes)
        tok = (self.sem[eng], self.seq[eng] + 1)
        if signal:
            self.seq[eng] += 1
        self.ops[eng].append(("op", fn, self.sem[eng] if signal else None, 1))
        for b in reads:
            b.r.append(tok)
        for b in writes:
            b.w = tok
            b.r = []
        return tok

    def dma(self, q, out_ap, in_ap, reads=(), writes=(), **kw):
        self._waits(q, reads, writes)
        owner = (list(writes) + list(reads))[0]
        s = self._dsem(owner)
        owner.dcnt += 16
        tok = (s, owner.dcnt)
        self.ops[q].append(("op", (lambda e, o=out_ap, i=in_ap, kw=kw: e.dma_start(out=o, in_=i, **kw)), s, 16))
        for b in reads:
            b.r.append(tok)
        for b in writes:
            b.w = tok
            b.r = []
        self.dma_tokens.append(tok)
        return tok

    def barrier(self):
        best = {}
        for (s_, v) in self.dma_tokens:
            k = id(s_)
            self.semobjs[k] = s_
            best[k] = max(best.get(k, 0), v)
        for e in self.ENG:
            if self.seq[e] > 0:
                k = id(self.sem[e])
                self.semobjs[k] = self.sem[e]
                best[k] = self.seq[e]
        for e in self.ENG:
            kn = self.known[e]
            for k, v in best.items():
                if self.semobjs[k] is self.sem[e]:
                    continue
                if kn.get(k, 0) >= v:
                    continue
                kn[k] = v
                self.ops[e].append(("wait", self.semobjs[k], v))

    def finish(self):
        nc = self.nc
        best = {}
        for (s, v) in self.dma_tokens:
            k = id(s)
            self.semobjs[k] = s
            best[k] = max(best.get(k, 0), v)
        for e in self.ENG:
            if e != "sp" and self.seq[e] > 0:
                k = id(self.sem[e])
                self.semobjs[k] = self.sem[e]
                best[k] = self.seq[e]
        for k, v in best.items():
            self.ops["sp"].append(("wait", self.semobjs[k], v))
        self._emit_block()

    def finish_part(self):
        self._emit_block()
        self.ops = {e: [] for e in self.ENG}

    def _emit_block(self):
        nc = self.nc
        ops = self.ops

        def replay(e, lst):
            for it in lst:
                if it[0] == "wait":
                    e.wait_ge(it[1], it[2])
                else:
                    ins = it[1](e)
                    if it[2] is not None:
                        ins.then_inc(it[2], it[3])

        with nc.Block() as block:
            @block.tensor
            def _(e):
                replay(e, ops["pe"])

            @block.scalar
            def _(e):
                replay(e, ops["act"])

            @block.vector
            def _(e):
                replay(e, ops["dve"])

            @block.gpsimd
            def _(e):
                replay(e, ops["pool"])

            @block.sync
            def _(e):
                replay(e, ops["sp"])


class Ring:
    def __init__(self, bufs):
        self.bufs = bufs
        self.i = 0

    def next(self):
        b = self.bufs[self.i % len(self.bufs)]
        self.i += 1
        return b


def dram_in(nc, name, shape, dt):
    return nc.dram_tensor(name, list(shape), dt, kind="ExternalInput").ap()


def dram_out(nc, name, shape, dt):
    return nc.dram_tensor(name, list(shape), dt, kind="ExternalOutput").ap()


TOK = S_LEN // NCORES
NTT = TOK // 128
P1_TILES = [("A", 512), ("A", 512), ("A", 128), ("B", 512), ("B", 256), ("C", 512), ("C", 512), ("F", 18)]
P1_NCOLS = sum(w for _, w in P1_TILES)
P5_TILES = [("D", 512), ("D", 512)] + [("E", 512)] * 6
P5_NCOLS = 4096


def col_ranges():
    sp = [384] * 4 + [256] + [64] * 6 + [12, 256] + [384] * 3 + [6, 384, 3072]
    names = ["mq", "mk", "mv", "mz", "nq", "kc", "vc", "ksl", "vsl", "kw", "vw", "ng", "nz", "fq", "fk", "fv", "ff", "fz", "gl"]
    off = np.concatenate([[0], np.cumsum(sp)])
    return {n: np.arange(off[i], off[i + 1]) for i, n in enumerate(names)}


def p1_col_perm():
    rng = col_ranges()
    return np.concatenate([rng[n] for n in ["mq", "mk", "nq", "ksl", "kw", "fq", "fk", "mv", "fv", "kc", "vc", "vsl", "vw", "ng", "ff"]])


def p5_col_perm():
    rng = col_ranges()
    return np.concatenate([rng[n] for n in ["mz", "nz", "fz", "gl"]])


def emit_xg(S, x, gnorm, ID, EPSC, PST):
    XG = S.sb([128, 8, TOK], BF16, "XG")
    RSTD = S.sb([128, NTT], F32, "RSTD")
    GREP = S.sb([128, DM], F32, "GREP")
    S.dma("pool", GREP[:], gnorm.partition_broadcast(128), writes=[GREP])
    XS = Ring([S.sb([128, DM], F32, "XS") for _ in range(2)])
    XB = Ring([S.sb([128, DM], BF16, "XB") for _ in range(2)])
    JUNK = S.sb([128, DM], BF16, "JUNK")
    SSQ = S.sb([128, NTT], F32, "SSQ")
    for tt in range(NTT):
        b = XS.next(); xb = XB.next()
        S.dma("sp", b[:], x[tt * 128:(tt + 1) * 128, :], writes=[b])
        S.op("act", lambda e, b=b, tt=tt: e.activation(out=JUNK[:], in_=b[:], func=AF.Square, accum_out=SSQ[:, tt:tt + 1]), reads=[b], writes=[JUNK, SSQ])
        S.op("pool", lambda e, b=b, xb=xb: e.tensor_tensor(out=xb[:], in0=b[:], in1=GREP[:], op=ALU.mult), reads=[b, GREP], writes=[xb])
        for c in range(8):
            pt = PST.next()
            S.op("pe", lambda e, pt=pt, xb=xb, c=c: e.matmul(pt[:], lhsT=xb[:, c * 128:(c + 1) * 128], rhs=ID[:], start=True, stop=True), reads=[xb, ID], writes=[pt])
            if c % 2 == 0:
                S.op("act", lambda e, pt=pt, c=c, tt=tt: e.activation(out=XG[:, c, tt * 128:(tt + 1) * 128], in_=pt[:], func=AF.Copy), reads=[pt], writes=[XG])
            else:
                S.op("dve", lambda e, pt=pt, c=c, tt=tt: e.tensor_copy(out=XG[:, c, tt * 128:(tt + 1) * 128], in_=pt[:]), reads=[pt], writes=[XG])
    S.op("act", lambda e: e.activation(out=SSQ[:], in_=SSQ[:], func=AF.Sqrt, bias=EPSC[:], scale=1.0 / DM), reads=[SSQ, EPSC], writes=[SSQ])
    S.op("dve", lambda e: e.reciprocal(out=RSTD[:], in_=SSQ[:]), reads=[SSQ], writes=[RSTD])
    return XG, RSTD


def proj_loop(S, XG, RSTD, w, tiles, sinks, EPSC, BIAS=None, GAINS=None, CS=None, kmean=None, nwbuf=2):
    WF = Ring([S.sb([128, 8, 512], F32, "WF") for _ in range(nwbuf)])
    WB = Ring([S.sb([128, 8, 512], BF16, "WB") for _ in range(2)])
    PS = Ring([S.ps([128, 512], F32, "PS") for _ in range(4)])
    kinds = set(k for k, _ in tiles)
    OF = Ring([S.sb([128, 512], F32, "OF") for _ in range(3)]) if kinds & {"D", "E", "F"} else None
    TMP = Ring([S.sb([128, 512], F32, "TMP") for _ in range(2)]) if kinds & {"E", "F"} else None
    if kinds & {"A", "B", "C"}:
        Y = Ring([S.sb([128, 512], F32, "Y") for _ in range(2)])
        SQ = Ring([S.sb([128, 512], F32, "SQ") for _ in range(2)])
        RS = Ring([S.sb([128, 16], F32, "RS") for _ in range(2)])
        RT = Ring([S.sb([128, 4, 8, 8], F32, "RT") for _ in range(2)])
        OBF = Ring([S.sb([128, 512], BF16, "OBF") for _ in range(3)])
    c0 = 0
    kcol = {k: 0 for k in "ABCDEF"}
    for ti, (kind, wd) in enumerate(tiles):
        wf = WF.next()
        wb = WB.next()
        S.dma("sp" if ti % 2 == 0 else "pool", wf[:, :, 0:wd], w[:, c0:c0 + wd].rearrange("(c p) n -> p c n", p=128), writes=[wf])
        for half in range(2):
            eng = "pool" if half == 0 else "dve"
            S.op(eng, lambda e, wf=wf, wb=wb, half=half, wd=wd: e.tensor_copy(out=wb[:, 4 * half:4 * half + 4, 0:wd], in_=wf[:, 4 * half:4 * half + 4, 0:wd]),
                 reads=[wf], writes=[wb])
        k0 = kcol[kind]
        for tt in range(NTT):
            ps = PS.next()
            for c in range(8):
                S.op("pe", lambda e, ps=ps, wb=wb, c=c, tt=tt, wd=wd: e.matmul(ps[:, 0:wd], lhsT=XG[:, c, tt * 128:(tt + 1) * 128], rhs=wb[:, c, 0:wd],
                                                                               start=(c == 0), stop=(c == 7)),
                     reads=[XG, wb], writes=[ps], signal=(c == 7))
            rs_t = RSTD[:, tt:tt + 1]
            rows = slice(tt * 128, (tt + 1) * 128)
            if kind == "C":
                o = OBF.next()
                S.op("act", lambda e, o=o, ps=ps, rs_t=rs_t, wd=wd: e.activation(out=o[:, 0:wd], in_=ps[:, 0:wd], func=AF.Copy, scale=rs_t),
                     reads=[ps, RSTD], writes=[o])
                S.dma("sp", sinks["C"][rows, k0:k0 + wd], o[:, 0:wd], reads=[o])
            elif kind == "D":
                o = OF.next()
                S.op("act", lambda e, o=o, ps=ps, rs_t=rs_t, wd=wd: e.activation(out=o[:, 0:wd], in_=ps[:, 0:wd], func=AF.Silu, scale=rs_t),
                     reads=[ps, RSTD], writes=[o])
                S.dma("sp", sinks["D"][rows, k0:k0 + wd], o[:, 0:wd], reads=[o])
            elif kind == "E":
                t = TMP.next()
                o = OF.next()
                S.op("dve", lambda e, t=t, ps=ps, rs_t=rs_t, k0=k0, wd=wd: e.scalar_tensor_tensor(out=t[:, 0:wd], in0=ps[:, 0:wd], scalar=rs_t, in1=BIAS[:, k0:k0 + wd],
                                                                                               op0=ALU.mult, op1=ALU.add),
                     reads=[ps, RSTD, BIAS], writes=[t])
                S.op("act", lambda e, o=o, t=t, wd=wd: e.activation(out=o[:, 0:wd], in_=t[:, 0:wd], func=AF.Sigmoid), reads=[t], writes=[o])
                S.dma("sp", sinks["E"][rows, k0:k0 + wd], o[:, 0:wd], reads=[o])
            elif kind == "F":
                t = TMP.next()
                o = OF.next()
                S.op("act", lambda e, o=o, ps=ps, rs_t=rs_t: e.activation(out=o[:, 0:12], in_=ps[:, 0:12], func=AF.Sigmoid, scale=rs_t),
                     reads=[ps, RSTD], writes=[o])
                S.op("dve", lambda e, t=t, ps=ps, rs_t=rs_t: e.scalar_tensor_tensor(out=t[:, 0:6], in0=ps[:, 12:18], scalar=rs_t, in1=BIAS[:, 0:6],
                                                                                 op0=ALU.mult, op1=ALU.add),
                     reads=[ps, RSTD, BIAS], writes=[t])
                S.op("act", lambda e, t=t: e.activation(out=t[:, 8:14], in_=t[:, 0:6], func=AF.Exp, scale=-1.0), reads=[t], writes=[t])
                S.op("act", lambda e, t=t: e.activation(out=t[:, 16:22], in_=t[:, 8:14], func=AF.Ln, bias=1.0), reads=[t], writes=[t])
                S.op("dve", lambda e, t=t, o=o: e.tensor_scalar(out=o[:, 12:18], in0=t[:, 16:22], scalar1=-1.0, scalar2=None, op0=ALU.mult),
                     reads=[t], writes=[o])
                S.dma("sp", sinks["F"][rows, :], o[:, 0:18], reads=[o])
            else:
                nh = wd // 64
                y = Y.next(); sq = SQ.next(); rs = RS.next(); o = OBF.next()
                goff = k0 if kind == "A" else 1152 + k0
                S.op("act", lambda e, y=y, ps=ps, rs_t=rs_t, wd=wd: e.activation(out=y[:, 0:wd], in_=ps[:, 0:wd], func=AF.Copy, scale=rs_t),
                     reads=[ps, RSTD], writes=[y])
                S.op("pool", lambda e, y=y, sq=sq, wd=wd: e.tensor_tensor(out=sq[:, 0:wd], in0=y[:, 0:wd], in1=y[:, 0:wd], op=ALU.mult),
                     reads=[y], writes=[sq])
                S.op("dve", lambda e, sq=sq, rs=rs, nh=nh, wd=wd: e.tensor_reduce(out=rs[:, 0:nh], in_=sq[:, 0:wd].rearrange("p (h d) -> p h d", d=64), axis=AX.X, op=ALU.add),
                     reads=[sq], writes=[rs])
                S.op("act", lambda e, rs=rs, nh=nh: e.activation(out=rs[:, 0:nh], in_=rs[:, 0:nh], func=AF.Sqrt, bias=EPSC[:], scale=1.0 / 64), reads=[rs, EPSC], writes=[rs])
                S.op("dve", lambda e, rs=rs, nh=nh: e.reciprocal(out=rs[:, 0:nh], in_=rs[:, 0:nh]), reads=[rs], writes=[rs])
                S.op("dve", lambda e, y=y, rs=rs, nh=nh, wd=wd: e.tensor_tensor(out=y[:, 0:wd].rearrange("p (h d) -> p h d", d=64), in0=y[:, 0:wd].rearrange("p (h d) -> p h d", d=64),
                                                                              in1=rs[:, 0:nh].unsqueeze(2).to_broadcast([128, nh, 64]), op=ALU.mult),
                     reads=[y, rs], writes=[y])
                if kind == "B":
                    S.op("pool", lambda e, y=y, o=o, goff=goff, wd=wd: e.tensor_tensor(out=o[:, 0:wd], in0=y[:, 0:wd], in1=GAINS[:, goff:goff + wd], op=ALU.mult),
                         reads=[y, GAINS], writes=[o])
                    S.dma("sp", sinks["B"][rows, k0:k0 + wd], o[:, 0:wd], reads=[o])
                else:
                    rt = RT.next()
                    S.op("pool", lambda e, y=y, goff=goff, wd=wd: e.tensor_tensor(out=y[:, 0:wd], in0=y[:, 0:wd], in1=GAINS[:, goff:goff + wd], op=ALU.mult),
                         reads=[y, GAINS], writes=[y])
                    yv = y[:, 0:wd].rearrange("p (h d) -> p h d", d=64)
                    ov = o[:, 0:wd].rearrange("p (h d) -> p h d", d=64)
                    cosb = CS[:, 0, tt, :].unsqueeze(1).to_broadcast([128, nh, 8])
                    sinb = CS[:, 1, tt, :].unsqueeze(1).to_broadcast([128, nh, 8])
                    S.op("act", lambda e, o=o, y=y, wd=wd: e.activation(out=o[:, 0:wd], in_=y[:, 0:wd], func=AF.Copy), reads=[y], writes=[o])
                    S.op("dve", lambda e, rt=rt, yv=yv, cosb=cosb, nh=nh: e.tensor_tensor(out=rt[:, 0, 0:nh, :], in0=yv[:, :, 0:8], in1=cosb, op=ALU.mult), reads=[y, CS], writes=[rt])
                    S.op("dve", lambda e, rt=rt, yv=yv, sinb=sinb, nh=nh: e.tensor_tensor(out=rt[:, 1, 0:nh, :], in0=yv[:, :, 8:16], in1=sinb, op=ALU.mult), reads=[y, CS], writes=[rt])
                    S.op("dve", lambda e, rt=rt, yv=yv, cosb=cosb, nh=nh: e.tensor_tensor(out=rt[:, 2, 0:nh, :], in0=yv[:, :, 8:16], in1=cosb, op=ALU.mult), reads=[y, CS], writes=[rt])
                    S.op("dve", lambda e, rt=rt, yv=yv, sinb=sinb, nh=nh: e.tensor_tensor(out=rt[:, 3, 0:nh, :], in0=yv[:, :, 0:8], in1=sinb, op=ALU.mult), reads=[y, CS], writes=[rt])
                    S.op("dve", lambda e, rt=rt, ov=ov, nh=nh: e.tensor_tensor(out=ov[:, :, 0:8], in0=rt[:, 0, 0:nh, :], in1=rt[:, 1, 0:nh, :], op=ALU.subtract), reads=[rt], writes=[o])
                    S.op("dve", lambda e, rt=rt, ov=ov, nh=nh: e.tensor_tensor(out=ov[:, :, 8:16], in0=rt[:, 2, 0:nh, :], in1=rt[:, 3, 0:nh, :], op=ALU.add), reads=[rt], writes=[o])
                    if kmean is not None:
                        KMP, C256 = kmean
                        for hc in range(0, wd, 64):
                            gc = k0 + hc
                            if 384 <= gc < 768:
                                h = (gc - 384) // 64
                                S.op("pe", lambda e, o=o, hc=hc, h=h, tt=tt: e.matmul(KMP[:, h * NTT + tt:h * NTT + tt + 1], lhsT=o[:, hc:hc + 64], rhs=C256[:, 0:1], start=True, stop=True),
                                     reads=[o, C256], writes=[KMP])
                    S.dma("sp", sinks["A"][rows, k0:k0 + wd], o[:, 0:wd], reads=[o])
        kcol[kind] += wd
        c0 += wd


def build_p1():
    nc = bass.Bass("TRN2", target_bir_lowering=False)
    x = dram_in(nc, "x", [TOK, DM], F32)
    gnorm = dram_in(nc, "gnorm", [1, DM], F32)
    w = dram_in(nc, "w", [DM, P1_NCOLS], F32)
    cs = dram_in(nc, "cs", [128, 2, NTT, 8], F32)
    gains = dram_in(nc, "gains", [1, 1920], F32)
    bias = dram_in(nc, "bias", [1, 6], F32)
    ident = dram_in(nc, "ident", [128, 128], BF16)
    oA = dram_out(nc, "oA", [TOK, 1152], BF16)
    oB = dram_out(nc, "oB", [TOK, 768], BF16)
    oC = dram_out(nc, "oC", [TOK, 1024], BF16)
    oF = dram_out(nc, "oF", [TOK, 18], F32)
    okm = dram_out(nc, "okm", [64, 48], F32)
    with ExitStack() as st:
        S = Sched(nc, st)
        EPSC = S.sb([128, 1], F32, "EPSC")
        S.op("dve", lambda e: e.memset(EPSC[:], EPS), writes=[EPSC])
        C256 = S.sb([128, 1], BF16, "C256")
        S.op("dve", lambda e: e.memset(C256[:], 1.0 / 256), writes=[C256])
        ID = S.sb([128, 128], BF16, "ID")
        S.dma("sp", ID[:], ident, writes=[ID])
        CS = S.sb([128, 2, NTT, 8], F32, "CS")
        GAINS = S.sb([128, 1920], F32, "GAINS")
        BIAS = S.sb([128, 6], F32, "BIAS")
        S.dma("sp", CS[:], cs, writes=[CS])
        S.dma("pool", GAINS[:], gains.partition_broadcast(128), writes=[GAINS])
        S.dma("pool", BIAS[:], bias.partition_broadcast(128), writes=[BIAS])
        PST = Ring([S.ps([128, 128], F32, "PST") for _ in range(2)])
        KMP = S.ps([64, 96], F32, "KMP")
        XG, RSTD = emit_xg(S, x, gnorm, ID, EPSC, PST)
        proj_loop(S, XG, RSTD, w, P1_TILES, {"A": oA, "B": oB, "C": oC, "F": oF}, EPSC, BIAS=BIAS, GAINS=GAINS, CS=CS, kmean=(KMP, C256))
        KMS = S.sb([64, 96], F32, "KMS")
        KM8 = S.sb([64, 48], F32, "KM8")
        S.op("act", lambda e: e.activation(out=KMS[:], in_=KMP[:], func=AF.Copy), reads=[KMP], writes=[KMS])
        kv = KMS[:].rearrange("p (h b t) -> p h b t", h=6, t=2)
        S.op("dve", lambda e: e.tensor_tensor(out=KM8[:].rearrange("p (h b) -> p h b", h=6), in0=kv[:, :, :, 0], in1=kv[:, :, :, 1], op=ALU.add), reads=[KMS], writes=[KM8])
        S.dma("sp", okm, KM8[:], reads=[KM8])
        S.finish()
    return nc


def rope_cs(pos):
    inv = (500000.0 ** (-np.arange(0, 16, 2, dtype=np.float32) / np.float32(16))).astype(np.float32)
    ang = pos.astype(np.float32)[:, None] * inv[None, :]
    return np.cos(ang).astype(np.float32), np.sin(ang).astype(np.float32)


def run_p1(xl, norm_g, w_in, b_f, b_gate, moba_qk_g, nsa_q_g, nsa_k_g, fox_qk_g):
    nc = get_nc("p1", build_p1)
    wr = np.ascontiguousarray(w_in[:, p1_col_perm()])
    gains = np.concatenate([np.tile(moba_qk_g[0], 6), np.tile(moba_qk_g[1], 6), np.tile(nsa_q_g, 4), nsa_k_g[1], nsa_k_g[2],
                            np.tile(fox_qk_g[0], 6), np.tile(fox_qk_g[1], 6)])[None, :].astype(np.float32)
    in_maps = []
    for c in range(NCORES):
        xs = xl[c * TOK:(c + 1) * TOK]
        cos, sin = rope_cs(np.arange(c * TOK, (c + 1) * TOK))
        cs = np.stack([cos.reshape(NTT, 128, 8).transpose(1, 0, 2), sin.reshape(NTT, 128, 8).transpose(1, 0, 2)], axis=1)
        in_maps.append({"x": np.ascontiguousarray(xs), "gnorm": np.ascontiguousarray(norm_g[None, :]), "w": wr,
                        "cs": np.ascontiguousarray(cs), "gains": gains, "bias": np.ascontiguousarray(b_f[None, :]), "ident": IDENT})
    res = run(nc, in_maps)
    out = {}
    for k in ["oA", "oB", "oC", "oF"]:
        out[k] = np.concatenate([r[k] for r in res], axis=0)
    out["kmT"] = np.ascontiguousarray(np.concatenate([r["okm"].reshape(64, 6, 8) for r in res], axis=2).reshape(64, 384))
    return out


def build_attn(jobs, tag):
    nc = bass.Bass("TRN2", target_bir_lowering=False)
    ident = dram_in(nc, "ident", [128, 128], BF16)
    ins = []
    for j, jb in enumerate(jobs):
        ins.append(dict(
            ka=dram_in(nc, f"ka{j}", [jb["Kc"], jb["NK"]], BF16),
            va=dram_in(nc, f"va{j}", [128, (jb["NK"] // 128) * jb["W"]], BF16),
            qa=dram_in(nc, f"qa{j}", [jb["Kc"], jb["nvar"] * jb["NQ"]], BF16),
            mk=dram_in(nc, f"mk{j}", [128, jb["nmask"] * 512], BF16),
            o=dram_out(nc, f"o{j}", [jb["W"], jb["NQ"]], F32)))
    mNK = max(jb["NK"] for jb in jobs)
    mVA = max((jb["NK"] // 128) * jb["W"] for jb in jobs)
    mQA = max(jb["nvar"] * jb["NQ"] for jb in jobs)
    mMK = max(jb["nmask"] for jb in jobs)
    nset = min(2, len(jobs))
    with ExitStack() as st:
        S = Sched(nc, st)
        ID = S.sb([128, 128], BF16, "ID")
        S.dma("sp", ID[:], ident, writes=[ID])
        sets = [dict(KA=S.sb([128, mNK], BF16, "KA"), VA=S.sb([128, mVA], BF16, "VA"), QA=S.sb([128, mQA], BF16, "QA"),
                     MK=S.sb([128, mMK * 512], BF16, "MK")) for _ in range(nset)]
        PSS = Ring([S.ps([128, 512], F32, "PSS") for _ in range(4)])
        LOOK = 2
        ACCB = [S.ps([128, 512], F32, "ACC") for _ in range(4)]
        acc_i = [0]
        PT = Ring([S.sb([128, 512], BF16, "PT") for _ in range(3)])
        OB = Ring([S.sb([128, 512], F32, "OB") for _ in range(3)])
        for j, jb in enumerate(jobs):
            sset = sets[j % nset]
            KA, VA, QA, MK = sset["KA"], sset["VA"], sset["QA"], sset["MK"]
            Kc, W, NQ = jb["Kc"], jb["W"], jb["NQ"]
            io = ins[j]
            S.dma("sp", KA[0:Kc, 0:jb["NK"]], io["ka"], writes=[KA])
            S.dma("sp", QA[0:Kc, 0:jb["nvar"] * NQ], io["qa"], writes=[QA])
            S.dma("sp", VA[:, 0:(jb["NK"] // 128) * W], io["va"], writes=[VA])
            S.dma("sp", MK[:, 0:jb["nmask"] * 512], io["mk"], writes=[MK])
            chunks = [(c0_, min(128, W - c0_)) for c0_ in range(0, W, 128)]
            pendq = []
            for lg, pairs in enumerate(jb["sched"]):
                npairs = len(pairs)
                if len(chunks) == 1:
                    accs = [ACCB[acc_i[0] % 4]]
                    acc_i[0] += 1
                else:
                    accs = ACCB[0:len(chunks)]

                def emit_pv(pt, kt, first, last, lg=lg, accs=accs):
                    for ci, (cc, wc) in enumerate(chunks):
                        acc = accs[ci]
                        S.op("pe", lambda e, pt=pt, kt=kt, first=first, last=last, W=W, VA=VA, acc=acc, cc=cc, wc=wc: e.matmul(
                            acc[0:wc, :], lhsT=VA[:, kt * W + cc:kt * W + cc + wc], rhs=pt[:], start=first, stop=last),
                            reads=[pt, VA], writes=[acc], signal=(ci == len(chunks) - 1))
                    if last:
                        for ci, (cc, wc) in enumerate(chunks):
                            acc = accs[ci]
                            ob = OB.next()
                            S.op("dve", lambda e, ob=ob, acc=acc, wc=wc: e.tensor_copy(out=ob[0:wc, :], in_=acc[0:wc, :]), reads=[acc], writes=[ob])
                            S.dma("pool", io["o"][cc:cc + wc, lg * 512:(lg + 1) * 512], ob[0:wc, :], reads=[ob])

                for pi, (kt, var, midx) in enumerate(pairs):
                    ps = PSS.next()
                    q0 = var * NQ + lg * 512
                    S.op("pe", lambda e, ps=ps, kt=kt, q0=q0, midx=midx, KA=KA, QA=QA, Kc=Kc: e.matmul(
                        ps[:], lhsT=KA[0:Kc, kt * 128:(kt + 1) * 128], rhs=QA[0:Kc, q0:q0 + 512], start=True, stop=(midx is None)),
                        reads=[KA, QA], writes=[ps], signal=(midx is None))
                    if midx is not None:
                        S.op("pe", lambda e, ps=ps, midx=midx, MK=MK: e.matmul(
                            ps[:], lhsT=ID[:], rhs=MK[:, midx * 512:(midx + 1) * 512], start=False, stop=True),
                            reads=[ID, MK], writes=[ps], signal=True)
                    pt = PT.next()
                    S.op("act", lambda e, pt=pt, ps=ps: e.activation(out=pt[:], in_=ps[:], func=AF.Exp, scale=SCALE), reads=[ps], writes=[pt])
                    pendq.append((emit_pv, (pt, kt, pi == 0, pi == npairs - 1)))
                    if len(pendq) > LOOK:
                        f_, a_ = pendq.pop(0)
                        f_(*a_)
            for f_, a_ in pendq:
                f_(*a_)
        S.finish()
    return nc


def build_p2():
    nc = bass.Bass("TRN2", target_bir_lowering=False)
    lf = dram_in(nc, "lf", [6, S_LEN], F32)
    flk = dram_in(nc, "flk", [128, 16, 128], BF16)
    flv = dram_in(nc, "flv", [128, 16, 128], BF16)
    w1 = dram_in(nc, "w1", [128, 2, 16, 256], F32)
    w2 = dram_in(nc, "w2", [128, 2, 2, 64], F32)
    pe = dram_in(nc, "pe", [128, 2, 16], F32)
    kgain = dram_in(nc, "kgain", [1, 64], F32)
    csc = dram_in(nc, "csc", [128, 2, 8], F32)
    qT = dram_in(nc, "qT", [64, 6 * TOK], BF16)
    kmT = dram_in(nc, "kmT", [64, 384], F32)
    gm = dram_in(nc, "gm", [128, 3, NTT, 64], F32)
    csplit = dram_out(nc, "csplit", [6, 6, S_LEN], BF16)
    kcmp = dram_out(nc, "kcmp", [128, 64], BF16)
    vcmp = dram_out(nc, "vcmp", [128, 64], BF16)
    mb = dram_out(nc, "mb", [TOK, 384], BF16)
    with ExitStack() as st:
        S = Sched(nc, st)
        EPSC = S.sb([128, 1], F32, "EPSC")
        S.op("dve", lambda e: e.memset(EPSC[:], EPS), writes=[EPSC])
        CH = 512
        ONES = S.sb([6, CH], F32, "ONES")
        S.op("dve", lambda e: e.memset(ONES[:], 1.0), writes=[ONES])
        LFr = Ring([S.sb([6, CH], F32, "LF") for _ in range(2)])
        Cr = Ring([S.sb([6, CH], F32, "C8") for _ in range(2)])
        OCr = Ring([S.sb([6, 6, CH], BF16, "OC") for _ in range(2)])
        Hf = S.sb([6, CH], F32, "Hf")
        R1 = S.sb([6, CH], F32, "R1")
        prev = None
        for ci in range(S_LEN // CH):
            l = LFr.next(); c8 = Cr.next(); oc = OCr.next()
            S.dma("sp", l[:], lf[:, ci * CH:(ci + 1) * CH], writes=[l])
            S.op("dve", lambda e, l=l: e.tensor_scalar(out=l[:], in0=l[:], scalar1=8.0, scalar2=None, op0=ALU.mult), reads=[l], writes=[l])
            if prev is None:
                S.op("dve", lambda e, l=l, c8=c8: e.tensor_tensor_scan(out=c8[:], data0=ONES[:], data1=l[:], initial=0.0, op0=ALU.mult, op1=ALU.add),
                     reads=[ONES, l], writes=[c8])
            else:
                S.op("dve", lambda e, l=l, c8=c8, prev=prev: e.tensor_tensor_scan(out=c8[:], data0=ONES[:], data1=l[:], initial=prev[:, CH - 1:CH], op0=ALU.mult, op1=ALU.add),
                     reads=[ONES, l, prev], writes=[c8])
            prev = c8
            S.op("dve", lambda e, c8=c8, oc=oc: e.tensor_copy(out=oc[:, 0, :], in_=c8[:]), reads=[c8], writes=[oc])
            S.op("dve", lambda e, oc=oc: e.tensor_copy(out=Hf[:], in_=oc[:, 0, :]), reads=[oc], writes=[Hf])
            S.op("dve", lambda e, c8=c8: e.tensor_tensor(out=R1[:], in0=c8[:], in1=Hf[:], op=ALU.subtract), reads=[c8, Hf], writes=[R1])
            S.op("dve", lambda e, oc=oc: e.tensor_copy(out=oc[:, 1, :], in_=R1[:]), reads=[R1], writes=[oc])
            S.op("dve", lambda e, oc=oc: e.tensor_copy(out=Hf[:], in_=oc[:, 1, :]), reads=[oc], writes=[Hf])
            S.op("dve", lambda e: e.tensor_tensor(out=R1[:], in0=R1[:], in1=Hf[:], op=ALU.subtract), reads=[R1, Hf], writes=[R1])
            S.op("dve", lambda e, oc=oc: e.tensor_copy(out=oc[:, 2, :], in_=R1[:]), reads=[R1], writes=[oc])
            S.op("dve", lambda e, oc=oc: e.tensor_scalar(out=oc[:, 3:6, :], in0=oc[:, 0:3, :], scalar1=-1.0, scalar2=None, op0=ALU.mult), reads=[oc], writes=[oc])
            S.dma("sp", csplit[:, :, ci * CH:(ci + 1) * CH], oc[:], reads=[oc])
        W1F = S.sb([128, 2, 16, 256], F32, "W1F"); W1B = S.sb([128, 2, 16, 256], BF16, "W1B")
        W2F = S.sb([128, 2, 2, 64], F32, "W2F"); W2B = S.sb([128, 2, 2, 64], BF16, "W2B")
        PEF = S.sb([128, 2, 16], F32, "PEF"); PEB = S.sb([128, 2, 16], BF16, "PEB")
        FL = [S.sb([128, 16, 128], BF16, "FLK"), S.sb([128, 16, 128], BF16, "FLV")]
        KG = S.sb([128, 64], F32, "KG"); CSC = S.sb([128, 2, 8], F32, "CSC")
        S.dma("sp", W1F[:], w1, writes=[W1F]); S.dma("sp", W2F[:], w2, writes=[W2F]); S.dma("sp", PEF[:], pe, writes=[PEF])
        S.dma("sp", FL[0][:], flk, writes=[FL[0]]); S.dma("sp", FL[1][:], flv, writes=[FL[1]])
        S.dma("sp", KG[:], kgain.partition_broadcast(128), writes=[KG]); S.dma("sp", CSC[:], csc, writes=[CSC])
        S.op("dve", lambda e: e.tensor_copy(out=W1B[:], in_=W1F[:]), reads=[W1F], writes=[W1B])
        S.op("dve", lambda e: e.tensor_copy(out=W2B[:], in_=W2F[:]), reads=[W2F], writes=[W2B])
        S.op("dve", lambda e: e.tensor_copy(out=PEB[:], in_=PEF[:]), reads=[PEF], writes=[PEB])
        B1 = S.sb([128, 4], F32, "B1")
        HS = S.sb([128, 2, 128], BF16, "HS")
        PB = S.ps([128, 8], F32, "PB")
        PH = Ring([S.ps([128, 128], F32, "PH") for _ in range(2)])
        PO = S.ps([128, 64], F32, "PO")
        YC = S.sb([128, 64], F32, "YC"); SQC = S.sb([128, 64], F32, "SQC"); RSC = S.sb([128, 1], F32, "RSC")
        RTC = S.sb([128, 4, 8], F32, "RTC"); OKC = S.sb([128, 64], BF16, "OKC"); OVC = S.sb([128, 64], BF16, "OVC")
        for kv in range(2):
            for half in range(2):
                idx = kv * 2 + half
                for j in range(16):
                    S.op("pe", lambda e, kv=kv, half=half, j=j, idx=idx: e.matmul(PB[:, idx:idx + 1], lhsT=W1B[:, kv, j, half * 128:(half + 1) * 128], rhs=PEB[:, kv, j:j + 1],
                                                                                   start=(j == 0), stop=(j == 15)), reads=[W1B, PEB], writes=[PB], signal=(j == 15))
                S.op("dve", lambda e, idx=idx: e.tensor_copy(out=B1[:, idx:idx + 1], in_=PB[:, idx:idx + 1]), reads=[PB], writes=[B1])
                ph = PH.next()
                for j in range(16):
                    S.op("pe", lambda e, ph=ph, kv=kv, half=half, j=j: e.matmul(ph[:], lhsT=W1B[:, kv, j, half * 128:(half + 1) * 128], rhs=FL[kv][:, j, :],
                                                                                start=(j == 0), stop=(j == 15)), reads=[W1B, FL[kv]], writes=[ph], signal=(j == 15))
                S.op("act", lambda e, ph=ph, half=half, idx=idx: e.activation(out=HS[:, half, :], in_=ph[:], func=AF.Silu, bias=B1[:, idx:idx + 1]),
                     reads=[ph, B1], writes=[HS])
            for half in range(2):
                S.op("pe", lambda e, kv=kv, half=half: e.matmul(PO[:], lhsT=HS[:, half, :], rhs=W2B[:, kv, half, :], start=(half == 0), stop=(half == 1)),
                     reads=[HS, W2B], writes=[PO], signal=(half == 1))
            if kv == 1:
                S.op("act", lambda e: e.activation(out=OVC[:], in_=PO[:], func=AF.Copy), reads=[PO], writes=[OVC])
                S.dma("sp", vcmp, OVC[:], reads=[OVC])
            else:
                S.op("act", lambda e: e.activation(out=YC[:], in_=PO[:], func=AF.Copy), reads=[PO], writes=[YC])
                S.op("dve", lambda e: e.tensor_tensor(out=SQC[:], in0=YC[:], in1=YC[:], op=ALU.mult), reads=[YC], writes=[SQC])
                S.op("dve", lambda e: e.tensor_reduce(out=RSC[:], in_=SQC[:], axis=AX.X, op=ALU.add), reads=[SQC], writes=[RSC])
                S.op("act", lambda e: e.activation(out=RSC[:], in_=RSC[:], func=AF.Sqrt, bias=EPSC[:], scale=1.0 / 64), reads=[RSC, EPSC], writes=[RSC])
                S.op("dve", lambda e: e.reciprocal(out=RSC[:], in_=RSC[:]), reads=[RSC], writes=[RSC])
                S.op("dve", lambda e: e.scalar_tensor_tensor(out=YC[:], in0=YC[:], scalar=RSC[:, 0:1], in1=KG[:], op0=ALU.mult, op1=ALU.mult), reads=[YC, RSC, KG], writes=[YC])
                S.op("act", lambda e: e.activation(out=OKC[:], in_=YC[:], func=AF.Copy), reads=[YC], writes=[OKC])
                S.op("dve", lambda e: e.tensor_tensor(out=RTC[:, 0, :], in0=YC[:, 0:8], in1=CSC[:, 0, :], op=ALU.mult), reads=[YC, CSC], writes=[RTC])
                S.op("dve", lambda e: e.tensor_tensor(out=RTC[:, 1, :], in0=YC[:, 8:16], in1=CSC[:, 1, :], op=ALU.mult), reads=[YC, CSC], writes=[RTC])
                S.op("dve", lambda e: e.tensor_tensor(out=RTC[:, 2, :], in0=YC[:, 8:16], in1=CSC[:, 0, :], op=ALU.mult), reads=[YC, CSC], writes=[RTC])
                S.op("dve", lambda e: e.tensor_tensor(out=RTC[:, 3, :], in0=YC[:, 0:8], in1=CSC[:, 1, :], op=ALU.mult), reads=[YC, CSC], writes=[RTC])
                S.op("dve", lambda e: e.tensor_tensor(out=OKC[:, 0:8], in0=RTC[:, 0, :], in1=RTC[:, 1, :], op=ALU.subtract), reads=[RTC], writes=[OKC])
                S.op("dve", lambda e: e.tensor_tensor(out=OKC[:, 8:16], in0=RTC[:, 2, :], in1=RTC[:, 3, :], op=ALU.add), reads=[RTC], writes=[OKC])
                S.dma("sp", kcmp, OKC[:], reads=[OKC])
        KMF = S.sb([64, 384], F32, "KMF")
        KMB = S.sb([64, 384], BF16, "KMB")
        S.dma("sp", KMF[:], kmT, writes=[KMF])
        S.op("act", lambda e: e.activation(out=KMB[:], in_=KMF[:], func=AF.Copy), reads=[KMF], writes=[KMB])
        QT = S.sb([64, 6 * TOK], BF16, "QT")
        S.dma("sp", QT[:], qT, writes=[QT])
        GM = S.sb([128, 3, NTT, 64], F32, "GM")
        S.dma("sp", GM[:], gm, writes=[GM])
        PG = Ring([S.ps([128, 64], F32, "PG") for _ in range(2)])
        GS = Ring([S.sb([128, 64], F32, "GS") for _ in range(2)])
        M8 = Ring([S.sb([128, 8], F32, "M8") for _ in range(2)])
        T1 = Ring([S.sb([128, 64], F32, "T1") for _ in range(2)])
        MBO = Ring([S.sb([128, 384], BF16, "MBO") for _ in range(2)])
        for lt in range(NTT):
            mbo = MBO.next()
            for h in range(6):
                pg = PG.next(); gs = GS.next(); m8 = M8.next(); t1 = T1.next()
                S.op("pe", lambda e, pg=pg, h=h, lt=lt: e.matmul(pg[:], lhsT=QT[:, h * TOK + lt * 128:h * TOK + (lt + 1) * 128], rhs=KMB[:, h * 64:(h + 1) * 64], start=True, stop=True),
                     reads=[QT, KMB], writes=[pg])
                S.op("dve", lambda e, pg=pg, gs=gs, lt=lt: e.tensor_tensor(out=gs[:], in0=pg[:], in1=GM[:, 0, lt, :], op=ALU.add), reads=[pg, GM], writes=[gs])
                S.op("dve", lambda e, gs=gs, m8=m8: e.max(out=m8[:], in_=gs[:]), reads=[gs], writes=[m8])
                S.op("dve", lambda e, gs=gs, m8=m8, t1=t1, lt=lt: e.scalar_tensor_tensor(out=t1[:], in0=gs[:], scalar=m8[:, 2:3], in1=GM[:, 1, lt, :], op0=ALU.is_ge, op1=ALU.mult),
                     reads=[gs, m8, GM], writes=[t1])
                S.op("dve", lambda e, t1=t1, lt=lt: e.tensor_tensor(out=t1[:], in0=t1[:], in1=GM[:, 2, lt, :], op=ALU.add), reads=[t1, GM], writes=[t1])
                S.op("dve", lambda e, t1=t1, mbo=mbo, h=h: e.tensor_scalar(out=mbo[:, h * 64:(h + 1) * 64], in0=t1[:], scalar1=-NEGM, scalar2=NEGM, op0=ALU.mult, op1=ALU.add),
                     reads=[t1], writes=[mbo])
            S.dma("sp", mb[lt * 128:(lt + 1) * 128, :], mbo[:], reads=[mbo])
        S.finish()
    return nc


def build_p4():
    nc = bass.Bass("TRN2", target_bir_lowering=False)
    oc = dram_in(nc, "oc", [4, TOK, 257], F32)
    cm = dram_in(nc, "cm", [128, 3, NTT, 256], F32)
    mbs = dram_out(nc, "mbs", [TOK, 256], BF16)
    with ExitStack() as st:
        S = Sched(nc, st)
        CM = S.sb([128, 3, NTT, 256], F32, "CM")
        S.dma("sp", CM[:], cm, writes=[CM])
        OC = Ring([S.sb([128, 4, 257], F32, "OC") for _ in range(2)])
        RD = Ring([S.sb([128, 4], F32, "RD") for _ in range(2)])
        IMP = Ring([S.sb([128, 256], F32, "IMP") for _ in range(2)])
        RR = Ring([S.sb([128, 256], F32, "RR") for _ in range(2)])
        M8 = Ring([S.sb([128, 16], F32, "M8") for _ in range(2)])
        MBO = Ring([S.sb([128, 256], BF16, "MBO") for _ in range(2)])
        for lt in range(NTT):
            o = OC.next(); rd = RD.next(); imp = IMP.next(); rr = RR.next(); m8 = M8.next(); mbo = MBO.next()
            S.dma("sp", o[:], oc[:, lt * 128:(lt + 1) * 128, :].rearrange("h q w -> q h w"), writes=[o])
            S.op("dve", lambda e, o=o, rd=rd: e.tensor_scalar(out=rd[:], in0=o[:, :, 0], scalar1=1e-30, scalar2=None, op0=ALU.max), reads=[o], writes=[rd])
            S.op("dve", lambda e, rd=rd: e.reciprocal(out=rd[:], in_=rd[:]), reads=[rd], writes=[rd])
            S.op("dve", lambda e, o=o, rd=rd, imp=imp: e.tensor_scalar(out=imp[:], in0=o[:, 0, 1:257], scalar1=rd[:, 0:1], scalar2=None, op0=ALU.mult), reads=[o, rd], writes=[imp])
            for h in range(1, 4):
                S.op("dve", lambda e, o=o, rd=rd, imp=imp, h=h: e.scalar_tensor_tensor(out=imp[:], in0=o[:, h, 1:257], scalar=rd[:, h:h + 1], in1=imp[:], op0=ALU.mult, op1=ALU.add),
                     reads=[o, rd, imp], writes=[imp])
            S.op("dve", lambda e, imp=imp, lt=lt: e.tensor_tensor(out=imp[:], in0=imp[:], in1=CM[:, 0, lt, :], op=ALU.add), reads=[imp, CM], writes=[imp])
            S.op("dve", lambda e, imp=imp, m8=m8: e.max(out=m8[:, 0:8], in_=imp[:]), reads=[imp], writes=[m8])
            S.op("dve", lambda e, imp=imp, m8=m8, rr=rr: e.match_replace(out=rr[:], in_to_replace=m8[:, 0:8], in_values=imp[:], imm_value=-1e30), reads=[imp, m8], writes=[rr])
            S.op("dve", lambda e, rr=rr, m8=m8: e.max(out=m8[:, 8:16], in_=rr[:]), reads=[rr], writes=[m8])
            S.op("dve", lambda e, imp=imp, m8=m8, rr=rr, lt=lt: e.scalar_tensor_tensor(out=rr[:], in0=imp[:], scalar=m8[:, 12:13], in1=CM[:, 1, lt, :], op0=ALU.is_ge, op1=ALU.mult),
                 reads=[imp, m8, CM], writes=[rr])
            S.op("dve", lambda e, rr=rr, lt=lt: e.tensor_tensor(out=rr[:], in0=rr[:], in1=CM[:, 2, lt, :], op=ALU.add), reads=[rr, CM], writes=[rr])
            S.op("dve", lambda e, rr=rr, mbo=mbo: e.tensor_scalar(out=mbo[:], in0=rr[:], scalar1=-NEGM, scalar2=NEGM, op0=ALU.mult, op1=ALU.add), reads=[rr], writes=[mbo])
            S.dma("sp", mbs[lt * 128:(lt + 1) * 128, :], mbo[:], reads=[mbo])
        S.finish()
    return nc


def build_p5():
    nc = bass.Bass("TRN2", target_bir_lowering=False)
    xs = dram_in(nc, "xs", [TOK, DM], F32)
    gnorm = dram_in(nc, "gnorm", [1, DM], F32)
    wz = dram_in(nc, "wz", [DM, P5_NCOLS], F32)
    bgate = dram_in(nc, "bgate", [1, 3072], F32)
    ng = dram_in(nc, "ng", [TOK, 18], F32)
    om = dram_in(nc, "om", [TOK, 6 * 65], F32)
    ofx = dram_in(nc, "ofx", [TOK, 6 * 65], F32)
    on = dram_in(nc, "on", [TOK, 3, 4 * 65], F32)
    wup = dram_in(nc, "wup", [DM, DM], F32)
    wout = dram_in(nc, "wout", [DM, DM], F32)
    ident = dram_in(nc, "ident", [128, 128], BF16)
    out = dram_out(nc, "out", [TOK, DM], F32)
    zz = nc.dram_tensor("zz_scr", [TOK, 1024], F32).ap()
    gg = nc.dram_tensor("gg_scr", [TOK, 3072], F32).ap()
    with ExitStack() as st:
        S = Sched(nc, st)
        ID = S.sb([128, 128], BF16, "ID")
        S.dma("sp", ID[:], ident, writes=[ID])
        EPSC = S.sb([128, 1], F32, "EPSC")
        S.op("dve", lambda e: e.memset(EPSC[:], EPS), writes=[EPSC])
        PST = Ring([S.ps([128, 128], F32, "PST") for _ in range(2)])
        with ExitStack() as stA:
            SA = S
            old_stack = S.stack
            S.stack = stA
            BIAS = S.sb([128, 3072], F32, "BIAS")
            S.dma("pool", BIAS[:], bgate.partition_broadcast(128), writes=[BIAS])
            XG, RSTD = emit_xg(S, xs, gnorm, ID, EPSC, PST)
            proj_loop(S, XG, RSTD, wz, P5_TILES, {"D": zz, "E": gg}, EPSC, BIAS=BIAS, nwbuf=2)
            S.stack = old_stack
            S.barrier()
            S.finish_part()
        WUP = S.sb([128, 8, DM], BF16, "WUP"); WOUT = S.sb([128, 8, DM], BF16, "WOUT")
        WF = Ring([S.sb([128, 2, DM], F32, "WF") for _ in range(2)])
        for wi, (src, dst) in enumerate(((wup, WUP), (wout, WOUT))):
            for q4 in range(4):
                wf = WF.next()
                S.dma("sp", wf[:], src[q4 * 256:(q4 + 1) * 256, :].rearrange("(c p) n -> p c n", p=128), writes=[wf])
                S.op("pool", lambda e, wf=wf, dst=dst, q4=q4: e.tensor_copy(out=dst[:, 2 * q4:2 * q4 + 2, :], in_=wf[:]), reads=[wf], writes=[dst])
        X = Ring([S.sb([128, DM], F32, "X") for _ in range(2)])
        Z = Ring([S.sb([128, DM], F32, "Z") for _ in range(2)])
        G = Ring([S.sb([128, 3072], F32, "G") for _ in range(2)])
        NG = Ring([S.sb([128, 18], F32, "NG") for _ in range(2)])
        OM = Ring([S.sb([128, 6, 65], F32, "OM") for _ in range(2)])
        OFX = Ring([S.sb([128, 6, 65], F32, "OFX") for _ in range(2)])
        ON = Ring([S.sb([128, 3, 4, 65], F32, "ON") for _ in range(2)])
        RD = Ring([S.sb([128, 32], F32, "RD") for _ in range(2)])
        T32 = Ring([S.sb([128, DM], F32, "T32") for _ in range(2)])
        TN = Ring([S.sb([128, 256], F32, "TN") for _ in range(2)])
        TB = Ring([S.sb([128, DM], BF16, "TB") for _ in range(2)])
        TT = Ring([S.sb([128, 8, 128], BF16, "TT") for _ in range(2)])
        MG = Ring([S.sb([128, DM], F32, "MG") for _ in range(2)])
        TM = Ring([S.sb([128, 512], F32, "TM") for _ in range(2)])
        MGB = Ring([S.sb([128, DM], BF16, "MGB") for _ in range(2)])
        MT = Ring([S.sb([128, 8, 128], BF16, "MT") for _ in range(2)])
        OUT = Ring([S.sb([128, DM], F32, "OUT") for _ in range(2)])
        PSY = Ring([S.ps([128, 512], F32, "PSY") for _ in range(3)])
        PSO = Ring([S.ps([128, 512], F32, "PSO") for _ in range(2)])
        for tt in range(NTT):
            rows = slice(tt * 128, (tt + 1) * 128)
            x = X.next(); z = Z.next(); g = G.next(); ngt = NG.next(); o_m = OM.next(); o_f = OFX.next(); o_n = ON.next()
            rd = RD.next(); t32 = T32.next(); tn = TN.next(); tb = TB.next(); ttt = TT.next(); mg = MG.next(); mgb = MGB.next(); mt = MT.next(); ot = OUT.next()
            S.dma("sp", x[:], xs[rows, :], writes=[x]); S.dma("sp", z[:], zz[rows, :], writes=[z]); S.dma("sp", g[:], gg[rows, :], writes=[g])
            S.dma("sp", ngt[:], ng[rows, :], writes=[ngt])
            S.dma("sp", o_m[:], om[rows, :].rearrange("q (h w) -> q h w", w=65), writes=[o_m])
            S.dma("sp", o_f[:], ofx[rows, :].rearrange("q (h w) -> q h w", w=65), writes=[o_f])
            S.dma("sp", o_n[:], on[rows, :, :].rearrange("q j (h w) -> q j h w", w=65), writes=[o_n])
            S.op("dve", lambda e, rd=rd, o_m=o_m: e.reciprocal(out=rd[:, 0:6], in_=o_m[:, :, 64]), reads=[o_m], writes=[rd])
            S.op("dve", lambda e, rd=rd, o_f=o_f: e.reciprocal(out=rd[:, 6:12], in_=o_f[:, :, 64]), reads=[o_f], writes=[rd])
            S.op("dve", lambda e, rd=rd, o_m=o_m, t32=t32: e.tensor_tensor(out=t32[:, 0:384].rearrange("p (h d) -> p h d", d=64), in0=o_m[:, :, 0:64],
                                                                         in1=rd[:, 0:6].unsqueeze(2).to_broadcast([128, 6, 64]), op=ALU.mult), reads=[o_m, rd], writes=[t32])
            S.op("dve", lambda e, rd=rd, o_f=o_f, t32=t32: e.tensor_tensor(out=t32[:, 640:1024].rearrange("p (h d) -> p h d", d=64), in0=o_f[:, :, 0:64],
                                                                         in1=rd[:, 6:12].unsqueeze(2).to_broadcast([128, 6, 64]), op=ALU.mult), reads=[o_f, rd], writes=[t32])
            S.op("dve", lambda e, rd=rd, o_n=o_n: e.tensor_scalar(out=rd[:, 12:24].rearrange("p (j h) -> p j h", h=4), in0=o_n[:, :, :, 64], scalar1=1e-30, scalar2=None, op0=ALU.max),
                 reads=[o_n], writes=[rd])
            S.op("dve", lambda e, rd=rd: e.reciprocal(out=rd[:, 12:24], in_=rd[:, 12:24]), reads=[rd], writes=[rd])
            S.op("dve", lambda e, rd=rd, ngt=ngt: e.tensor_tensor(out=rd[:, 12:24].rearrange("p (j h) -> p j h", h=4), in0=rd[:, 12:24].rearrange("p (j h) -> p j h", h=4),
                                                                  in1=ngt[:, 0:12].rearrange("p (h j) -> p j h", j=3), op=ALU.mult), reads=[rd, ngt], writes=[rd])
            for j in range(3):
                dst = t32 if j == 0 else tn
                dv = (t32[:, 384:640] if j == 0 else tn[:, 0:256]).rearrange("p (h d) -> p h d", d=64)
                S.op("dve", lambda e, rd=rd, o_n=o_n, dv=dv, j=j: e.tensor_tensor(out=dv, in0=o_n[:, j, :, 0:64],
                                                                                  in1=rd[:, 12 + 4 * j:16 + 4 * j].unsqueeze(2).to_broadcast([128, 4, 64]), op=ALU.mult),
                     reads=[o_n, rd], writes=[dst])
                if j > 0:
                    S.op("pool", lambda e, t32=t32, tn=tn: e.tensor_tensor(out=t32[:, 384:640], in0=t32[:, 384:640], in1=tn[:, 0:256], op=ALU.add), reads=[t32, tn], writes=[t32])
            S.op("pool", lambda e, t32=t32, z=z, tb=tb: e.tensor_tensor(out=tb[:], in0=t32[:], in1=z[:], op=ALU.mult), reads=[t32, z], writes=[tb])
            for c in range(8):
                pt = PST.next()
                S.op("pe", lambda e, pt=pt, tb=tb, c=c: e.matmul(pt[:], lhsT=tb[:, c * 128:(c + 1) * 128], rhs=ID[:], start=True, stop=True), reads=[tb, ID], writes=[pt])
                S.op("act", lambda e, pt=pt, ttt=ttt, c=c: e.activation(out=ttt[:, c, :], in_=pt[:], func=AF.Copy), reads=[pt], writes=[ttt])
            for ct in range(2):
                cols = slice(ct * 512, (ct + 1) * 512)
                for b, chs in enumerate(((0, 1, 2), (3, 4), (5, 6, 7))):
                    py = PSY.next()
                    for k, ch in enumerate(chs):
                        S.op("pe", lambda e, py=py, ttt=ttt, ch=ch, cols=cols, k=k, n=len(chs): e.matmul(py[:], lhsT=ttt[:, ch, :], rhs=WUP[:, ch, cols], start=(k == 0), stop=(k == n - 1)),
                             reads=[ttt, WUP], writes=[py], signal=(k == len(chs) - 1))
                    gsl = slice(b * 1024 + ct * 512, b * 1024 + (ct + 1) * 512)
                    if b == 0:
                        S.op("dve", lambda e, py=py, g=g, mg=mg, cols=cols, gsl=gsl: e.tensor_tensor(out=mg[:, cols], in0=py[:], in1=g[:, gsl], op=ALU.mult), reads=[py, g], writes=[mg])
                    else:
                        tm = TM.next()
                        S.op("dve", lambda e, py=py, g=g, tm=tm, gsl=gsl: e.tensor_tensor(out=tm[:], in0=py[:], in1=g[:, gsl], op=ALU.mult), reads=[py, g], writes=[tm])
                        S.op("pool", lambda e, mg=mg, tm=tm, cols=cols: e.tensor_tensor(out=mg[:, cols], in0=mg[:, cols], in1=tm[:], op=ALU.add), reads=[mg, tm], writes=[mg])
            S.op("pool", lambda e, mg=mg, mgb=mgb: e.tensor_copy(out=mgb[:], in_=mg[:]), reads=[mg], writes=[mgb])
            for c in range(8):
                pt = PST.next()
                S.op("pe", lambda e, pt=pt, mgb=mgb, c=c: e.matmul(pt[:], lhsT=mgb[:, c * 128:(c + 1) * 128], rhs=ID[:], start=True, stop=True), reads=[mgb, ID], writes=[pt])
                S.op("act", lambda e, pt=pt, mt=mt, c=c: e.activation(out=mt[:, c, :], in_=pt[:], func=AF.Copy), reads=[pt], writes=[mt])
            for ct in range(2):
                cols = slice(ct * 512, (ct + 1) * 512)
                po = PSO.next()
                for c in range(8):
                    S.op("pe", lambda e, po=po, mt=mt, c=c, cols=cols: e.matmul(po[:], lhsT=mt[:, c, :], rhs=WOUT[:, c, cols], start=(c == 0), stop=(c == 7)),
                         reads=[mt, WOUT], writes=[po], signal=(c == 7))
                S.op("dve", lambda e, po=po, x=x, ot=ot, cols=cols: e.tensor_tensor(out=ot[:, cols], in0=po[:], in1=x[:, cols], op=ALU.add), reads=[po, x], writes=[ot])
            S.dma("pool", out[rows, :], ot[:], reads=[ot])
        S.finish()
    return nc


def bf(a):
    return np.ascontiguousarray(a).astype(NPBF) if a.dtype != NPBF else np.ascontiguousarray(a)


def mask_tile(fn):
    k = np.arange(128)[:, None]
    q = np.arange(512)[None, :]
    return np.where(fn(k, q), 0.0, NEGM).astype(np.float32)


def pack_masks(tiles):
    return bf(np.concatenate(tiles, axis=1))


M_CAUSAL = [mask_tile(lambda k, q, r=r: 128 * r + k <= q) for r in range(4)]
M_ZERO = np.zeros((128, 512), np.float32)
M_FULL = np.full((128, 512), NEGM, np.float32)
MASKS_PAR = [pack_masks(M_CAUSAL + [M_FULL] * 4), pack_masks([M_ZERO] * 4 + M_CAUSAL)]
MASKS_WIN = pack_masks([mask_tile(lambda k, q, r=r: (q - (128 * r + k) >= 0) & (q - (128 * r + k) < 512)) for r in range(-4, 4)])
MASKS_CMP = [pack_masks([mask_tile(lambda k, q, r=r0 - par: 16 * k + 31 + 512 * r <= q) for r0 in range(-4, 1)]) for par in range(2)]
IDENT = np.eye(128, dtype=np.float32).astype(NPBF)


def dense_sched(nvar):
    sched = []
    for i in range(16):
        sched.append([(kt, (kt // 32) if nvar == 4 else 0, (kt - 8 * i) if kt >= 8 * i else None) for kt in range(8 * i + 8)])
    return sched


def win_sched():
    return [[(8 * i + j, 0, j) for j in range(8)] for i in range(16)]


def cmp_sched():
    sched = []
    for i in range(16):
        prs = []
        for j in range(8):
            r0 = 4 * j - 2 * i
            if r0 >= 2:
                continue
            prs.append((j, 0, None if r0 <= -5 else r0 + 4))
        sched.append(prs)
    return sched


def va_pack(v, extra=None):
    nk = v.shape[0]
    cols = [v.astype(NPBF), np.ones((nk, 1), NPBF)]
    if extra is not None:
        cols.append(extra.astype(NPBF))
    va = np.concatenate(cols, axis=1)
    W = va.shape[1]
    return np.ascontiguousarray(va.reshape(nk // 128, 128, W).transpose(1, 0, 2).reshape(128, (nk // 128) * W))


def par_qidx(par):
    return np.concatenate([np.arange(1024 * i + 512 * par, 1024 * i + 512 * par + 512) for i in range(16)])


_NC_CACHE = {}


def get_nc(key, fn):
    if key not in _NC_CACHE:
        _NC_CACHE[key] = fn()
    return _NC_CACHE[key]


def run(nc, in_maps):
    res = run_bass_kernel_spmd(nc, in_maps, core_ids=list(range(NCORES)))
    return res.results


def layer_forward(xl, p):
    S = S_LEN
    o1 = run_p1(xl, p["norm_g"], p["w_in"], p["b_f"], p["b_gate"], p["moba_qk_g"], p["nsa_q_g"], p["nsa_k_g"], p["fox_qk_g"])
    oA, oB, oC, oF = (o1[k] for k in ("oA", "oB", "oC", "oF"))
    mq, mk, nq, ksl, kw = oA[:, 0:384], oA[:, 384:768], oA[:, 768:1024], oA[:, 1024:1088], oA[:, 1088:1152]
    fq, fk = oB[:, 0:384], oB[:, 384:768]
    mv, fv, kc, vc, vsl, vw = oC[:, 0:384], oC[:, 384:768], oC[:, 768:832], oC[:, 832:896], oC[:, 896:960], oC[:, 960:1024]
    lf = np.ascontiguousarray(oF[:, 12:18].T)
    gidx = (np.arange(1023) * 16)[:, None] + np.arange(32)[None, :]

    def flat_of(t):
        blocks = np.zeros((1024, 32, 64), NPBF)
        blocks[:1023] = t[gidx]
        return blocks.reshape(1024, 2048).T.reshape(16, 128, 1024)
    flk_all, flv_all = flat_of(kc), flat_of(vc)
    w1 = np.ascontiguousarray(p["cmp_w1"].reshape(2, 16, 128, 256).transpose(2, 0, 1, 3))
    w2 = np.ascontiguousarray(p["cmp_w2"].reshape(2, 2, 128, 64).transpose(2, 0, 1, 3))
    pe = np.ascontiguousarray(p["cmp_pe"].reshape(2, 16, 128).transpose(2, 0, 1))
    kgain = np.ascontiguousarray(p["nsa_k_g"][0][None, :])
    in_maps = []
    for c in range(NCORES):
        cosc, sinc = rope_cs(np.arange(128 * c, 128 * c + 128) * 16 + 31)
        qs = mq[c * TOK:(c + 1) * TOK]
        qT = np.ascontiguousarray(qs.reshape(TOK, 6, 64).transpose(2, 1, 0).reshape(64, 6 * TOK))
        gm = np.zeros((128, 3, NTT, 64), np.float32)
        for lt in range(NTT):
            cur = (16 * c + lt) // 2
            gm[:, 0, lt, cur:] = -1e30
            gm[:, 1, lt, :cur] = 1.0
            gm[:, 2, lt, cur] = 1.0
        in_maps.append({"lf": lf, "flk": np.ascontiguousarray(flk_all[:, :, 128 * c:128 * c + 128].transpose(1, 0, 2)),
                        "flv": np.ascontiguousarray(flv_all[:, :, 128 * c:128 * c + 128].transpose(1, 0, 2)),
                        "w1": w1, "w2": w2, "pe": pe, "kgain": kgain, "csc": np.ascontiguousarray(np.stack([cosc, sinc], axis=1)),
                        "qT": qT, "kmT": o1["kmT"], "gm": gm})
    r2 = run(get_nc("p2", build_p2), in_maps)
    csplit = r2[0]["csplit"]
    kcmp = np.concatenate([r["kcmp"] for r in r2], axis=0)
    vcmp = np.concatenate([r["vcmp"] for r in r2], axis=0)
    mb = np.concatenate([r["mb"] for r in r2], axis=0)
    keys = np.arange(S)
    E_moba = (keys[None, :] // 256 == np.arange(64)[:, None]).astype(NPBF)
    E_slc = ((keys[None, :] // 64) % 64 == np.arange(64)[:, None]).astype(NPBF)
    ones3 = np.ones((3, S), NPBF)

    def dense_job_inputs(branch, h, par):
        qi = par_qidx(par)
        KA = np.zeros((128, S), NPBF)
        QA = np.zeros((128, 8192), NPBF)
        if branch == "fox":
            KA[0:64] = fk[:, h * 64:(h + 1) * 64].T
            KA[64:67] = csplit[h, 3:6]
            KA[67:70] = ones3
            QA[0:64] = fq[qi, h * 64:(h + 1) * 64].T
            QA[64:67] = ones3[:, qi]
            QA[67:70] = csplit[h, 0:3][:, qi]
            VA = va_pack(fv[:, h * 64:(h + 1) * 64])
        else:
            KA[0:64] = mk[:, h * 64:(h + 1) * 64].T
            KA[64:128] = E_moba
            QA[0:64] = mq[qi, h * 64:(h + 1) * 64].T
            QA[64:128] = mb[qi, h * 64:(h + 1) * 64].T
            VA = va_pack(mv[:, h * 64:(h + 1) * 64])
        return KA, VA, QA, MASKS_PAR[par]
    djobs = [(br, h, par) for br in ("fox", "moba") for h in range(6) for par in range(2)]
    overlap = np.zeros((1024, 256), np.float32)
    for m in range(256):
        lo_, hi_ = max(4 * m - 1, 0), min(4 * m + 3, 1022)
        overlap[lo_:hi_ + 1, m] = 1.0
    kcmpT = np.ascontiguousarray(kcmp.T)
    va_cmp = va_pack(vcmp, overlap)
    kwT = np.ascontiguousarray(kw.T)
    va_win = va_pack(vw)
    spec_d = dict(Kc=128, NK=S, W=65, nvar=1, NQ=8192, nmask=8, sched=dense_sched(1))
    jobs3 = [spec_d, spec_d, spec_d, dict(Kc=64, NK=S, W=65, nvar=1, NQ=8192, nmask=8, sched=win_sched()),
             dict(Kc=64, NK=1024, W=321, nvar=1, NQ=8192, nmask=5, sched=cmp_sched())]
    kw_sh = [np.concatenate([np.zeros((64, 512), NPBF), kwT[:, :S - 512]], axis=1), kwT]
    vw_aug = np.concatenate([vw.astype(NPBF), np.ones((S, 1), NPBF)], axis=1)
    vw_sh = [np.concatenate([np.zeros((512, 65), NPBF), vw_aug[:S - 512]], axis=0), vw_aug]
    va_win = [np.ascontiguousarray(v.reshape(S // 128, 128, 65).transpose(1, 0, 2).reshape(128, (S // 128) * 65)) for v in vw_sh]
    in_maps = []
    for c in range(NCORES):
        m = {"ident": IDENT}
        for s_ in range(3):
            KA, VA, QA, MK = dense_job_inputs(*djobs[3 * c + s_])
            m[f"ka{s_}"], m[f"va{s_}"], m[f"qa{s_}"], m[f"mk{s_}"] = KA, VA, QA, MK
        hh, par = c % 4, c // 4
        nqT = np.ascontiguousarray(nq[par_qidx(par), hh * 64:(hh + 1) * 64].T)
        m["ka3"], m["va3"], m["qa3"], m["mk3"] = np.ascontiguousarray(kw_sh[par]), va_win[par], nqT, MASKS_WIN
        m["ka4"], m["va4"], m["qa4"], m["mk4"] = kcmpT, va_cmp, nqT, MASKS_CMP[par]
        in_maps.append(m)
    r3 = run(get_nc("attn3", lambda: build_attn(jobs3, "a3")), in_maps)
    o_fox = np.zeros((6, S, 65), np.float32)
    o_moba = np.zeros((6, S, 65), np.float32)
    for jid, (br, h, par) in enumerate(djobs):
        (o_fox if br == "fox" else o_moba)[h, par_qidx(par)] = r3[jid // 3][f"o{jid % 3}"].T
    o_win = np.zeros((4, S, 65), np.float32)
    o_cmp = np.zeros((4, S, 321), np.float32)
    for c in range(NCORES):
        o_win[c % 4, par_qidx(c // 4)] = r3[c]["o3"].T
        o_cmp[c % 4, par_qidx(c // 4)] = r3[c]["o4"].T
    in_maps = []
    for c in range(NCORES):
        cm = np.zeros((128, 3, NTT, 256), np.float32)
        t = (c * TOK + np.arange(TOK)).reshape(NTT, 128).T
        cur = t // 64
        mm = np.arange(256)[None, None, :]
        cand = (mm >= 1) & (mm <= cur[:, :, None] - 2)
        forced = (mm == 0) | (mm == cur[:, :, None]) | (mm == cur[:, :, None] - 1)
        cm[:, 0] = np.where(cand, 0.0, -1e30)
        cm[:, 1] = cand
        cm[:, 2] = forced
        in_maps.append({"oc": np.ascontiguousarray(o_cmp[:, c * TOK:(c + 1) * TOK, 64:321]), "cm": cm})
    r4 = run(get_nc("p4", build_p4), in_maps)
    mbs = np.concatenate([r["mbs"] for r in r4], axis=0)
    spec_s = dict(Kc=128, NK=S, W=65, nvar=4, NQ=8192, nmask=8, sched=dense_sched(4))
    KAs = np.zeros((128, S), NPBF)
    KAs[0:64] = ksl.T
    KAs[64:128] = E_slc
    va_s = va_pack(vsl)
    in_maps = []
    for c in range(NCORES):
        h, par = c // 2, c % 2
        qi = par_qidx(par)
        QA = np.zeros((128, 4, 8192), NPBF)
        QA[0:64] = nq[qi, h * 64:(h + 1) * 64].T[:, None, :]
        QA[64:128] = mbs[qi].T.reshape(4, 64, 8192).transpose(1, 0, 2)
        in_maps.append({"ident": IDENT, "ka0": KAs, "va0": va_s, "qa0": QA.reshape(128, 4 * 8192), "mk0": MASKS_PAR[par]})
    r5 = run(get_nc("attn5", lambda: build_attn([spec_s], "a5")), in_maps)
    o_slc = np.zeros((4, S, 65), np.float32)
    for c in range(NCORES):
        o_slc[c // 2, par_qidx(c % 2)] = r5[c]["o0"].T
    wz = np.ascontiguousarray(p["w_in"][:, p5_col_perm()])
    wup = np.ascontiguousarray(np.concatenate([p["w_up_moba"], p["w_up_nsa"], p["w_up_fox"]], axis=0))
    om = np.ascontiguousarray(o_moba.transpose(1, 0, 2).reshape(S, 6 * 65))
    ofx = np.ascontiguousarray(o_fox.transpose(1, 0, 2).reshape(S, 6 * 65))
    on = np.ascontiguousarray(np.stack([o_cmp[:, :, 0:65], o_slc, o_win], axis=0).transpose(2, 0, 1, 3).reshape(S, 3, 4 * 65))
    in_maps = []
    for c in range(NCORES):
        sl = slice(c * TOK, (c + 1) * TOK)
        in_maps.append({"xs": np.ascontiguousarray(xl[sl]), "gnorm": np.ascontiguousarray(p["norm_g"][None, :]), "wz": wz,
                        "bgate": np.ascontiguousarray(p["b_gate"][None, :]),
                        "ng": np.ascontiguousarray(oF[sl]), "om": om[sl], "ofx": ofx[sl], "on": on[sl], "wup": wup, "wout": p["w_out"], "ident": IDENT})
    r6 = run(get_nc("p5", build_p5), in_maps)
    dbg = dict(o_fox=o_fox, o_moba=o_moba, o_win=o_win, o_cmp=o_cmp, o_slc=o_slc, mb=mb, mbs=mbs, kcmp=kcmp, vcmp=vcmp, csplit=csplit)
    return np.concatenate([r["out"] for r in r6], axis=0), dbg


PARAM_KEYS = ["norm_g", "w_in", "b_f", "b_gate", "moba_qk_g", "nsa_q_g", "nsa_k_g", "fox_qk_g", "cmp_pe", "cmp_w1", "cmp_w2",
              "w_up_moba", "w_up_nsa", "w_up_fox", "w_out"]


def kernel(**inputs):
    x = np.asarray(inputs["x"], np.float32)
    xl = np.ascontiguousarray(x[0])
    for l in range(2):
        p = {k: np.ascontiguousarray(np.asarray(inputs[k], np.float32)[l]) for k in PARAM_KEYS}
        xl, _ = layer_forward(xl, p)
    return xl[None].astype(np.float32)
```

```python
import numpy as np
import ml_dtypes
from contextlib import ExitStack
import concourse.bass as bass
import concourse.mybir as mybir
from concourse.bass_utils import run_bass_kernel_spmd

F32 = mybir.dt.float32
BF16 = mybir.dt.bfloat16
AF = mybir.ActivationFunctionType
ALU = mybir.AluOpType
AX = mybir.AxisListType
NPBF = ml_dtypes.bfloat16

NCORES = 8
S_LEN = 16384
DM = 1024
HD = 64
EPS = 1e-6
SCALE = 0.125
NEGM = -30000.0


class Buf:
    def __init__(self, t, name):
        self.t = t
        self.name = name
        self.w = None
        self.r = []
        self.dsem = None
        self.dcnt = 0

    def __getitem__(self, k):
        return self.t[k]


class Sched:
    ENG = ("pe", "act", "dve", "pool", "sp")

    def __init__(self, nc, stack):
        self.nc = nc
        self.stack = stack
        self.sem_stack = stack
        self.ops = {e: [] for e in self.ENG}
        self.sem = {e: stack.enter_context(nc.semaphore("S_" + e)) for e in self.ENG}
        self.seq = {e: 0 for e in self.ENG}
        self.known = {e: {} for e in self.ENG}
        self.semobjs = {}
        self.dma_tokens = []
        self.nb = 0

    def sb(self, shape, dt, name=None):
        self.nb += 1
        name = (name or "sb") + f"_{self.nb}"
        t = self.stack.enter_context(self.nc.sbuf_tensor(name, list(shape), dt))
        return Buf(t, name)

    def ps(self, shape, dt, name=None):
        self.nb += 1
        name = (name or "ps") + f"_{self.nb}"
        t = self.stack.enter_context(self.nc.psum_tensor(name, list(shape), dt))
        return Buf(t, name)

    def _dsem(self, b):
        if b.dsem is None:
            b.dsem = self.sem_stack.enter_context(self.nc.semaphore("D_" + b.name))
        return b.dsem

    def _waits(self, eng, reads, writes):
        toks = []
        for b in list(reads) + list(writes):
            if b.w is not None:
                toks.append(b.w)
        for b in writes:
            toks.extend(b.r)
        best = {}
        for (s, v) in toks:
            k = id(s)
            self.semobjs[k] = s
            if v > best.get(k, 0):
                best[k] = v
        kn = self.known[eng]
        for k, v in best.items():
            if eng == "pe" and self.semobjs[k] is self.sem["pe"]:
                continue
            if kn.get(k, 0) >= v:
                continue
            kn[k] = v
            self.ops[eng].append(("wait", self.semobjs[k], v))

    def op(self, eng, fn, reads=(), writes=(), signal=True):
        self._waits(eng, reads, writes)
        tok = (self.sem[eng], self.seq[eng] + 1)
        if signal:
            self.seq[eng] += 1
        self.ops[eng].append(("op", fn, self.sem[eng] if signal else None, 1))
        for b in reads:
            b.r.append(tok)
        for b in writes:
            b.w = tok
            b.r = []
        return tok

    def dma(self, q, out_ap, in_ap, reads=(), writes=(), **kw):
        self._waits(q, reads, writes)
        owner = (list(writes) + list(reads))[0]
        s = self._dsem(owner)
        owner.dcnt += 16
        tok = (s, owner.dcnt)
        self.ops[q].append(("op", (lambda e, o=out_ap, i=in_ap, kw=kw: e.dma_start(out=o, in_=i, **kw)), s, 16))
        for b in reads:
            b.r.append(tok)
        for b in writes:
            b.w = tok
            b.r = []
        self.dma_tokens.append(tok)
        return tok

    def barrier(self):
        best = {}
        for (s_, v) in self.dma_tokens:
            k = id(s_)
            self.semobjs[k] = s_
            best[k] = max(best.get(k, 0), v)
        for e in self.ENG:
            if self.seq[e] > 0:
                k = id(self.sem[e])
                self.semobjs[k] = self.sem[e]
                best[k] = self.seq[e]
        for e in self.ENG:
            kn = self.known[e]
            for k, v in best.items():
                if self.semobjs[k] is self.sem[e]:
                    continue
                if kn.get(k, 0) >= v:
                    continue
                kn[k] = v
                self.ops[e].append(("wait", self.semobjs[k], v))

    def finish(self):
        nc = self.nc
        best = {}
        for (s, v) in self.dma_tokens:
            k = id(s)
            self.semobjs[k] = s
            best[k] = max(best.get(k, 0), v)
        for e in self.ENG:
            if e != "sp" and self.seq[e] > 0:
                k = id(self.sem[e])
                self.semobjs[k] = self.sem[e]
                best[k] = self.seq[e]
        for k, v in best.items():
            self.ops["sp"].append(("wait", self.semobjs[k], v))
        self._emit_block()

    def finish_part(self):
        self._emit_block()
        self.ops = {e: [] for e in self.ENG}

    def _emit_block(self):
        nc = self.nc
        ops = self.ops

        def replay(e, lst):
            for it in lst:
                if it[0] == "wait":
                    e.wait_ge(it[1], it[2])
                else:
                    ins = it[1](e)
                    if it[2] is not None:
                        ins.then_inc(it[2], it[3])

        with nc.Block() as block:
            @block.tensor
            def _(e):
                replay(e, ops["pe"])

            @block.scalar
            def _(e):
                replay(e, ops["act"])

            @block.vector
            def _(e):
                replay(e, ops["dve"])

            @block.gpsimd
            def _(e):
                replay(e, ops["pool"])

            @block.sync
            def _(e):
                replay(e, ops["sp"])


class Ring:
    def __init__(self, bufs):
        self.bufs = bufs
        self.i = 0

    def next(self):
        b = self.bufs[self.i % len(self.bufs)]
        self.i += 1
        return b


def dram_in(nc, name, shape, dt):
    return nc.dram_tensor(name, list(shape), dt, kind="ExternalInput").ap()


def dram_out(nc, name, shape, dt):
    return nc.dram_tensor(name, list(shape), dt, kind="ExternalOutput").ap()


TOK = S_LEN // NCORES
NTT = TOK // 128
P1_TILES = [("A", 512), ("A", 512), ("A", 128), ("B", 512), ("B", 256), ("C", 512), ("C", 512), ("F", 18)]
P1_NCOLS = sum(w for _, w in P1_TILES)
P5_TILES = [("D", 512), ("D", 512)] + [("E", 512)] * 6
P5_NCOLS = 4096


def col_ranges():
    sp = [384] * 4 + [256] + [64] * 6 + [12, 256] + [384] * 3 + [6, 384, 3072]
    names = ["mq", "mk", "mv", "mz", "nq", "kc", "vc", "ksl", "vsl", "kw", "vw", "ng", "nz", "fq", "fk", "fv", "ff", "fz", "gl"]
    off = np.concatenate([[0], np.cumsum(sp)])
    return {n: np.arange(off[i], off[i + 1]) for i, n in enumerate(names)}


def p1_col_perm():
    rng = col_ranges()
    return np.concatenate([rng[n] for n in ["mq", "mk", "nq", "ksl", "kw", "fq", "fk", "mv", "fv", "kc", "vc", "vsl", "vw", "ng", "ff"]])


def p5_col_perm():
    rng = col_ranges()
    return np.concatenate([rng[n] for n in ["mz", "nz", "fz", "gl"]])


def emit_xg(S, x, gnorm, ID, EPSC, PST):
    XG = S.sb([128, 8, TOK], BF16, "XG")
    RSTD = S.sb([128, NTT], F32, "RSTD")
    GREP = S.sb([128, DM], F32, "GREP")
    S.dma("pool", GREP[:], gnorm.partition_broadcast(128), writes=[GREP])
    XS = Ring([S.sb([128, DM], F32, "XS") for _ in range(2)])
    XB = Ring([S.sb([128, DM], BF16, "XB") for _ in range(2)])
    JUNK = S.sb([128, DM], BF16, "JUNK")
    SSQ = S.sb([128, NTT], F32, "SSQ")
    for tt in range(NTT):
        b = XS.next(); xb = XB.next()
        S.dma("sp", b[:], x[tt * 128:(tt + 1) * 128, :], writes=[b])
        S.op("act", lambda e, b=b, tt=tt: e.activation(out=JUNK[:], in_=b[:], func=AF.Square, accum_out=SSQ[:, tt:tt + 1]), reads=[b], writes=[JUNK, SSQ])
        S.op("pool", lambda e, b=b, xb=xb: e.tensor_tensor(out=xb[:], in0=b[:], in1=GREP[:], op=ALU.mult), reads=[b, GREP], writes=[xb])
        for c in range(8):
            pt = PST.next()
            S.op("pe", lambda e, pt=pt, xb=xb, c=c: e.matmul(pt[:], lhsT=xb[:, c * 128:(c + 1) * 128], rhs=ID[:], start=True, stop=True), reads=[xb, ID], writes=[pt])
            if c % 2 == 0:
                S.op("act", lambda e, pt=pt, c=c, tt=tt: e.activation(out=XG[:, c, tt * 128:(tt + 1) * 128], in_=pt[:], func=AF.Copy), reads=[pt], writes=[XG])
            else:
                S.op("dve", lambda e, pt=pt, c=c, tt=tt: e.tensor_copy(out=XG[:, c, tt * 128:(tt + 1) * 128], in_=pt[:]), reads=[pt], writes=[XG])
    S.op("act", lambda e: e.activation(out=SSQ[:], in_=SSQ[:], func=AF.Sqrt, bias=EPSC[:], scale=1.0 / DM), reads=[SSQ, EPSC], writes=[SSQ])
    S.op("dve", lambda e: e.reciprocal(out=RSTD[:], in_=SSQ[:]), reads=[SSQ], writes=[RSTD])
    return XG, RSTD


def proj_loop(S, XG, RSTD, w, tiles, sinks, EPSC, BIAS=None, GAINS=None, CS=None, kmean=None, nwbuf=2):
    WF = Ring([S.sb([128, 8, 512], F32, "WF") for _ in range(nwbuf)])
    WB = Ring([S.sb([128, 8, 512], BF16, "WB") for _ in range(2)])
    PS = Ring([S.ps([128, 512], F32, "PS") for _ in range(4)])
    kinds = set(k for k, _ in tiles)
    OF = Ring([S.sb([128, 512], F32, "OF") for _ in range(4)]) if kinds & {"D", "E", "F"} else None
    TMP = Ring([S.sb([128, 512], F32, "TMP") for _ in range(3)]) if kinds & {"E", "F"} else None
    if kinds & {"A", "B", "C"}:
        Y = Ring([S.sb([128, 512], F32, "Y") for _ in range(4)])
        SQ = Ring([S.sb([128, 512], F32, "SQ") for _ in range(3)])
        RS = Ring([S.sb([128, 16], F32, "RS") for _ in range(4)])
        RT = Ring([S.sb([128, 4, 8, 8], F32, "RT") for _ in range(4)])
        OBF = Ring([S.sb([128, 512], BF16, "OBF") for _ in range(4)])
    c0 = 0
    kcol = {k: 0 for k in "ABCDEF"}
    offs = np.concatenate([[0], np.cumsum([wd_ for _, wd_ in tiles])]).tolist()

    def load_w(ti_):
        wd_ = tiles[ti_][1]
        wf_ = WF.bufs[ti_ % len(WF.bufs)]
        S.dma("sp" if ti_ % 2 == 0 else "pool", wf_[:, :, 0:wd_], w[:, offs[ti_]:offs[ti_] + wd_].rearrange("(c p) n -> p c n", p=128), writes=[wf_])

    load_w(0)
    for ti, (kind, wd) in enumerate(tiles):
        wf = WF.bufs[ti % len(WF.bufs)]
        wb = WB.next()
        if ti + 1 < len(tiles):
            load_w(ti + 1)
        for half in range(2):
            eng = "pool" if half == 0 else "dve"
            S.op(eng, lambda e, wf=wf, wb=wb, half=half, wd=wd: e.tensor_copy(out=wb[:, 4 * half:4 * half + 4, 0:wd], in_=wf[:, 4 * half:4 * half + 4, 0:wd]),
                 reads=[wf], writes=[wb])
        k0 = kcol[kind]
        for tt in range(NTT):
            ps = PS.next()
            for c in range(8):
                S.op("pe", lambda e, ps=ps, wb=wb, c=c, tt=tt, wd=wd: e.matmul(ps[:, 0:wd], lhsT=XG[:, c, tt * 128:(tt + 1) * 128], rhs=wb[:, c, 0:wd],
                                                                               start=(c == 0), stop=(c == 7)),
                     reads=[XG, wb], writes=[ps], signal=(c == 7))
            rs_t = RSTD[:, tt:tt + 1]
            rows = slice(tt * 128, (tt + 1) * 128)
            if kind == "C":
                o = OBF.next()
                S.op("act", lambda e, o=o, ps=ps, rs_t=rs_t, wd=wd: e.activation(out=o[:, 0:wd], in_=ps[:, 0:wd], func=AF.Copy, scale=rs_t),
                     reads=[ps, RSTD], writes=[o])
                S.dma("sp", sinks["C"][rows, k0:k0 + wd], o[:, 0:wd], reads=[o])
            elif kind == "D":
                o = OF.next()
                S.op("act", lambda e, o=o, ps=ps, rs_t=rs_t, wd=wd: e.activation(out=o[:, 0:wd], in_=ps[:, 0:wd], func=AF.Silu, scale=rs_t),
                     reads=[ps, RSTD], writes=[o])
                S.dma("sp", sinks["D"][rows, k0:k0 + wd], o[:, 0:wd], reads=[o])
            elif kind == "E":
                t = TMP.next()
                o = OF.next()
                S.op("dve", lambda e, t=t, ps=ps, rs_t=rs_t, k0=k0, wd=wd: e.scalar_tensor_tensor(out=t[:, 0:wd], in0=ps[:, 0:wd], scalar=rs_t, in1=BIAS[:, k0:k0 + wd],
                                                                                               op0=ALU.mult, op1=ALU.add),
                     reads=[ps, RSTD, BIAS], writes=[t])
                S.op("act", lambda e, o=o, t=t, wd=wd: e.activation(out=o[:, 0:wd], in_=t[:, 0:wd], func=AF.Sigmoid), reads=[t], writes=[o])
                S.dma("sp", sinks["E"][rows, k0:k0 + wd], o[:, 0:wd], reads=[o])
            elif kind == "F":
                t = TMP.next()
                o = OF.next()
                S.op("act", lambda e, o=o, ps=ps, rs_t=rs_t: e.activation(out=o[:, 0:12], in_=ps[:, 0:12], func=AF.Sigmoid, scale=rs_t),
                     reads=[ps, RSTD], writes=[o])
                S.op("dve", lambda e, t=t, ps=ps, rs_t=rs_t: e.scalar_tensor_tensor(out=t[:, 0:6], in0=ps[:, 12:18], scalar=rs_t, in1=BIAS[:, 0:6],
                                                                                 op0=ALU.mult, op1=ALU.add),
                     reads=[ps, RSTD, BIAS], writes=[t])
                S.op("act", lambda e, t=t: e.activation(out=t[:, 8:14], in_=t[:, 0:6], func=AF.Exp, scale=-1.0), reads=[t], writes=[t])
                S.op("act", lambda e, t=t: e.activation(out=t[:, 16:22], in_=t[:, 8:14], func=AF.Ln, bias=1.0), reads=[t], writes=[t])
                S.op("dve", lambda e, t=t, o=o: e.tensor_scalar(out=o[:, 12:18], in0=t[:, 16:22], scalar1=-1.0, scalar2=None, op0=ALU.mult),
                     reads=[t], writes=[o])
                S.dma("sp", sinks["F"][rows, :], o[:, 0:18], reads=[o])
            else:
                nh = wd // 64
                y = Y.next(); sq = SQ.next(); rs = RS.next(); o = OBF.next()
                goff = k0 if kind == "A" else 1152 + k0
                S.op("act", lambda e, y=y, ps=ps, rs_t=rs_t, wd=wd: e.activation(out=y[:, 0:wd], in_=ps[:, 0:wd], func=AF.Copy, scale=rs_t),
                     reads=[ps, RSTD], writes=[y])
                S.op("pool", lambda e, y=y, sq=sq, wd=wd: e.tensor_tensor(out=sq[:, 0:wd], in0=y[:, 0:wd], in1=y[:, 0:wd], op=ALU.mult),
                     reads=[y], writes=[sq])
                S.op("dve", lambda e, sq=sq, rs=rs, nh=nh, wd=wd: e.tensor_reduce(out=rs[:, 0:nh], in_=sq[:, 0:wd].rearrange("p (h d) -> p h d", d=64), axis=AX.X, op=ALU.add),
                     reads=[sq], writes=[rs])
                S.op("act", lambda e, rs=rs, nh=nh: e.activation(out=rs[:, 0:nh], in_=rs[:, 0:nh], func=AF.Sqrt, bias=EPSC[:], scale=1.0 / 64), reads=[rs, EPSC], writes=[rs])
                S.op("dve", lambda e, rs=rs, nh=nh: e.reciprocal(out=rs[:, 0:nh], in_=rs[:, 0:nh]), reads=[rs], writes=[rs])
                S.op("dve", lambda e, y=y, rs=rs, nh=nh, wd=wd: e.tensor_tensor(out=y[:, 0:wd].rearrange("p (h d) -> p h d", d=64), in0=y[:, 0:wd].rearrange("p (h d) -> p h d", d=64),
                                                                              in1=rs[:, 0:nh].unsqueeze(2).to_broadcast([128, nh, 64]), op=ALU.mult),
                     reads=[y, rs], writes=[y])
                if kind == "B":
                    S.op("pool", lambda e, y=y, o=o, goff=goff, wd=wd: e.tensor_tensor(out=o[:, 0:wd], in0=y[:, 0:wd], in1=GAINS[:, goff:goff + wd], op=ALU.mult),
                         reads=[y, GAINS], writes=[o])
                    S.dma("sp", sinks["B"][rows, k0:k0 + wd], o[:, 0:wd], reads=[o])
                else:
                    rt = RT.next()
                    S.op("pool", lambda e, y=y, goff=goff, wd=wd: e.tensor_tensor(out=y[:, 0:wd], in0=y[:, 0:wd], in1=GAINS[:, goff:goff + wd], op=ALU.mult),
                         reads=[y, GAINS], writes=[y])
                    yv = y[:, 0:wd].rearrange("p (h d) -> p h d", d=64)
                    ov = o[:, 0:wd].rearrange("p (h d) -> p h d", d=64)
                    cosb = CS[:, 0, tt, :].unsqueeze(1).to_broadcast([128, nh, 8])
                    sinb = CS[:, 1, tt, :].unsqueeze(1).to_broadcast([128, nh, 8])
                    S.op("act", lambda e, o=o, y=y, wd=wd: e.activation(out=o[:, 0:wd], in_=y[:, 0:wd], func=AF.Copy), reads=[y], writes=[o])
                    S.op("dve", lambda e, rt=rt, yv=yv, cosb=cosb, nh=nh: e.tensor_tensor(out=rt[:, 0, 0:nh, :], in0=yv[:, :, 0:8], in1=cosb, op=ALU.mult), reads=[y, CS], writes=[rt])
                    S.op("dve", lambda e, rt=rt, yv=yv, sinb=sinb, nh=nh: e.tensor_tensor(out=rt[:, 1, 0:nh, :], in0=yv[:, :, 8:16], in1=sinb, op=ALU.mult), reads=[y, CS], writes=[rt])
                    S.op("dve", lambda e, rt=rt, yv=yv, cosb=cosb, nh=nh: e.tensor_tensor(out=rt[:, 2, 0:nh, :], in0=yv[:, :, 8:16], in1=cosb, op=ALU.mult), reads=[y, CS], writes=[rt])
                    S.op("dve", lambda e, rt=rt, yv=yv, sinb=sinb, nh=nh: e.tensor_tensor(out=rt[:, 3, 0:nh, :], in0=yv[:, :, 0:8], in1=sinb, op=ALU.mult), reads=[y, CS], writes=[rt])
                    S.op("dve", lambda e, rt=rt, ov=ov, nh=nh: e.tensor_tensor(out=ov[:, :, 0:8], in0=rt[:, 0, 0:nh, :], in1=rt[:, 1, 0:nh, :], op=ALU.subtract), reads=[rt], writes=[o])
                    S.op("dve", lambda e, rt=rt, ov=ov, nh=nh: e.tensor_tensor(out=ov[:, :, 8:16], in0=rt[:, 2, 0:nh, :], in1=rt[:, 3, 0:nh, :], op=ALU.add), reads=[rt], writes=[o])
                    if kmean is not None:
                        KMP, C256 = kmean
                        for hc in range(0, wd, 64):
                            gc = k0 + hc
                            if 384 <= gc < 768:
                                h = (gc - 384) // 64
                                S.op("pe", lambda e, o=o, hc=hc, h=h, tt=tt: e.matmul(KMP[:, h * NTT + tt:h * NTT + tt + 1], lhsT=o[:, hc:hc + 64], rhs=C256[:, 0:1], start=True, stop=True),
                                     reads=[o, C256], writes=[KMP])
                    S.dma("sp", sinks["A"][rows, k0:k0 + wd], o[:, 0:wd], reads=[o])
        kcol[kind] += wd
        c0 += wd


def build_p1():
    nc = bass.Bass("TRN2", target_bir_lowering=False)
    x = dram_in(nc, "x", [TOK, DM], F32)
    gnorm = dram_in(nc, "gnorm", [1, DM], F32)
    w = dram_in(nc, "w", [DM, P1_NCOLS], F32)
    cs = dram_in(nc, "cs", [128, 2, NTT, 8], F32)
    gains = dram_in(nc, "gains", [1, 1920], F32)
    bias = dram_in(nc, "bias", [1, 6], F32)
    ident = dram_in(nc, "ident", [128, 128], BF16)
    oA = dram_out(nc, "oA", [TOK, 1152], BF16)
    oB = dram_out(nc, "oB", [TOK, 768], BF16)
    oC = dram_out(nc, "oC", [TOK, 1024], BF16)
    oF = dram_out(nc, "oF", [TOK, 18], F32)
    okm = dram_out(nc, "okm", [64, 48], F32)
    with ExitStack() as st:
        S = Sched(nc, st)
        EPSC = S.sb([128, 1], F32, "EPSC")
        S.op("dve", lambda e: e.memset(EPSC[:], EPS), writes=[EPSC])
        C256 = S.sb([128, 1], BF16, "C256")
        S.op("dve", lambda e: e.memset(C256[:], 1.0 / 256), writes=[C256])
        ID = S.sb([128, 128], BF16, "ID")
        S.dma("sp", ID[:], ident, writes=[ID])
        CS = S.sb([128, 2, NTT, 8], F32, "CS")
        GAINS = S.sb([128, 1920], F32, "GAINS")
        BIAS = S.sb([128, 6], F32, "BIAS")
        S.dma("sp", CS[:], cs, writes=[CS])
        S.dma("pool", GAINS[:], gains.partition_broadcast(128), writes=[GAINS])
        S.dma("pool", BIAS[:], bias.partition_broadcast(128), writes=[BIAS])
        PST = Ring([S.ps([128, 128], F32, "PST") for _ in range(3)])
        KMP = S.ps([64, 96], F32, "KMP")
        XG, RSTD = emit_xg(S, x, gnorm, ID, EPSC, PST)
        proj_loop(S, XG, RSTD, w, P1_TILES, {"A": oA, "B": oB, "C": oC, "F": oF}, EPSC, BIAS=BIAS, GAINS=GAINS, CS=CS, kmean=(KMP, C256))
        KMS = S.sb([64, 96], F32, "KMS")
        KM8 = S.sb([64, 48], F32, "KM8")
        S.op("act", lambda e: e.activation(out=KMS[:], in_=KMP[:], func=AF.Copy), reads=[KMP], writes=[KMS])
        kv = KMS[:].rearrange("p (h b t) -> p h b t", h=6, t=2)
        S.op("dve", lambda e: e.tensor_tensor(out=KM8[:].rearrange("p (h b) -> p h b", h=6), in0=kv[:, :, :, 0], in1=kv[:, :, :, 1], op=ALU.add), reads=[KMS], writes=[KM8])
        S.dma("sp", okm, KM8[:], reads=[KM8])
        S.finish()
    return nc


def rope_cs(pos):
    inv = (500000.0 ** (-np.arange(0, 16, 2, dtype=np.float32) / np.float32(16))).astype(np.float32)
    ang = pos.astype(np.float32)[:, None] * inv[None, :]
    return np.cos(ang).astype(np.float32), np.sin(ang).astype(np.float32)


def run_p1(xl, norm_g, w_in, b_f, b_gate, moba_qk_g, nsa_q_g, nsa_k_g, fox_qk_g):
    nc = get_nc("p1", build_p1)
    wr = np.ascontiguousarray(w_in[:, p1_col_perm()])
    gains = np.concatenate([np.tile(moba_qk_g[0], 6), np.tile(moba_qk_g[1], 6), np.tile(nsa_q_g, 4), nsa_k_g[1], nsa_k_g[2],
                            np.tile(fox_qk_g[0], 6), np.tile(fox_qk_g[1], 6)])[None, :].astype(np.float32)
    in_maps = []
    for c in range(NCORES):
        xs = xl[c * TOK:(c + 1) * TOK]
        cos, sin = rope_cs(np.arange(c * TOK, (c + 1) * TOK))
        cs = np.stack([cos.reshape(NTT, 128, 8).transpose(1, 0, 2), sin.reshape(NTT, 128, 8).transpose(1, 0, 2)], axis=1)
        in_maps.append({"x": np.ascontiguousarray(xs), "gnorm": np.ascontiguousarray(norm_g[None, :]), "w": wr,
                        "cs": np.ascontiguousarray(cs), "gains": gains, "bias": np.ascontiguousarray(b_f[None, :]), "ident": IDENT})
    res = run(nc, in_maps)
    out = {}
    for k in ["oA", "oB", "oC", "oF"]:
        out[k] = np.concatenate([r[k] for r in res], axis=0)
    out["kmT"] = np.ascontiguousarray(np.concatenate([r["okm"].reshape(64, 6, 8) for r in res], axis=2).reshape(64, 384))
    return out


def build_attn(jobs, tag):
    nc = bass.Bass("TRN2", target_bir_lowering=False)
    ident = dram_in(nc, "ident", [128, 128], BF16)
    ins = []
    for j, jb in enumerate(jobs):
        ins.append(dict(
            ka=dram_in(nc, f"ka{j}", [jb["Kc"], jb["NK"]], BF16),
            va=dram_in(nc, f"va{j}", [128, (jb["NK"] // 128) * jb["W"]], BF16),
            qa=dram_in(nc, f"qa{j}", [jb["Kc"], jb["nvar"] * jb["NQ"]], BF16),
            mk=dram_in(nc, f"mk{j}", [128, jb["nmask"] * 512], BF16),
            o=dram_out(nc, f"o{j}", [jb["W"], jb["NQ"]], F32)))
    mNK = max(jb["NK"] for jb in jobs)
    mVA = max((jb["NK"] // 128) * jb["W"] for jb in jobs)
    mQA = max(jb["nvar"] * jb["NQ"] for jb in jobs)
    mMK = max(jb["nmask"] for jb in jobs)
    nset = min(2, len(jobs))
    with ExitStack() as st:
        S = Sched(nc, st)
        ID = S.sb([128, 128], BF16, "ID")
        S.dma("sp", ID[:], ident, writes=[ID])
        sets = [dict(KA=S.sb([128, mNK], BF16, "KA"), VA=S.sb([128, mVA], BF16, "VA"), QA=S.sb([128, mQA], BF16, "QA"),
                     MK=S.sb([128, mMK * 512], BF16, "MK")) for _ in range(nset)]
        PSS = Ring([S.ps([128, 512], F32, "PSS") for _ in range(4)])
        LOOK = 3
        ACCB = [S.ps([128, 512], F32, "ACC") for _ in range(4)]
        acc_i = [0]
        PT = Ring([S.sb([128, 512], BF16, "PT") for _ in range(LOOK + 2)])
        OB = Ring([S.sb([128, 512], F32, "OB") for _ in range(3)])
        for j, jb in enumerate(jobs):
            sset = sets[j % nset]
            KA, VA, QA, MK = sset["KA"], sset["VA"], sset["QA"], sset["MK"]
            Kc, W, NQ = jb["Kc"], jb["W"], jb["NQ"]
            io = ins[j]
            S.dma("sp", KA[0:Kc, 0:jb["NK"]], io["ka"], writes=[KA])
            S.dma("sp", QA[0:Kc, 0:jb["nvar"] * NQ], io["qa"], writes=[QA])
            S.dma("sp", VA[:, 0:(jb["NK"] // 128) * W], io["va"], writes=[VA])
            S.dma("sp", MK[:, 0:jb["nmask"] * 512], io["mk"], writes=[MK])
            chunks = [(c0_, min(128, W - c0_)) for c0_ in range(0, W, 128)]
            pendq = []
            for lg, pairs in enumerate(jb["sched"]):
                npairs = len(pairs)
                if len(chunks) == 1:
                    accs = [ACCB[acc_i[0] % 4]]
                    acc_i[0] += 1
                else:
                    accs = ACCB[0:len(chunks)]

                def emit_pv(pt, kt, first, last, lg=lg, accs=accs):
                    for ci, (cc, wc) in enumerate(chunks):
                        acc = accs[ci]
                        S.op("pe", lambda e, pt=pt, kt=kt, first=first, last=last, W=W, VA=VA, acc=acc, cc=cc, wc=wc: e.matmul(
                            acc[0:wc, :], lhsT=VA[:, kt * W + cc:kt * W + cc + wc], rhs=pt[:], start=first, stop=last),
                            reads=[pt, VA], writes=[acc], signal=(ci == len(chunks) - 1))
                    if last:
                        for ci, (cc, wc) in enumerate(chunks):
                            acc = accs[ci]
                            ob = OB.next()
                            S.op("dve", lambda e, ob=ob, acc=acc, wc=wc: e.tensor_copy(out=ob[0:wc, :], in_=acc[0:wc, :]), reads=[acc], writes=[ob])
                            S.dma("pool", io["o"][cc:cc + wc, lg * 512:(lg + 1) * 512], ob[0:wc, :], reads=[ob])

                for pi, (kt, var, midx) in enumerate(pairs):
                    ps = PSS.next()
                    q0 = var * NQ + lg * 512
                    S.op("pe", lambda e, ps=ps, kt=kt, q0=q0, midx=midx, KA=KA, QA=QA, Kc=Kc: e.matmul(
                        ps[:], lhsT=KA[0:Kc, kt * 128:(kt + 1) * 128], rhs=QA[0:Kc, q0:q0 + 512], start=True, stop=(midx is None)),
                        reads=[KA, QA], writes=[ps], signal=(midx is None))
                    if midx is not None:
                        S.op("pe", lambda e, ps=ps, midx=midx, MK=MK: e.matmul(
                            ps[:], lhsT=ID[:], rhs=MK[:, midx * 512:(midx + 1) * 512], start=False, stop=True),
                            reads=[ID, MK], writes=[ps], signal=True)
                    pt = PT.next()
                    S.op("act", lambda e, pt=pt, ps=ps: e.activation(out=pt[:], in_=ps[:], func=AF.Exp, scale=SCALE), reads=[ps], writes=[pt])
                    pendq.append((emit_pv, (pt, kt, pi == 0, pi == npairs - 1)))
                    if len(pendq) > LOOK:
                        f_, a_ = pendq.pop(0)
                        f_(*a_)
            for f_, a_ in pendq:
                f_(*a_)
        S.finish()
    return nc


def build_p2():
    nc = bass.Bass("TRN2", target_bir_lowering=False)
    lf = dram_in(nc, "lf", [6, S_LEN], F32)
    flk = dram_in(nc, "flk", [128, 16, 128], BF16)
    flv = dram_in(nc, "flv", [128, 16, 128], BF16)
    w1 = dram_in(nc, "w1", [128, 2, 16, 256], F32)
    w2 = dram_in(nc, "w2", [128, 2, 2, 64], F32)
    pe = dram_in(nc, "pe", [128, 2, 16], F32)
    kgain = dram_in(nc, "kgain", [1, 64], F32)
    csc = dram_in(nc, "csc", [128, 2, 8], F32)
    qT = dram_in(nc, "qT", [64, 6 * TOK], BF16)
    kmT = dram_in(nc, "kmT", [64, 384], F32)
    gm = dram_in(nc, "gm", [128, 3, NTT, 64], F32)
    csplit = dram_out(nc, "csplit", [6, 6, S_LEN], BF16)
    kcmp = dram_out(nc, "kcmp", [128, 64], BF16)
    vcmp = dram_out(nc, "vcmp", [128, 64], BF16)
    mb = dram_out(nc, "mb", [TOK, 384], BF16)
    with ExitStack() as st:
        S = Sched(nc, st)
        EPSC = S.sb([128, 1], F32, "EPSC")
        S.op("dve", lambda e: e.memset(EPSC[:], EPS), writes=[EPSC])
        CH = 512
        ONES = S.sb([6, CH], F32, "ONES")
        S.op("dve", lambda e: e.memset(ONES[:], 1.0), writes=[ONES])
        LFr = Ring([S.sb([6, CH], F32, "LF") for _ in range(2)])
        Cr = Ring([S.sb([6, CH], F32, "C8") for _ in range(2)])
        OCr = Ring([S.sb([6, 6, CH], BF16, "OC") for _ in range(2)])
        Hf = S.sb([6, CH], F32, "Hf")
        R1 = S.sb([6, CH], F32, "R1")
        prev = None
        for ci in range(S_LEN // CH):
            l = LFr.next(); c8 = Cr.next(); oc = OCr.next()
            S.dma("sp", l[:], lf[:, ci * CH:(ci + 1) * CH], writes=[l])
            S.op("dve", lambda e, l=l: e.tensor_scalar(out=l[:], in0=l[:], scalar1=8.0, scalar2=None, op0=ALU.mult), reads=[l], writes=[l])
            if prev is None:
                S.op("dve", lambda e, l=l, c8=c8: e.tensor_tensor_scan(out=c8[:], data0=ONES[:], data1=l[:], initial=0.0, op0=ALU.mult, op1=ALU.add),
                     reads=[ONES, l], writes=[c8])
            else:
                S.op("dve", lambda e, l=l, c8=c8, prev=prev: e.tensor_tensor_scan(out=c8[:], data0=ONES[:], data1=l[:], initial=prev[:, CH - 1:CH], op0=ALU.mult, op1=ALU.add),
                     reads=[ONES, l, prev], writes=[c8])
            prev = c8
            S.op("dve", lambda e, c8=c8, oc=oc: e.tensor_copy(out=oc[:, 0, :], in_=c8[:]), reads=[c8], writes=[oc])
            S.op("dve", lambda e, oc=oc: e.tensor_copy(out=Hf[:], in_=oc[:, 0, :]), reads=[oc], writes=[Hf])
            S.op("dve", lambda e, c8=c8: e.tensor_tensor(out=R1[:], in0=c8[:], in1=Hf[:], op=ALU.subtract), reads=[c8, Hf], writes=[R1])
            S.op("dve", lambda e, oc=oc: e.tensor_copy(out=oc[:, 1, :], in_=R1[:]), reads=[R1], writes=[oc])
            S.op("dve", lambda e, oc=oc: e.tensor_copy(out=Hf[:], in_=oc[:, 1, :]), reads=[oc], writes=[Hf])
            S.op("dve", lambda e: e.tensor_tensor(out=R1[:], in0=R1[:], in1=Hf[:], op=ALU.subtract), reads=[R1, Hf], writes=[R1])
            S.op("dve", lambda e, oc=oc: e.tensor_copy(out=oc[:, 2, :], in_=R1[:]), reads=[R1], writes=[oc])
            S.op("dve", lambda e, oc=oc: e.tensor_scalar(out=oc[:, 3:6, :], in0=oc[:, 0:3, :], scalar1=-1.0, scalar2=None, op0=ALU.mult), reads=[oc], writes=[oc])
            S.dma("sp", csplit[:, :, ci * CH:(ci + 1) * CH], oc[:], reads=[oc])
        W1F = S.sb([128, 2, 16, 256], F32, "W1F"); W1B = S.sb([128, 2, 16, 256], BF16, "W1B")
        W2F = S.sb([128, 2, 2, 64], F32, "W2F"); W2B = S.sb([128, 2, 2, 64], BF16, "W2B")
        PEF = S.sb([128, 2, 16], F32, "PEF"); PEB = S.sb([128, 2, 16], BF16, "PEB")
        FL = [S.sb([128, 16, 128], BF16, "FLK"), S.sb([128, 16, 128], BF16, "FLV")]
        KG = S.sb([128, 64], F32, "KG"); CSC = S.sb([128, 2, 8], F32, "CSC")
        S.dma("sp", W1F[:], w1, writes=[W1F]); S.dma("sp", W2F[:], w2, writes=[W2F]); S.dma("sp", PEF[:], pe, writes=[PEF])
        S.dma("sp", FL[0][:], flk, writes=[FL[0]]); S.dma("sp", FL[1][:], flv, writes=[FL[1]])
        S.dma("sp", KG[:], kgain.partition_broadcast(128), writes=[KG]); S.dma("sp", CSC[:], csc, writes=[CSC])
        S.op("dve", lambda e: e.tensor_copy(out=W1B[:], in_=W1F[:]), reads=[W1F], writes=[W1B])
        S.op("dve", lambda e: e.tensor_copy(out=W2B[:], in_=W2F[:]), reads=[W2F], writes=[W2B])
        S.op("dve", lambda e: e.tensor_copy(out=PEB[:], in_=PEF[:]), reads=[PEF], writes=[PEB])
        B1 = S.sb([128, 4], F32, "B1")
        HS = S.sb([128, 2, 128], BF16, "HS")
        PB = S.ps([128, 8], F32, "PB")
        PH = Ring([S.ps([128, 128], F32, "PH") for _ in range(2)])
        PO = S.ps([128, 64], F32, "PO")
        YC = S.sb([128, 64], F32, "YC"); SQC = S.sb([128, 64], F32, "SQC"); RSC = S.sb([128, 1], F32, "RSC")
        RTC = S.sb([128, 4, 8], F32, "RTC"); OKC = S.sb([128, 64], BF16, "OKC"); OVC = S.sb([128, 64], BF16, "OVC")
        for kv in range(2):
            for half in range(2):
                idx = kv * 2 + half
                for j in range(16):
                    S.op("pe", lambda e, kv=kv, half=half, j=j, idx=idx: e.matmul(PB[:, idx:idx + 1], lhsT=W1B[:, kv, j, half * 128:(half + 1) * 128], rhs=PEB[:, kv, j:j + 1],
                                                                                   start=(j == 0), stop=(j == 15)), reads=[W1B, PEB], writes=[PB], signal=(j == 15))
                S.op("dve", lambda e, idx=idx: e.tensor_copy(out=B1[:, idx:idx + 1], in_=PB[:, idx:idx + 1]), reads=[PB], writes=[B1])
                ph = PH.next()
                for j in range(16):
                    S.op("pe", lambda e, ph=ph, kv=kv, half=half, j=j: e.matmul(ph[:], lhsT=W1B[:, kv, j, half * 128:(half + 1) * 128], rhs=FL[kv][:, j, :],
                                                                                start=(j == 0), stop=(j == 15)), reads=[W1B, FL[kv]], writes=[ph], signal=(j == 15))
                S.op("act", lambda e, ph=ph, half=half, idx=idx: e.activation(out=HS[:, half, :], in_=ph[:], func=AF.Silu, bias=B1[:, idx:idx + 1]),
                     reads=[ph, B1], writes=[HS])
            for half in range(2):
                S.op("pe", lambda e, kv=kv, half=half: e.matmul(PO[:], lhsT=HS[:, half, :], rhs=W2B[:, kv, half, :], start=(half == 0), stop=(half == 1)),
                     reads=[HS, W2B], writes=[PO], signal=(half == 1))
            if kv == 1:
                S.op("act", lambda e: e.activation(out=OVC[:], in_=PO[:], func=AF.Copy), reads=[PO], writes=[OVC])
                S.dma("sp", vcmp, OVC[:], reads=[OVC])
            else:
                S.op("act", lambda e: e.activation(out=YC[:], in_=PO[:], func=AF.Copy), reads=[PO], writes=[YC])
                S.op("dve", lambda e: e.tensor_tensor(out=SQC[:], in0=YC[:], in1=YC[:], op=ALU.mult), reads=[YC], writes=[SQC])
                S.op("dve", lambda e: e.tensor_reduce(out=RSC[:], in_=SQC[:], axis=AX.X, op=ALU.add), reads=[SQC], writes=[RSC])
                S.op("act", lambda e: e.activation(out=RSC[:], in_=RSC[:], func=AF.Sqrt, bias=EPSC[:], scale=1.0 / 64), reads=[RSC, EPSC], writes=[RSC])
                S.op("dve", lambda e: e.reciprocal(out=RSC[:], in_=RSC[:]), reads=[RSC], writes=[RSC])
                S.op("dve", lambda e: e.scalar_tensor_tensor(out=YC[:], in0=YC[:], scalar=RSC[:, 0:1], in1=KG[:], op0=ALU.mult, op1=ALU.mult), reads=[YC, RSC, KG], writes=[YC])
                S.op("act", lambda e: e.activation(out=OKC[:], in_=YC[:], func=AF.Copy), reads=[YC], writes=[OKC])
                S.op("dve", lambda e: e.tensor_tensor(out=RTC[:, 0, :], in0=YC[:, 0:8], in1=CSC[:, 0, :], op=ALU.mult), reads=[YC, CSC], writes=[RTC])
                S.op("dve", lambda e: e.tensor_tensor(out=RTC[:, 1, :], in0=YC[:, 8:16], in1=CSC[:, 1, :], op=ALU.mult), reads=[YC, CSC], writes=[RTC])
                S.op("dve", lambda e: e.tensor_tensor(out=RTC[:, 2, :], in0=YC[:, 8:16], in1=CSC[:, 0, :], op=ALU.mult), reads=[YC, CSC], writes=[RTC])
                S.op("dve", lambda e: e.tensor_tensor(out=RTC[:, 3, :], in0=YC[:, 0:8], in1=CSC[:, 1, :], op=ALU.mult), reads=[YC, CSC], writes=[RTC])
                S.op("dve", lambda e: e.tensor_tensor(out=OKC[:, 0:8], in0=RTC[:, 0, :], in1=RTC[:, 1, :], op=ALU.subtract), reads=[RTC], writes=[OKC])
                S.op("dve", lambda e: e.tensor_tensor(out=OKC[:, 8:16], in0=RTC[:, 2, :], in1=RTC[:, 3, :], op=ALU.add), reads=[RTC], writes=[OKC])
                S.dma("sp", kcmp, OKC[:], reads=[OKC])
        KMF = S.sb([64, 384], F32, "KMF")
        KMB = S.sb([64, 384], BF16, "KMB")
        S.dma("sp", KMF[:], kmT, writes=[KMF])
        S.op("act", lambda e: e.activation(out=KMB[:], in_=KMF[:], func=AF.Copy), reads=[KMF], writes=[KMB])
        QT = S.sb([64, 6 * TOK], BF16, "QT")
        S.dma("sp", QT[:], qT, writes=[QT])
        GM = S.sb([128, 3, NTT, 64], F32, "GM")
        S.dma("sp", GM[:], gm, writes=[GM])
        PG = Ring([S.ps([128, 64], F32, "PG") for _ in range(2)])
        GS = Ring([S.sb([128, 64], F32, "GS") for _ in range(2)])
        M8 = Ring([S.sb([128, 8], F32, "M8") for _ in range(2)])
        T1 = Ring([S.sb([128, 64], F32, "T1") for _ in range(2)])
        MBO = Ring([S.sb([128, 384], BF16, "MBO") for _ in range(2)])
        for lt in range(NTT):
            mbo = MBO.next()
            for h in range(6):
                pg = PG.next(); gs = GS.next(); m8 = M8.next(); t1 = T1.next()
                S.op("pe", lambda e, pg=pg, h=h, lt=lt: e.matmul(pg[:], lhsT=QT[:, h * TOK + lt * 128:h * TOK + (lt + 1) * 128], rhs=KMB[:, h * 64:(h + 1) * 64], start=True, stop=True),
                     reads=[QT, KMB], writes=[pg])
                S.op("dve", lambda e, pg=pg, gs=gs, lt=lt: e.tensor_tensor(out=gs[:], in0=pg[:], in1=GM[:, 0, lt, :], op=ALU.add), reads=[pg, GM], writes=[gs])
                S.op("dve", lambda e, gs=gs, m8=m8: e.max(out=m8[:], in_=gs[:]), reads=[gs], writes=[m8])
                S.op("dve", lambda e, gs=gs, m8=m8, t1=t1, lt=lt: e.scalar_tensor_tensor(out=t1[:], in0=gs[:], scalar=m8[:, 2:3], in1=GM[:, 1, lt, :], op0=ALU.is_ge, op1=ALU.mult),
                     reads=[gs, m8, GM], writes=[t1])
                S.op("dve", lambda e, t1=t1, lt=lt: e.tensor_tensor(out=t1[:], in0=t1[:], in1=GM[:, 2, lt, :], op=ALU.add), reads=[t1, GM], writes=[t1])
                S.op("dve", lambda e, t1=t1, mbo=mbo, h=h: e.tensor_scalar(out=mbo[:, h * 64:(h + 1) * 64], in0=t1[:], scalar1=-NEGM, scalar2=NEGM, op0=ALU.mult, op1=ALU.add),
                     reads=[t1], writes=[mbo])
            S.dma("sp", mb[lt * 128:(lt + 1) * 128, :], mbo[:], reads=[mbo])
        S.finish()
    return nc


def build_p4():
    nc = bass.Bass("TRN2", target_bir_lowering=False)
    oc = dram_in(nc, "oc", [4, TOK, 257], F32)
    cm = dram_in(nc, "cm", [128, 3, NTT, 256], F32)
    mbs = dram_out(nc, "mbs", [TOK, 256], BF16)
    with ExitStack() as st:
        S = Sched(nc, st)
        CM = S.sb([128, 3, NTT, 256], F32, "CM")
        S.dma("sp", CM[:], cm, writes=[CM])
        OC = Ring([S.sb([128, 4, 257], F32, "OC") for _ in range(2)])
        RD = Ring([S.sb([128, 4], F32, "RD") for _ in range(2)])
        IMP = Ring([S.sb([128, 256], F32, "IMP") for _ in range(2)])
        RR = Ring([S.sb([128, 256], F32, "RR") for _ in range(2)])
        M8 = Ring([S.sb([128, 16], F32, "M8") for _ in range(2)])
        MBO = Ring([S.sb([128, 256], BF16, "MBO") for _ in range(2)])
        for lt in range(NTT):
            o = OC.next(); rd = RD.next(); imp = IMP.next(); rr = RR.next(); m8 = M8.next(); mbo = MBO.next()
            S.dma("sp", o[:], oc[:, lt * 128:(lt + 1) * 128, :].rearrange("h q w -> q h w"), writes=[o])
            S.op("dve", lambda e, o=o, rd=rd: e.tensor_scalar(out=rd[:], in0=o[:, :, 0], scalar1=1e-30, scalar2=None, op0=ALU.max), reads=[o], writes=[rd])
            S.op("dve", lambda e, rd=rd: e.reciprocal(out=rd[:], in_=rd[:]), reads=[rd], writes=[rd])
            S.op("dve", lambda e, o=o, rd=rd, imp=imp: e.tensor_scalar(out=imp[:], in0=o[:, 0, 1:257], scalar1=rd[:, 0:1], scalar2=None, op0=ALU.mult), reads=[o, rd], writes=[imp])
            for h in range(1, 4):
                S.op("dve", lambda e, o=o, rd=rd, imp=imp, h=h: e.scalar_tensor_tensor(out=imp[:], in0=o[:, h, 1:257], scalar=rd[:, h:h + 1], in1=imp[:], op0=ALU.mult, op1=ALU.add),
                     reads=[o, rd, imp], writes=[imp])
            S.op("dve", lambda e, imp=imp, lt=lt: e.tensor_tensor(out=imp[:], in0=imp[:], in1=CM[:, 0, lt, :], op=ALU.add), reads=[imp, CM], writes=[imp])
            S.op("dve", lambda e, imp=imp, m8=m8: e.max(out=m8[:, 0:8], in_=imp[:]), reads=[imp], writes=[m8])
            S.op("dve", lambda e, imp=imp, m8=m8, rr=rr: e.match_replace(out=rr[:], in_to_replace=m8[:, 0:8], in_values=imp[:], imm_value=-1e30), reads=[imp, m8], writes=[rr])
            S.op("dve", lambda e, rr=rr, m8=m8: e.max(out=m8[:, 8:16], in_=rr[:]), reads=[rr], writes=[m8])
            S.op("dve", lambda e, imp=imp, m8=m8, rr=rr, lt=lt: e.scalar_tensor_tensor(out=rr[:], in0=imp[:], scalar=m8[:, 12:13], in1=CM[:, 1, lt, :], op0=ALU.is_ge, op1=ALU.mult),
                 reads=[imp, m8, CM], writes=[rr])
            S.op("dve", lambda e, rr=rr, lt=lt: e.tensor_tensor(out=rr[:], in0=rr[:], in1=CM[:, 2, lt, :], op=ALU.add), reads=[rr, CM], writes=[rr])
            S.op("dve", lambda e, rr=rr, mbo=mbo: e.tensor_scalar(out=mbo[:], in0=rr[:], scalar1=-NEGM, scalar2=NEGM, op0=ALU.mult, op1=ALU.add), reads=[rr], writes=[mbo])
            S.dma("sp", mbs[lt * 128:(lt + 1) * 128, :], mbo[:], reads=[mbo])
        S.finish()
    return nc


def build_p5():
    nc = bass.Bass("TRN2", target_bir_lowering=False)
    xs = dram_in(nc, "xs", [TOK, DM], F32)
    gnorm = dram_in(nc, "gnorm", [1, DM], F32)
    wz = dram_in(nc, "wz", [DM, P5_NCOLS], F32)
    bgate = dram_in(nc, "bgate", [1, 3072], F32)
    ng = dram_in(nc, "ng", [TOK, 18], F32)
    om = dram_in(nc, "om", [TOK, 6 * 65], F32)
    ofx = dram_in(nc, "ofx", [TOK, 6 * 65], F32)
    on = dram_in(nc, "on", [TOK, 3, 4 * 65], F32)
    wup = dram_in(nc, "wup", [DM, DM], F32)
    wout = dram_in(nc, "wout", [DM, DM], F32)
    ident = dram_in(nc, "ident", [128, 128], BF16)
    out = dram_out(nc, "out", [TOK, DM], F32)
    zz = nc.dram_tensor("zz_scr", [TOK, 1024], F32).ap()
    gg = nc.dram_tensor("gg_scr", [TOK, 3072], F32).ap()
    with ExitStack() as st:
        S = Sched(nc, st)
        ID = S.sb([128, 128], BF16, "ID")
        S.dma("sp", ID[:], ident, writes=[ID])
        EPSC = S.sb([128, 1], F32, "EPSC")
        S.op("dve", lambda e: e.memset(EPSC[:], EPS), writes=[EPSC])
        PST = Ring([S.ps([128, 128], F32, "PST") for _ in range(3)])
        with ExitStack() as stA:
            SA = S
            old_stack = S.stack
            S.stack = stA
            BIAS = S.sb([128, 3072], F32, "BIAS")
            S.dma("pool", BIAS[:], bgate.partition_broadcast(128), writes=[BIAS])
            XG, RSTD = emit_xg(S, xs, gnorm, ID, EPSC, PST)
            proj_loop(S, XG, RSTD, wz, P5_TILES, {"D": zz, "E": gg}, EPSC, BIAS=BIAS, nwbuf=2)
            S.stack = old_stack
            S.barrier()
            S.finish_part()
        WUP = S.sb([128, 8, DM], BF16, "WUP"); WOUT = S.sb([128, 8, DM], BF16, "WOUT")
        WF = Ring([S.sb([128, 2, DM], F32, "WF") for _ in range(2)])
        for wi, (src, dst) in enumerate(((wup, WUP), (wout, WOUT))):
            for q4 in range(4):
                wf = WF.next()
                S.dma("sp", wf[:], src[q4 * 256:(q4 + 1) * 256, :].rearrange("(c p) n -> p c n", p=128), writes=[wf])
                S.op("pool", lambda e, wf=wf, dst=dst, q4=q4: e.tensor_copy(out=dst[:, 2 * q4:2 * q4 + 2, :], in_=wf[:]), reads=[wf], writes=[dst])
        X = Ring([S.sb([128, DM], F32, "X") for _ in range(2)])
        Z = Ring([S.sb([128, DM], F32, "Z") for _ in range(2)])
        G = Ring([S.sb([128, 3072], F32, "G") for _ in range(2)])
        NG = Ring([S.sb([128, 18], F32, "NG") for _ in range(2)])
        OM = Ring([S.sb([128, 6, 65], F32, "OM") for _ in range(2)])
        OFX = Ring([S.sb([128, 6, 65], F32, "OFX") for _ in range(2)])
        ON = Ring([S.sb([128, 3, 4, 65], F32, "ON") for _ in range(2)])
        RD = Ring([S.sb([128, 32], F32, "RD") for _ in range(2)])
        T32 = Ring([S.sb([128, DM], F32, "T32") for _ in range(2)])
        TN = Ring([S.sb([128, 256], F32, "TN") for _ in range(2)])
        TB = Ring([S.sb([128, DM], BF16, "TB") for _ in range(2)])
        TT = Ring([S.sb([128, 8, 128], BF16, "TT") for _ in range(2)])
        MG = Ring([S.sb([128, DM], F32, "MG") for _ in range(2)])
        TM = Ring([S.sb([128, 512], F32, "TM") for _ in range(2)])
        MGB = Ring([S.sb([128, DM], BF16, "MGB") for _ in range(2)])
        MT = Ring([S.sb([128, 8, 128], BF16, "MT") for _ in range(2)])
        OUT = Ring([S.sb([128, DM], F32, "OUT") for _ in range(2)])
        PSY = Ring([S.ps([128, 512], F32, "PSY") for _ in range(3)])
        PSO = Ring([S.ps([128, 512], F32, "PSO") for _ in range(2)])
        def loads5(tt_):
            rows_ = slice(tt_ * 128, (tt_ + 1) * 128)
            x = X.next(); z = Z.next(); g = G.next(); ngt = NG.next(); o_m = OM.next(); o_f = OFX.next(); o_n = ON.next()
            S.dma("sp", x[:], xs[rows_, :], writes=[x]); S.dma("sp", z[:], zz[rows_, :], writes=[z]); S.dma("sp", g[:], gg[rows_, :], writes=[g])
            S.dma("sp", ngt[:], ng[rows_, :], writes=[ngt])
            S.dma("sp", o_m[:], om[rows_, :].rearrange("q (h w) -> q h w", w=65), writes=[o_m])
            S.dma("sp", o_f[:], ofx[rows_, :].rearrange("q (h w) -> q h w", w=65), writes=[o_f])
            S.dma("sp", o_n[:], on[rows_, :, :].rearrange("q j (h w) -> q j h w", w=65), writes=[o_n])
            return x, z, g, ngt, o_m, o_f, o_n

        nxt5 = loads5(0)
        for tt in range(NTT):
            rows = slice(tt * 128, (tt + 1) * 128)
            x, z, g, ngt, o_m, o_f, o_n = nxt5
            if tt + 1 < NTT:
                nxt5 = loads5(tt + 1)
            rd = RD.next(); t32 = T32.next(); tn = TN.next(); tb = TB.next(); ttt = TT.next(); mg = MG.next(); mgb = MGB.next(); mt = MT.next(); ot = OUT.next()
            S.op("dve", lambda e, rd=rd, o_m=o_m: e.reciprocal(out=rd[:, 0:6], in_=o_m[:, :, 64]), reads=[o_m], writes=[rd])
            S.op("dve", lambda e, rd=rd, o_f=o_f: e.reciprocal(out=rd[:, 6:12], in_=o_f[:, :, 64]), reads=[o_f], writes=[rd])
            S.op("dve", lambda e, rd=rd, o_m=o_m, t32=t32: e.tensor_tensor(out=t32[:, 0:384].rearrange("p (h d) -> p h d", d=64), in0=o_m[:, :, 0:64],
                                                                         in1=rd[:, 0:6].unsqueeze(2).to_broadcast([128, 6, 64]), op=ALU.mult), reads=[o_m, rd], writes=[t32])
            S.op("dve", lambda e, rd=rd, o_f=o_f, t32=t32: e.tensor_tensor(out=t32[:, 640:1024].rearrange("p (h d) -> p h d", d=64), in0=o_f[:, :, 0:64],
                                                                         in1=rd[:, 6:12].unsqueeze(2).to_broadcast([128, 6, 64]), op=ALU.mult), reads=[o_f, rd], writes=[t32])
            S.op("dve", lambda e, rd=rd, o_n=o_n: e.tensor_scalar(out=rd[:, 12:24].rearrange("p (j h) -> p j h", h=4), in0=o_n[:, :, :, 64], scalar1=1e-30, scalar2=None, op0=ALU.max),
                 reads=[o_n], writes=[rd])
            S.op("dve", lambda e, rd=rd: e.reciprocal(out=rd[:, 12:24], in_=rd[:, 12:24]), reads=[rd], writes=[rd])
            S.op("dve", lambda e, rd=rd, ngt=ngt: e.tensor_tensor(out=rd[:, 12:24].rearrange("p (j h) -> p j h", h=4), in0=rd[:, 12:24].rearrange("p (j h) -> p j h", h=4),
                                                                  in1=ngt[:, 0:12].rearrange("p (h j) -> p j h", j=3), op=ALU.mult), reads=[rd, ngt], writes=[rd])
            for j in range(3):
                dst = t32 if j == 0 else tn
                dv = (t32[:, 384:640] if j == 0 else tn[:, 0:256]).rearrange("p (h d) -> p h d", d=64)
                S.op("dve", lambda e, rd=rd, o_n=o_n, dv=dv, j=j: e.tensor_tensor(out=dv, in0=o_n[:, j, :, 0:64],
                                                                                  in1=rd[:, 12 + 4 * j:16 + 4 * j].unsqueeze(2).to_broadcast([128, 4, 64]), op=ALU.mult),
                     reads=[o_n, rd], writes=[dst])
                if j > 0:
                    S.op("pool", lambda e, t32=t32, tn=tn: e.tensor_tensor(out=t32[:, 384:640], in0=t32[:, 384:640], in1=tn[:, 0:256], op=ALU.add), reads=[t32, tn], writes=[t32])
            S.op("pool", lambda e, t32=t32, z=z, tb=tb: e.tensor_tensor(out=tb[:], in0=t32[:], in1=z[:], op=ALU.mult), reads=[t32, z], writes=[tb])
            for c in range(8):
                pt = PST.next()
                S.op("pe", lambda e, pt=pt, tb=tb, c=c: e.matmul(pt[:], lhsT=tb[:, c * 128:(c + 1) * 128], rhs=ID[:], start=True, stop=True), reads=[tb, ID], writes=[pt])
                S.op("act", lambda e, pt=pt, ttt=ttt, c=c: e.activation(out=ttt[:, c, :], in_=pt[:], func=AF.Copy), reads=[pt], writes=[ttt])
            for ct in range(2):
                cols = slice(ct * 512, (ct + 1) * 512)
                for b, chs in enumerate(((0, 1, 2), (3, 4), (5, 6, 7))):
                    py = PSY.next()
                    for k, ch in enumerate(chs):
                        S.op("pe", lambda e, py=py, ttt=ttt, ch=ch, cols=cols, k=k, n=len(chs): e.matmul(py[:], lhsT=ttt[:, ch, :], rhs=WUP[:, ch, cols], start=(k == 0), stop=(k == n - 1)),
                             reads=[ttt, WUP], writes=[py], signal=(k == len(chs) - 1))
                    gsl = slice(b * 1024 + ct * 512, b * 1024 + (ct + 1) * 512)
                    if b == 0:
                        S.op("dve", lambda e, py=py, g=g, mg=mg, cols=cols, gsl=gsl: e.tensor_tensor(out=mg[:, cols], in0=py[:], in1=g[:, gsl], op=ALU.mult), reads=[py, g], writes=[mg])
                    else:
                        tm = TM.next()
                        S.op("dve", lambda e, py=py, g=g, tm=tm, gsl=gsl: e.tensor_tensor(out=tm[:], in0=py[:], in1=g[:, gsl], op=ALU.mult), reads=[py, g], writes=[tm])
                        S.op("pool", lambda e, mg=mg, tm=tm, cols=cols: e.tensor_tensor(out=mg[:, cols], in0=mg[:, cols], in1=tm[:], op=ALU.add), reads=[mg, tm], writes=[mg])
            S.op("pool", lambda e, mg=mg, mgb=mgb: e.tensor_copy(out=mgb[:], in_=mg[:]), reads=[mg], writes=[mgb])
            for c in range(8):
                pt = PST.next()
                S.op("pe", lambda e, pt=pt, mgb=mgb, c=c: e.matmul(pt[:], lhsT=mgb[:, c * 128:(c + 1) * 128], rhs=ID[:], start=True, stop=True), reads=[mgb, ID], writes=[pt])
                S.op("act", lambda e, pt=pt, mt=mt, c=c: e.activation(out=mt[:, c, :], in_=pt[:], func=AF.Copy), reads=[pt], writes=[mt])
            for ct in range(2):
                cols = slice(ct * 512, (ct + 1) * 512)
                po = PSO.next()
                for c in range(8):
                    S.op("pe", lambda e, po=po, mt=mt, c=c, cols=cols: e.matmul(po[:], lhsT=mt[:, c, :], rhs=WOUT[:, c, cols], start=(c == 0), stop=(c == 7)),
                         reads=[mt, WOUT], writes=[po], signal=(c == 7))
                S.op("dve", lambda e, po=po, x=x, ot=ot, cols=cols: e.tensor_tensor(out=ot[:, cols], in0=po[:], in1=x[:, cols], op=ALU.add), reads=[po, x], writes=[ot])
            S.dma("sp", out[rows, :], ot[:], reads=[ot])
        S.finish()
    return nc


def bf(a):
    return np.ascontiguousarray(a).astype(NPBF) if a.dtype != NPBF else np.ascontiguousarray(a)


def mask_tile(fn):
    k = np.arange(128)[:, None]
    q = np.arange(512)[None, :]
    return np.where(fn(k, q), 0.0, NEGM).astype(np.float32)


def pack_masks(tiles):
    return bf(np.concatenate(tiles, axis=1))


M_CAUSAL = [mask_tile(lambda k, q, r=r: 128 * r + k <= q) for r in range(4)]
M_ZERO = np.zeros((128, 512), np.float32)
M_FULL = np.full((128, 512), NEGM, np.float32)
MASKS_PAR = [pack_masks(M_CAUSAL + [M_FULL] * 4), pack_masks([M_ZERO] * 4 + M_CAUSAL)]
MASKS_WIN = pack_masks([mask_tile(lambda k, q, r=r: (q - (128 * r + k) >= 0) & (q - (128 * r + k) < 512)) for r in range(-4, 4)])
MASKS_CMP = [pack_masks([mask_tile(lambda k, q, r=r0 - par: 16 * k + 31 + 512 * r <= q) for r0 in range(-4, 1)]) for par in range(2)]
IDENT = np.eye(128, dtype=np.float32).astype(NPBF)


def dense_sched(nvar):
    sched = []
    for i in range(16):
        sched.append([(kt, (kt // 32) if nvar == 4 else 0, (kt - 8 * i) if kt >= 8 * i else None) for kt in range(8 * i + 8)])
    return sched


def win_sched():
    return [[(8 * i + j, 0, j) for j in range(8)] for i in range(16)]


def cmp_sched():
    sched = []
    for i in range(16):
        prs = []
        for j in range(8):
            r0 = 4 * j - 2 * i
            if r0 >= 2:
                continue
            prs.append((j, 0, None if r0 <= -5 else r0 + 4))
        sched.append(prs)
    return sched


def va_pack(v, extra=None):
    nk = v.shape[0]
    cols = [v.astype(NPBF), np.ones((nk, 1), NPBF)]
    if extra is not None:
        cols.append(extra.astype(NPBF))
    va = np.concatenate(cols, axis=1)
    W = va.shape[1]
    return np.ascontiguousarray(va.reshape(nk // 128, 128, W).transpose(1, 0, 2).reshape(128, (nk // 128) * W))


def par_qidx(par):
    return np.concatenate([np.arange(1024 * i + 512 * par, 1024 * i + 512 * par + 512) for i in range(16)])


_NC_CACHE = {}


def get_nc(key, fn):
    if key not in _NC_CACHE:
        _NC_CACHE[key] = fn()
    return _NC_CACHE[key]


def run(nc, in_maps):
    res = run_bass_kernel_spmd(nc, in_maps, core_ids=list(range(NCORES)))
    return res.results


def layer_forward(xl, p):
    S = S_LEN
    o1 = run_p1(xl, p["norm_g"], p["w_in"], p["b_f"], p["b_gate"], p["moba_qk_g"], p["nsa_q_g"], p["nsa_k_g"], p["fox_qk_g"])
    oA, oB, oC, oF = (o1[k] for k in ("oA", "oB", "oC", "oF"))
    mq, mk, nq, ksl, kw = oA[:, 0:384], oA[:, 384:768], oA[:, 768:1024], oA[:, 1024:1088], oA[:, 1088:1152]
    fq, fk = oB[:, 0:384], oB[:, 384:768]
    mv, fv, kc, vc, vsl, vw = oC[:, 0:384], oC[:, 384:768], oC[:, 768:832], oC[:, 832:896], oC[:, 896:960], oC[:, 960:1024]
    lf = np.ascontiguousarray(oF[:, 12:18].T)
    gidx = (np.arange(1023) * 16)[:, None] + np.arange(32)[None, :]

    def flat_of(t):
        blocks = np.zeros((1024, 32, 64), NPBF)
        blocks[:1023] = t[gidx]
        return blocks.reshape(1024, 2048).T.reshape(16, 128, 1024)
    flk_all, flv_all = flat_of(kc), flat_of(vc)
    w1 = np.ascontiguousarray(p["cmp_w1"].reshape(2, 16, 128, 256).transpose(2, 0, 1, 3))
    w2 = np.ascontiguousarray(p["cmp_w2"].reshape(2, 2, 128, 64).transpose(2, 0, 1, 3))
    pe = np.ascontiguousarray(p["cmp_pe"].reshape(2, 16, 128).transpose(2, 0, 1))
    kgain = np.ascontiguousarray(p["nsa_k_g"][0][None, :])
    in_maps = []
    for c in range(NCORES):
        cosc, sinc = rope_cs(np.arange(128 * c, 128 * c + 128) * 16 + 31)
        qs = mq[c * TOK:(c + 1) * TOK]
        qT = np.ascontiguousarray(qs.reshape(TOK, 6, 64).transpose(2, 1, 0).reshape(64, 6 * TOK))
        gm = np.zeros((128, 3, NTT, 64), np.float32)
        for lt in range(NTT):
            cur = (16 * c + lt) // 2
            gm[:, 0, lt, cur:] = -1e30
            gm[:, 1, lt, :cur] = 1.0
            gm[:, 2, lt, cur] = 1.0
        in_maps.append({"lf": lf, "flk": np.ascontiguousarray(flk_all[:, :, 128 * c:128 * c + 128].transpose(1, 0, 2)),
                        "flv": np.ascontiguousarray(flv_all[:, :, 128 * c:128 * c + 128].transpose(1, 0, 2)),
                        "w1": w1, "w2": w2, "pe": pe, "kgain": kgain, "csc": np.ascontiguousarray(np.stack([cosc, sinc], axis=1)),
                        "qT": qT, "kmT": o1["kmT"], "gm": gm})
    r2 = run(get_nc("p2", build_p2), in_maps)
    csplit = r2[0]["csplit"]
    kcmp = np.concatenate([r["kcmp"] for r in r2], axis=0)
    vcmp = np.concatenate([r["vcmp"] for r in r2], axis=0)
    mb = np.concatenate([r["mb"] for r in r2], axis=0)
    keys = np.arange(S)
    E_moba = (keys[None, :] // 256 == np.arange(64)[:, None]).astype(NPBF)
    E_slc = ((keys[None, :] // 64) % 64 == np.arange(64)[:, None]).astype(NPBF)
    ones3 = np.ones((3, S), NPBF)

    def dense_job_inputs(branch, h, par):
        qi = par_qidx(par)
        KA = np.zeros((128, S), NPBF)
        QA = np.zeros((128, 8192), NPBF)
        if branch == "fox":
            KA[0:64] = fk[:, h * 64:(h + 1) * 64].T
            KA[64:67] = csplit[h, 3:6]
            KA[67:70] = ones3
            QA[0:64] = fq[qi, h * 64:(h + 1) * 64].T
            QA[64:67] = ones3[:, qi]
            QA[67:70] = csplit[h, 0:3][:, qi]
            VA = va_pack(fv[:, h * 64:(h + 1) * 64])
        else:
            KA[0:64] = mk[:, h * 64:(h + 1) * 64].T
            KA[64:128] = E_moba
            QA[0:64] = mq[qi, h * 64:(h + 1) * 64].T
            QA[64:128] = mb[qi, h * 64:(h + 1) * 64].T
            VA = va_pack(mv[:, h * 64:(h + 1) * 64])
        return KA, VA, QA, MASKS_PAR[par]
    djobs = [(br, h, par) for br in ("fox", "moba") for h in range(6) for par in range(2)]
    overlap = np.zeros((1024, 256), np.float32)
    for m in range(256):
        lo_, hi_ = max(4 * m - 1, 0), min(4 * m + 3, 1022)
        overlap[lo_:hi_ + 1, m] = 1.0
    kcmpT = np.ascontiguousarray(kcmp.T)
    va_cmp = va_pack(vcmp, overlap)
    kwT = np.ascontiguousarray(kw.T)
    va_win = va_pack(vw)
    spec_d = dict(Kc=128, NK=S, W=65, nvar=1, NQ=8192, nmask=8, sched=dense_sched(1))
    jobs3 = [spec_d, spec_d, spec_d, dict(Kc=64, NK=S, W=65, nvar=1, NQ=8192, nmask=8, sched=win_sched()),
             dict(Kc=64, NK=1024, W=321, nvar=1, NQ=8192, nmask=5, sched=cmp_sched())]
    kw_sh = [np.concatenate([np.zeros((64, 512), NPBF), kwT[:, :S - 512]], axis=1), kwT]
    vw_aug = np.concatenate([vw.astype(NPBF), np.ones((S, 1), NPBF)], axis=1)
    vw_sh = [np.concatenate([np.zeros((512, 65), NPBF), vw_aug[:S - 512]], axis=0), vw_aug]
    va_win = [np.ascontiguousarray(v.reshape(S // 128, 128, 65).transpose(1, 0, 2).reshape(128, (S // 128) * 65)) for v in vw_sh]
    in_maps = []
    for c in range(NCORES):
        m = {"ident": IDENT}
        for s_ in range(3):
            KA, VA, QA, MK = dense_job_inputs(*djobs[3 * c + s_])
            m[f"ka{s_}"], m[f"va{s_}"], m[f"qa{s_}"], m[f"mk{s_}"] = KA, VA, QA, MK
        hh, par = c % 4, c // 4
        nqT = np.ascontiguousarray(nq[par_qidx(par), hh * 64:(hh + 1) * 64].T)
        m["ka3"], m["va3"], m["qa3"], m["mk3"] = np.ascontiguousarray(kw_sh[par]), va_win[par], nqT, MASKS_WIN
        m["ka4"], m["va4"], m["qa4"], m["mk4"] = kcmpT, va_cmp, nqT, MASKS_CMP[par]
        in_maps.append(m)
    r3 = run(get_nc("attn3", lambda: build_attn(jobs3, "a3")), in_maps)
    o_fox = np.zeros((6, S, 65), np.float32)
    o_moba = np.zeros((6, S, 65), np.float32)
    for jid, (br, h, par) in enumerate(djobs):
        (o_fox if br == "fox" else o_moba)[h, par_qidx(par)] = r3[jid // 3][f"o{jid % 3}"].T
    o_win = np.zeros((4, S, 65), np.float32)
    o_cmp = np.zeros((4, S, 321), np.float32)
    for c in range(NCORES):
        o_win[c % 4, par_qidx(c // 4)] = r3[c]["o3"].T
        o_cmp[c % 4, par_qidx(c // 4)] = r3[c]["o4"].T
    in_maps = []
    for c in range(NCORES):
        cm = np.zeros((128, 3, NTT, 256), np.float32)
        t = (c * TOK + np.arange(TOK)).reshape(NTT, 128).T
        cur = t // 64
        mm = np.arange(256)[None, None, :]
        cand = (mm >= 1) & (mm <= cur[:, :, None] - 2)
        forced = (mm == 0) | (mm == cur[:, :, None]) | (mm == cur[:, :, None] - 1)
        cm[:, 0] = np.where(cand, 0.0, -1e30)
        cm[:, 1] = cand
        cm[:, 2] = forced
        in_maps.append({"oc": np.ascontiguousarray(o_cmp[:, c * TOK:(c + 1) * TOK, 64:321]), "cm": cm})
    r4 = run(get_nc("p4", build_p4), in_maps)
    mbs = np.concatenate([r["mbs"] for r in r4], axis=0)
    spec_s = dict(Kc=128, NK=S, W=65, nvar=4, NQ=8192, nmask=8, sched=dense_sched(4))
    KAs = np.zeros((128, S), NPBF)
    KAs[0:64] = ksl.T
    KAs[64:128] = E_slc
    va_s = va_pack(vsl)
    in_maps = []
    for c in range(NCORES):
        h, par = c // 2, c % 2
        qi = par_qidx(par)
        QA = np.zeros((128, 4, 8192), NPBF)
        QA[0:64] = nq[qi, h * 64:(h + 1) * 64].T[:, None, :]
        QA[64:128] = mbs[qi].T.reshape(4, 64, 8192).transpose(1, 0, 2)
        in_maps.append({"ident": IDENT, "ka0": KAs, "va0": va_s, "qa0": QA.reshape(128, 4 * 8192), "mk0": MASKS_PAR[par]})
    r5 = run(get_nc("attn5", lambda: build_attn([spec_s], "a5")), in_maps)
    o_slc = np.zeros((4, S, 65), np.float32)
    for c in range(NCORES):
        o_slc[c // 2, par_qidx(c % 2)] = r5[c]["o0"].T
    wz = np.ascontiguousarray(p["w_in"][:, p5_col_perm()])
    wup = np.ascontiguousarray(np.concatenate([p["w_up_moba"], p["w_up_nsa"], p["w_up_fox"]], axis=0))
    om = np.ascontiguousarray(o_moba.transpose(1, 0, 2).reshape(S, 6 * 65))
    ofx = np.ascontiguousarray(o_fox.transpose(1, 0, 2).reshape(S, 6 * 65))
    on = np.ascontiguousarray(np.stack([o_cmp[:, :, 0:65], o_slc, o_win], axis=0).transpose(2, 0, 1, 3).reshape(S, 3, 4 * 65))
    in_maps = []
    for c in range(NCORES):
        sl = slice(c * TOK, (c + 1) * TOK)
        in_maps.append({"xs": np.ascontiguousarray(xl[sl]), "gnorm": np.ascontiguousarray(p["norm_g"][None, :]), "wz": wz,
                        "bgate": np.ascontiguousarray(p["b_gate"][None, :]),
                        "ng": np.ascontiguousarray(oF[sl]), "om": om[sl], "ofx": ofx[sl], "on": on[sl], "wup": wup, "wout": p["w_out"], "ident": IDENT})
    r6 = run(get_nc("p5", build_p5), in_maps)
    dbg = dict(o_fox=o_fox, o_moba=o_moba, o_win=o_win, o_cmp=o_cmp, o_slc=o_slc, mb=mb, mbs=mbs, kcmp=kcmp, vcmp=vcmp, csplit=csplit)
    return np.concatenate([r["out"] for r in r6], axis=0), dbg


PARAM_KEYS = ["norm_g", "w_in", "b_f", "b_gate", "moba_qk_g", "nsa_q_g", "nsa_k_g", "fox_qk_g", "cmp_pe", "cmp_w1", "cmp_w2",
              "w_up_moba", "w_up_nsa", "w_up_fox", "w_out"]


def kernel(**inputs):
    x = np.asarray(inputs["x"], np.float32)
    xl = np.ascontiguousarray(x[0])
    for l in range(2):
        p = {k: np.ascontiguousarray(np.asarray(inputs[k], np.float32)[l]) for k in PARAM_KEYS}
        xl, _ = layer_forward(xl, p)
    return xl[None].astype(np.float32)
```

```python
import numpy as np
import ml_dtypes
from contextlib import ExitStack
import concourse.bass as bass
import concourse.mybir as mybir
from concourse.bass_utils import run_bass_kernel_spmd

F32 = mybir.dt.float32
BF16 = mybir.dt.bfloat16
AF = mybir.ActivationFunctionType
ALU = mybir.AluOpType
AX = mybir.AxisListType
NPBF = ml_dtypes.bfloat16

NCORES = 8
S_LEN = 16384
DM = 1024
HD = 64
EPS = 1e-6
SCALE = 0.125
NEGM = -30000.0


class Buf:
    def __init__(self, t, name):
        self.t = t
        self.name = name
        self.w = None
        self.r = []
        self.dsem = None
        self.dcnt = 0

    def __getitem__(self, k):
        return self.t[k]


class Sched:
    ENG = ("pe", "act", "dve", "pool", "sp")

    def __init__(self, nc, stack):
        self.nc = nc
        self.stack = stack
        self.sem_stack = stack
        self.ops = {e: [] for e in self.ENG}
        self.sem = {e: stack.enter_context(nc.semaphore("S_" + e)) for e in self.ENG}
        self.seq = {e: 0 for e in self.ENG}
        self.known = {e: {} for e in self.ENG}
        self.semobjs = {}
        self.dma_tokens = []
        self.nb = 0

    def sb(self, shape, dt, name=None):
        self.nb += 1
        name = (name or "sb") + f"_{self.nb}"
        t = self.stack.enter_context(self.nc.sbuf_tensor(name, list(shape), dt))
        return Buf(t, name)

    def ps(self, shape, dt, name=None):
        self.nb += 1
        name = (name or "ps") + f"_{self.nb}"
        t = self.stack.enter_context(self.nc.psum_tensor(name, list(shape), dt))
        return Buf(t, name)

    def _dsem(self, b):
        if b.dsem is None:
            b.dsem = self.sem_stack.enter_context(self.nc.semaphore("D_" + b.name))
        return b.dsem

    def _waits(self, eng, reads, writes):
        toks = []
        for b in list(reads) + list(writes):
            if b.w is not None:
                toks.append(b.w)
        for b in writes:
            toks.extend(b.r)
        best = {}
        for (s, v) in toks:
            k = id(s)
            self.semobjs[k] = s
            if v > best.get(k, 0):
                best[k] = v
        kn = self.known[eng]
        for k, v in best.items():
            if eng == "pe" and self.semobjs[k] is self.sem["pe"]:
                continue
            if kn.get(k, 0) >= v:
                continue
            kn[k] = v
            self.ops[eng].append(("wait", self.semobjs[k], v))

    def op(self, eng, fn, reads=(), writes=(), signal=True):
        self._waits(eng, reads, writes)
        tok = (self.sem[eng], self.seq[eng] + 1)
        if signal:
            self.seq[eng] += 1
        self.ops[eng].append(("op", fn, self.sem[eng] if signal else None, 1))
        for b in reads:
            b.r.append(tok)
        for b in writes:
            b.w = tok
            b.r = []
        return tok

    def dma(self, q, out_ap, in_ap, reads=(), writes=(), **kw):
        self._waits(q, reads, writes)
        owner = (list(writes) + list(reads))[0]
        s = self._dsem(owner)
        owner.dcnt += 16
        tok = (s, owner.dcnt)
        self.ops[q].append(("op", (lambda e, o=out_ap, i=in_ap, kw=kw: e.dma_start(out=o, in_=i, **kw)), s, 16))
        for b in reads:
            b.r.append(tok)
        for b in writes:
            b.w = tok
            b.r = []
        self.dma_tokens.append(tok)
        return tok

    def barrier(self):
        best = {}
        for (s_, v) in self.dma_tokens:
            k = id(s_)
            self.semobjs[k] = s_
            best[k] = max(best.get(k, 0), v)
        for e in self.ENG:
            if self.seq[e] > 0:
                k = id(self.sem[e])
                self.semobjs[k] = self.sem[e]
                best[k] = self.seq[e]
        for e in self.ENG:
            kn = self.known[e]
            for k, v in best.items():
                if self.semobjs[k] is self.sem[e]:
                    continue
                if kn.get(k, 0) >= v:
                    continue
                kn[k] = v
                self.ops[e].append(("wait", self.semobjs[k], v))

    def finish(self):
        nc = self.nc
        best = {}
        for (s, v) in self.dma_tokens:
            k = id(s)
            self.semobjs[k] = s
            best[k] = max(best.get(k, 0), v)
        for e in self.ENG:
            if e != "sp" and self.seq[e] > 0:
                k = id(self.sem[e])
                self.semobjs[k] = self.sem[e]
                best[k] = self.seq[e]
        for k, v in best.items():
            self.ops["sp"].append(("wait", self.semobjs[k], v))
        self._emit_block()

    def finish_part(self):
        self._emit_block()
        self.ops = {e: [] for e in self.ENG}

    def _emit_block(self):
        nc = self.nc
        ops = self.ops

        def replay(e, lst):
            for it in lst:
                if it[0] == "wait":
                    e.wait_ge(it[1], it[2])
                else:
                    ins = it[1](e)
                    if it[2] is not None:
                        ins.then_inc(it[2], it[3])

        with nc.Block() as block:
            @block.tensor
            def _(e):
                replay(e, ops["pe"])

            @block.scalar
            def _(e):
                replay(e, ops["act"])

            @block.vector
            def _(e):
                replay(e, ops["dve"])

            @block.gpsimd
            def _(e):
                replay(e, ops["pool"])

            @block.sync
            def _(e):
                replay(e, ops["sp"])


class Ring:
    def __init__(self, bufs):
        self.bufs = bufs
        self.i = 0

    def next(self):
        b = self.bufs[self.i % len(self.bufs)]
        self.i += 1
        return b


def dram_in(nc, name, shape, dt):
    return nc.dram_tensor(name, list(shape), dt, kind="ExternalInput").ap()


def dram_out(nc, name, shape, dt):
    return nc.dram_tensor(name, list(shape), dt, kind="ExternalOutput").ap()


TOK = S_LEN // NCORES
NTT = TOK // 128
P1_TILES = [("A", 512), ("A", 512), ("A", 128), ("B", 512), ("B", 256), ("C", 512), ("C", 512), ("F", 18)]
P1_NCOLS = sum(w for _, w in P1_TILES)
P5_TILES = [("D", 512), ("D", 512)] + [("E", 512)] * 6
P5_NCOLS = 4096


def col_ranges():
    sp = [384] * 4 + [256] + [64] * 6 + [12, 256] + [384] * 3 + [6, 384, 3072]
    names = ["mq", "mk", "mv", "mz", "nq", "kc", "vc", "ksl", "vsl", "kw", "vw", "ng", "nz", "fq", "fk", "fv", "ff", "fz", "gl"]
    off = np.concatenate([[0], np.cumsum(sp)])
    return {n: np.arange(off[i], off[i + 1]) for i, n in enumerate(names)}


def p1_col_perm():
    rng = col_ranges()
    return np.concatenate([rng[n] for n in ["mq", "mk", "nq", "ksl", "kw", "fq", "fk", "mv", "fv", "kc", "vc", "vsl", "vw", "ng", "ff"]])


def p5_col_perm():
    rng = col_ranges()
    return np.concatenate([rng[n] for n in ["mz", "nz", "fz", "gl"]])


def emit_xg(S, x, gnorm, ID, EPSC, PST):
    XG = S.sb([128, 8, TOK], BF16, "XG")
    RSTD = S.sb([128, NTT], F32, "RSTD")
    GREP = S.sb([128, DM], F32, "GREP")
    S.dma("pool", GREP[:], gnorm.partition_broadcast(128), writes=[GREP])
    XS = Ring([S.sb([128, DM], F32, "XS") for _ in range(2)])
    XB = Ring([S.sb([128, DM], BF16, "XB") for _ in range(2)])
    JUNK = S.sb([128, DM], BF16, "JUNK")
    SSQ = S.sb([128, NTT], F32, "SSQ")
    for tt in range(NTT):
        b = XS.next(); xb = XB.next()
        S.dma("sp", b[:], x[tt * 128:(tt + 1) * 128, :], writes=[b])
        S.op("act", lambda e, b=b, tt=tt: e.activation(out=JUNK[:], in_=b[:], func=AF.Square, accum_out=SSQ[:, tt:tt + 1]), reads=[b], writes=[JUNK, SSQ])
        S.op("pool", lambda e, b=b, xb=xb: e.tensor_tensor(out=xb[:], in0=b[:], in1=GREP[:], op=ALU.mult), reads=[b, GREP], writes=[xb])
        for c in range(8):
            pt = PST.next()
            S.op("pe", lambda e, pt=pt, xb=xb, c=c: e.matmul(pt[:], lhsT=xb[:, c * 128:(c + 1) * 128], rhs=ID[:], start=True, stop=True), reads=[xb, ID], writes=[pt])
            if c % 2 == 0:
                S.op("act", lambda e, pt=pt, c=c, tt=tt: e.activation(out=XG[:, c, tt * 128:(tt + 1) * 128], in_=pt[:], func=AF.Copy), reads=[pt], writes=[XG])
            else:
                S.op("dve", lambda e, pt=pt, c=c, tt=tt: e.tensor_copy(out=XG[:, c, tt * 128:(tt + 1) * 128], in_=pt[:]), reads=[pt], writes=[XG])
    S.op("act", lambda e: e.activation(out=SSQ[:], in_=SSQ[:], func=AF.Sqrt, bias=EPSC[:], scale=1.0 / DM), reads=[SSQ, EPSC], writes=[SSQ])
    S.op("dve", lambda e: e.reciprocal(out=RSTD[:], in_=SSQ[:]), reads=[SSQ], writes=[RSTD])
    return XG, RSTD


def proj_loop(S, XG, RSTD, w, tiles, sinks, EPSC, BIAS=None, GAINS=None, CS=None, kmean=None, nwbuf=2):
    WF = Ring([S.sb([128, 8, 512], F32, "WF") for _ in range(nwbuf)])
    WB = Ring([S.sb([128, 8, 512], BF16, "WB") for _ in range(2)])
    PS = Ring([S.ps([128, 512], F32, "PS") for _ in range(4)])
    kinds = set(k for k, _ in tiles)
    OF = Ring([S.sb([128, 512], F32, "OF") for _ in range(4)]) if kinds & {"D", "E", "F"} else None
    TMP = Ring([S.sb([128, 512], F32, "TMP") for _ in range(3)]) if kinds & {"E", "F"} else None
    if kinds & {"A", "B", "C"}:
        Y = Ring([S.sb([128, 512], F32, "Y") for _ in range(4)])
        SQ = Ring([S.sb([128, 512], F32, "SQ") for _ in range(3)])
        RS = Ring([S.sb([128, 16], F32, "RS") for _ in range(4)])
        RT = Ring([S.sb([128, 4, 8, 8], F32, "RT") for _ in range(4)])
        OBF = Ring([S.sb([128, 512], BF16, "OBF") for _ in range(4)])
    c0 = 0
    kcol = {k: 0 for k in "ABCDEF"}
    offs = np.concatenate([[0], np.cumsum([wd_ for _, wd_ in tiles])]).tolist()

    def load_w(ti_):
        wd_ = tiles[ti_][1]
        wf_ = WF.bufs[ti_ % len(WF.bufs)]
        S.dma("sp" if ti_ % 2 == 0 else "pool", wf_[:, :, 0:wd_], w[:, offs[ti_]:offs[ti_] + wd_].rearrange("(c p) n -> p c n", p=128), writes=[wf_])

    pend1 = [None]
    pend2 = [None]
    load_w(0)
    for ti, (kind, wd) in enumerate(tiles):
        if kind not in ("A", "B"):
            if pend1[0] is not None:
                pf1, pf2 = pend1[0]
                pf1()
                if pend2[0] is not None:
                    pend2[0]()
                pf2()
                pend1[0] = None
                pend2[0] = None
        wf = WF.bufs[ti % len(WF.bufs)]
        wb = WB.next()
        if ti + 1 < len(tiles):
            load_w(ti + 1)
        for half in range(2):
            eng = "pool" if half == 0 else "dve"
            S.op(eng, lambda e, wf=wf, wb=wb, half=half, wd=wd: e.tensor_copy(out=wb[:, 4 * half:4 * half + 4, 0:wd], in_=wf[:, 4 * half:4 * half + 4, 0:wd]),
                 reads=[wf], writes=[wb])
        k0 = kcol[kind]
        for tt in range(NTT):
            ps = PS.next()
            for c in range(8):
                S.op("pe", lambda e, ps=ps, wb=wb, c=c, tt=tt, wd=wd: e.matmul(ps[:, 0:wd], lhsT=XG[:, c, tt * 128:(tt + 1) * 128], rhs=wb[:, c, 0:wd],
                                                                               start=(c == 0), stop=(c == 7)),
                     reads=[XG, wb], writes=[ps], signal=(c == 7))
            rs_t = RSTD[:, tt:tt + 1]
            rows = slice(tt * 128, (tt + 1) * 128)
            if kind == "C":
                o = OBF.next()
                S.op("act", lambda e, o=o, ps=ps, rs_t=rs_t, wd=wd: e.activation(out=o[:, 0:wd], in_=ps[:, 0:wd], func=AF.Copy, scale=rs_t),
                     reads=[ps, RSTD], writes=[o])
                S.dma("sp", sinks["C"][rows, k0:k0 + wd], o[:, 0:wd], reads=[o])
            elif kind == "D":
                o = OF.next()
                S.op("act", lambda e, o=o, ps=ps, rs_t=rs_t, wd=wd: e.activation(out=o[:, 0:wd], in_=ps[:, 0:wd], func=AF.Silu, scale=rs_t),
                     reads=[ps, RSTD], writes=[o])
                S.dma("sp", sinks["D"][rows, k0:k0 + wd], o[:, 0:wd], reads=[o])
            elif kind == "E":
                t = TMP.next()
                o = OF.next()
                S.op("dve", lambda e, t=t, ps=ps, rs_t=rs_t, k0=k0, wd=wd: e.scalar_tensor_tensor(out=t[:, 0:wd], in0=ps[:, 0:wd], scalar=rs_t, in1=BIAS[:, k0:k0 + wd],
                                                                                               op0=ALU.mult, op1=ALU.add),
                     reads=[ps, RSTD, BIAS], writes=[t])
                S.op("act", lambda e, o=o, t=t, wd=wd: e.activation(out=o[:, 0:wd], in_=t[:, 0:wd], func=AF.Sigmoid), reads=[t], writes=[o])
                S.dma("sp", sinks["E"][rows, k0:k0 + wd], o[:, 0:wd], reads=[o])
            elif kind == "F":
                t = TMP.next()
                o = OF.next()
                S.op("act", lambda e, o=o, ps=ps, rs_t=rs_t: e.activation(out=o[:, 0:12], in_=ps[:, 0:12], func=AF.Sigmoid, scale=rs_t),
                     reads=[ps, RSTD], writes=[o])
                S.op("dve", lambda e, t=t, ps=ps, rs_t=rs_t: e.scalar_tensor_tensor(out=t[:, 0:6], in0=ps[:, 12:18], scalar=rs_t, in1=BIAS[:, 0:6],
                                                                                 op0=ALU.mult, op1=ALU.add),
                     reads=[ps, RSTD, BIAS], writes=[t])
                S.op("act", lambda e, t=t: e.activation(out=t[:, 8:14], in_=t[:, 0:6], func=AF.Exp, scale=-1.0), reads=[t], writes=[t])
                S.op("act", lambda e, t=t: e.activation(out=t[:, 16:22], in_=t[:, 8:14], func=AF.Ln, bias=1.0), reads=[t], writes=[t])
                S.op("dve", lambda e, t=t, o=o: e.tensor_scalar(out=o[:, 12:18], in0=t[:, 16:22], scalar1=-1.0, scalar2=None, op0=ALU.mult),
                     reads=[t], writes=[o])
                S.dma("sp", sinks["F"][rows, :], o[:, 0:18], reads=[o])
            else:
                nh = wd // 64
                y = Y.next(); sq = SQ.next(); rs = RS.next(); o = OBF.next()
                goff = k0 if kind == "A" else 1152 + k0
                rt = RT.next() if kind == "A" else None

                def f0(y=y, sq=sq, rs=rs, ps=ps, rs_t=rs_t, wd=wd, nh=nh):
                    S.op("act", lambda e: e.activation(out=y[:, 0:wd], in_=ps[:, 0:wd], func=AF.Copy, scale=rs_t), reads=[ps, RSTD], writes=[y])
                    S.op("pool", lambda e: e.tensor_tensor(out=sq[:, 0:wd], in0=y[:, 0:wd], in1=y[:, 0:wd], op=ALU.mult), reads=[y], writes=[sq])
                    S.op("dve", lambda e: e.tensor_reduce(out=rs[:, 0:nh], in_=sq[:, 0:wd].rearrange("p (h d) -> p h d", d=64), axis=AX.X, op=ALU.add), reads=[sq], writes=[rs])

                def f1(y=y, rs=rs, o=o, wd=wd, nh=nh, goff=goff, kind=kind, rows=rows, k0=k0):
                    S.op("act", lambda e: e.activation(out=rs[:, 0:nh], in_=rs[:, 0:nh], func=AF.Sqrt, bias=EPSC[:], scale=1.0 / 64), reads=[rs, EPSC], writes=[rs])
                    S.op("dve", lambda e: e.reciprocal(out=rs[:, 0:nh], in_=rs[:, 0:nh]), reads=[rs], writes=[rs])
                    S.op("dve", lambda e: e.tensor_tensor(out=y[:, 0:wd].rearrange("p (h d) -> p h d", d=64), in0=y[:, 0:wd].rearrange("p (h d) -> p h d", d=64),
                                                          in1=rs[:, 0:nh].unsqueeze(2).to_broadcast([128, nh, 64]), op=ALU.mult), reads=[y, rs], writes=[y])
                    if kind == "B":
                        S.op("pool", lambda e: e.tensor_tensor(out=o[:, 0:wd], in0=y[:, 0:wd], in1=GAINS[:, goff:goff + wd], op=ALU.mult), reads=[y, GAINS], writes=[o])
                        S.dma("sp", sinks["B"][rows, k0:k0 + wd], o[:, 0:wd], reads=[o])
                    else:
                        S.op("pool", lambda e: e.tensor_tensor(out=y[:, 0:wd], in0=y[:, 0:wd], in1=GAINS[:, goff:goff + wd], op=ALU.mult), reads=[y, GAINS], writes=[y])

                def f2(y=y, o=o, rt=rt, wd=wd, nh=nh, kind=kind, rows=rows, k0=k0, tt=tt):
                    if kind == "B":
                        return
                    yv = y[:, 0:wd].rearrange("p (h d) -> p h d", d=64)
                    ov = o[:, 0:wd].rearrange("p (h d) -> p h d", d=64)
                    cosb = CS[:, 0, tt, :].unsqueeze(1).to_broadcast([128, nh, 8])
                    sinb = CS[:, 1, tt, :].unsqueeze(1).to_broadcast([128, nh, 8])
                    S.op("act", lambda e: e.activation(out=o[:, 0:wd], in_=y[:, 0:wd], func=AF.Copy), reads=[y], writes=[o])
                    S.op("dve", lambda e: e.tensor_tensor(out=rt[:, 0, 0:nh, :], in0=yv[:, :, 0:8], in1=cosb, op=ALU.mult), reads=[y, CS], writes=[rt])
                    S.op("dve", lambda e: e.tensor_tensor(out=rt[:, 1, 0:nh, :], in0=yv[:, :, 8:16], in1=sinb, op=ALU.mult), reads=[y, CS], writes=[rt])
                    S.op("dve", lambda e: e.tensor_tensor(out=rt[:, 2, 0:nh, :], in0=yv[:, :, 8:16], in1=cosb, op=ALU.mult), reads=[y, CS], writes=[rt])
                    S.op("dve", lambda e: e.tensor_tensor(out=rt[:, 3, 0:nh, :], in0=yv[:, :, 0:8], in1=sinb, op=ALU.mult), reads=[y, CS], writes=[rt])
                    S.op("dve", lambda e: e.tensor_tensor(out=ov[:, :, 0:8], in0=rt[:, 0, 0:nh, :], in1=rt[:, 1, 0:nh, :], op=ALU.subtract), reads=[rt], writes=[o])
                    S.op("dve", lambda e: e.tensor_tensor(out=ov[:, :, 8:16], in0=rt[:, 2, 0:nh, :], in1=rt[:, 3, 0:nh, :], op=ALU.add), reads=[rt], writes=[o])
                    if kmean is not None:
                        KMP, C256 = kmean
                        for hc in range(0, wd, 64):
                            gc = k0 + hc
                            if 384 <= gc < 768:
                                h = (gc - 384) // 64
                                S.op("pe", lambda e, hc=hc, h=h: e.matmul(KMP[:, h * NTT + tt:h * NTT + tt + 1], lhsT=o[:, hc:hc + 64], rhs=C256[:, 0:1], start=True, stop=True),
                                     reads=[o, C256], writes=[KMP])
                    S.dma("sp", sinks["A"][rows, k0:k0 + wd], o[:, 0:wd], reads=[o])

                f0()
                if pend1[0] is not None:
                    pf1, pf2 = pend1[0]
                    pf1()
                    if pend2[0] is not None:
                        pend2[0]()
                    pend2[0] = pf2
                pend1[0] = (f1, f2)
        kcol[kind] += wd
        c0 += wd
    if pend1[0] is not None:
        pf1, pf2 = pend1[0]
        pf1()
        if pend2[0] is not None:
            pend2[0]()
        pf2()
    elif pend2[0] is not None:
        pend2[0]()


def build_p1():
    nc = bass.Bass("TRN2", target_bir_lowering=False)
    x = dram_in(nc, "x", [TOK, DM], F32)
    gnorm = dram_in(nc, "gnorm", [1, DM], F32)
    w = dram_in(nc, "w", [DM, P1_NCOLS], F32)
    cs = dram_in(nc, "cs", [128, 2, NTT, 8], F32)
    gains = dram_in(nc, "gains", [1, 1920], F32)
    bias = dram_in(nc, "bias", [1, 6], F32)
    ident = dram_in(nc, "ident", [128, 128], BF16)
    oA = dram_out(nc, "oA", [TOK, 1152], BF16)
    oB = dram_out(nc, "oB", [TOK, 768], BF16)
    oC = dram_out(nc, "oC", [TOK, 1024], BF16)
    oF = dram_out(nc, "oF", [TOK, 18], F32)
    okm = dram_out(nc, "okm", [64, 48], F32)
    with ExitStack() as st:
        S = Sched(nc, st)
        EPSC = S.sb([128, 1], F32, "EPSC")
        S.op("dve", lambda e: e.memset(EPSC[:], EPS), writes=[EPSC])
        C256 = S.sb([128, 1], BF16, "C256")
        S.op("dve", lambda e: e.memset(C256[:], 1.0 / 256), writes=[C256])
        ID = S.sb([128, 128], BF16, "ID")
        S.dma("sp", ID[:], ident, writes=[ID])
        CS = S.sb([128, 2, NTT, 8], F32, "CS")
        GAINS = S.sb([128, 1920], F32, "GAINS")
        BIAS = S.sb([128, 6], F32, "BIAS")
        S.dma("sp", CS[:], cs, writes=[CS])
        S.dma("pool", GAINS[:], gains.partition_broadcast(128), writes=[GAINS])
        S.dma("pool", BIAS[:], bias.partition_broadcast(128), writes=[BIAS])
        PST = Ring([S.ps([128, 128], F32, "PST") for _ in range(3)])
        KMP = S.ps([64, 96], F32, "KMP")
        XG, RSTD = emit_xg(S, x, gnorm, ID, EPSC, PST)
        proj_loop(S, XG, RSTD, w, P1_TILES, {"A": oA, "B": oB, "C": oC, "F": oF}, EPSC, BIAS=BIAS, GAINS=GAINS, CS=CS, kmean=(KMP, C256))
        KMS = S.sb([64, 96], F32, "KMS")
        KM8 = S.sb([64, 48], F32, "KM8")
        S.op("act", lambda e: e.activation(out=KMS[:], in_=KMP[:], func=AF.Copy), reads=[KMP], writes=[KMS])
        kv = KMS[:].rearrange("p (h b t) -> p h b t", h=6, t=2)
        S.op("dve", lambda e: e.tensor_tensor(out=KM8[:].rearrange("p (h b) -> p h b", h=6), in0=kv[:, :, :, 0], in1=kv[:, :, :, 1], op=ALU.add), reads=[KMS], writes=[KM8])
        S.dma("sp", okm, KM8[:], reads=[KM8])
        S.finish()
    return nc


def rope_cs(pos):
    inv = (500000.0 ** (-np.arange(0, 16, 2, dtype=np.float32) / np.float32(16))).astype(np.float32)
    ang = pos.astype(np.float32)[:, None] * inv[None, :]
    return np.cos(ang).astype(np.float32), np.sin(ang).astype(np.float32)


def run_p1(xl, norm_g, w_in, b_f, b_gate, moba_qk_g, nsa_q_g, nsa_k_g, fox_qk_g):
    nc = get_nc("p1", build_p1)
    wr = np.ascontiguousarray(w_in[:, p1_col_perm()])
    gains = np.concatenate([np.tile(moba_qk_g[0], 6), np.tile(moba_qk_g[1], 6), np.tile(nsa_q_g, 4), nsa_k_g[1], nsa_k_g[2],
                            np.tile(fox_qk_g[0], 6), np.tile(fox_qk_g[1], 6)])[None, :].astype(np.float32)
    in_maps = []
    for c in range(NCORES):
        xs = xl[c * TOK:(c + 1) * TOK]
        cos, sin = rope_cs(np.arange(c * TOK, (c + 1) * TOK))
        cs = np.stack([cos.reshape(NTT, 128, 8).transpose(1, 0, 2), sin.reshape(NTT, 128, 8).transpose(1, 0, 2)], axis=1)
        in_maps.append({"x": np.ascontiguousarray(xs), "gnorm": np.ascontiguousarray(norm_g[None, :]), "w": wr,
                        "cs": np.ascontiguousarray(cs), "gains": gains, "bias": np.ascontiguousarray(b_f[None, :]), "ident": IDENT})
    res = run(nc, in_maps)
    out = {}
    for k in ["oA", "oB", "oC", "oF"]:
        out[k] = np.concatenate([r[k] for r in res], axis=0)
    out["kmT"] = np.ascontiguousarray(np.concatenate([r["okm"].reshape(64, 6, 8) for r in res], axis=2).reshape(64, 384))
    return out


def build_attn(jobs, tag):
    nc = bass.Bass("TRN2", target_bir_lowering=False)
    ident = dram_in(nc, "ident", [128, 128], BF16)
    ins = []
    for j, jb in enumerate(jobs):
        ins.append(dict(
            ka=dram_in(nc, f"ka{j}", [jb["Kc"], jb["NK"]], BF16),
            va=dram_in(nc, f"va{j}", [128, (jb["NK"] // 128) * jb["W"]], BF16),
            qa=dram_in(nc, f"qa{j}", [jb["Kc"], jb["nvar"] * jb["NQ"]], BF16),
            mk=dram_in(nc, f"mk{j}", [128, jb["nmask"] * 512], BF16),
            o=dram_out(nc, f"o{j}", [jb["W"], jb["NQ"]], F32)))
    mNK = max(jb["NK"] for jb in jobs)
    mVA = max((jb["NK"] // 128) * jb["W"] for jb in jobs)
    mQA = max(jb["nvar"] * jb["NQ"] for jb in jobs)
    mMK = max(jb["nmask"] for jb in jobs)
    nset = min(2, len(jobs))
    with ExitStack() as st:
        S = Sched(nc, st)
        ID = S.sb([128, 128], BF16, "ID")
        S.dma("sp", ID[:], ident, writes=[ID])
        sets = [dict(KA=S.sb([128, mNK], BF16, "KA"), VA=S.sb([128, mVA], BF16, "VA"), QA=S.sb([128, mQA], BF16, "QA"),
                     MK=S.sb([128, mMK * 512], BF16, "MK")) for _ in range(nset)]
        PSS = Ring([S.ps([128, 512], F32, "PSS") for _ in range(4)])
        LOOK = 3
        ACCB = [S.ps([128, 512], F32, "ACC") for _ in range(4)]
        acc_i = [0]
        PT = Ring([S.sb([128, 512], BF16, "PT") for _ in range(LOOK + 2)])
        OB = Ring([S.sb([128, 512], F32, "OB") for _ in range(3)])
        for j, jb in enumerate(jobs):
            sset = sets[j % nset]
            KA, VA, QA, MK = sset["KA"], sset["VA"], sset["QA"], sset["MK"]
            Kc, W, NQ = jb["Kc"], jb["W"], jb["NQ"]
            io = ins[j]
            S.dma("sp", KA[0:Kc, 0:jb["NK"]], io["ka"], writes=[KA])
            S.dma("sp", QA[0:Kc, 0:jb["nvar"] * NQ], io["qa"], writes=[QA])
            S.dma("sp", VA[:, 0:(jb["NK"] // 128) * W], io["va"], writes=[VA])
            S.dma("sp", MK[:, 0:jb["nmask"] * 512], io["mk"], writes=[MK])
            chunks = [(c0_, min(128, W - c0_)) for c0_ in range(0, W, 128)]
            pendq = []
            for lg, pairs in enumerate(jb["sched"]):
                npairs = len(pairs)
                if len(chunks) == 1:
                    accs = [ACCB[acc_i[0] % 4]]
                    acc_i[0] += 1
                else:
                    accs = ACCB[0:len(chunks)]

                def emit_pv(pt, kt, first, last, lg=lg, accs=accs):
                    for ci, (cc, wc) in enumerate(chunks):
                        acc = accs[ci]
                        S.op("pe", lambda e, pt=pt, kt=kt, first=first, last=last, W=W, VA=VA, acc=acc, cc=cc, wc=wc: e.matmul(
                            acc[0:wc, :], lhsT=VA[:, kt * W + cc:kt * W + cc + wc], rhs=pt[:], start=first, stop=last),
                            reads=[pt, VA], writes=[acc], signal=(ci == len(chunks) - 1))
                    if last:
                        for ci, (cc, wc) in enumerate(chunks):
                            acc = accs[ci]
                            ob = OB.next()
                            S.op("dve", lambda e, ob=ob, acc=acc, wc=wc: e.tensor_copy(out=ob[0:wc, :], in_=acc[0:wc, :]), reads=[acc], writes=[ob])
                            S.dma("pool", io["o"][cc:cc + wc, lg * 512:(lg + 1) * 512], ob[0:wc, :], reads=[ob])

                for pi, (kt, var, midx) in enumerate(pairs):
                    ps = PSS.next()
                    q0 = var * NQ + lg * 512
                    S.op("pe", lambda e, ps=ps, kt=kt, q0=q0, midx=midx, KA=KA, QA=QA, Kc=Kc: e.matmul(
                        ps[:], lhsT=KA[0:Kc, kt * 128:(kt + 1) * 128], rhs=QA[0:Kc, q0:q0 + 512], start=True, stop=(midx is None)),
                        reads=[KA, QA], writes=[ps], signal=(midx is None))
                    if midx is not None:
                        S.op("pe", lambda e, ps=ps, midx=midx, MK=MK: e.matmul(
                            ps[:], lhsT=ID[:], rhs=MK[:, midx * 512:(midx + 1) * 512], start=False, stop=True),
                            reads=[ID, MK], writes=[ps], signal=True)
                    pt = PT.next()
                    S.op("act", lambda e, pt=pt, ps=ps: e.activation(out=pt[:], in_=ps[:], func=AF.Exp, scale=SCALE), reads=[ps], writes=[pt])
                    pendq.append((emit_pv, (pt, kt, pi == 0, pi == npairs - 1)))
                    if len(pendq) > LOOK:
                        f_, a_ = pendq.pop(0)
                        f_(*a_)
            for f_, a_ in pendq:
                f_(*a_)
        S.finish()
    return nc


def build_p2():
    nc = bass.Bass("TRN2", target_bir_lowering=False)
    lf = dram_in(nc, "lf", [6, S_LEN], F32)
    flk = dram_in(nc, "flk", [128, 16, 128], BF16)
    flv = dram_in(nc, "flv", [128, 16, 128], BF16)
    w1 = dram_in(nc, "w1", [128, 2, 16, 256], F32)
    w2 = dram_in(nc, "w2", [128, 2, 2, 64], F32)
    pe = dram_in(nc, "pe", [128, 2, 16], F32)
    kgain = dram_in(nc, "kgain", [1, 64], F32)
    csc = dram_in(nc, "csc", [128, 2, 8], F32)
    qT = dram_in(nc, "qT", [64, 6 * TOK], BF16)
    kmT = dram_in(nc, "kmT", [64, 384], F32)
    gm = dram_in(nc, "gm", [128, 3, NTT, 64], F32)
    csplit = dram_out(nc, "csplit", [6, 6, S_LEN], BF16)
    kcmp = dram_out(nc, "kcmp", [128, 64], BF16)
    vcmp = dram_out(nc, "vcmp", [128, 64], BF16)
    mb = dram_out(nc, "mb", [TOK, 384], BF16)
    with ExitStack() as st:
        S = Sched(nc, st)
        EPSC = S.sb([128, 1], F32, "EPSC")
        S.op("dve", lambda e: e.memset(EPSC[:], EPS), writes=[EPSC])
        CH = 512
        ONES = S.sb([6, CH], F32, "ONES")
        S.op("dve", lambda e: e.memset(ONES[:], 1.0), writes=[ONES])
        LFr = Ring([S.sb([6, CH], F32, "LF") for _ in range(2)])
        Cr = Ring([S.sb([6, CH], F32, "C8") for _ in range(2)])
        OCr = Ring([S.sb([6, 6, CH], BF16, "OC") for _ in range(2)])
        Hf = S.sb([6, CH], F32, "Hf")
        R1 = S.sb([6, CH], F32, "R1")
        prev = None
        for ci in range(S_LEN // CH):
            l = LFr.next(); c8 = Cr.next(); oc = OCr.next()
            S.dma("sp", l[:], lf[:, ci * CH:(ci + 1) * CH], writes=[l])
            S.op("dve", lambda e, l=l: e.tensor_scalar(out=l[:], in0=l[:], scalar1=8.0, scalar2=None, op0=ALU.mult), reads=[l], writes=[l])
            if prev is None:
                S.op("dve", lambda e, l=l, c8=c8: e.tensor_tensor_scan(out=c8[:], data0=ONES[:], data1=l[:], initial=0.0, op0=ALU.mult, op1=ALU.add),
                     reads=[ONES, l], writes=[c8])
            else:
                S.op("dve", lambda e, l=l, c8=c8, prev=prev: e.tensor_tensor_scan(out=c8[:], data0=ONES[:], data1=l[:], initial=prev[:, CH - 1:CH], op0=ALU.mult, op1=ALU.add),
                     reads=[ONES, l, prev], writes=[c8])
            prev = c8
            S.op("dve", lambda e, c8=c8, oc=oc: e.tensor_copy(out=oc[:, 0, :], in_=c8[:]), reads=[c8], writes=[oc])
            S.op("dve", lambda e, oc=oc: e.tensor_copy(out=Hf[:], in_=oc[:, 0, :]), reads=[oc], writes=[Hf])
            S.op("dve", lambda e, c8=c8: e.tensor_tensor(out=R1[:], in0=c8[:], in1=Hf[:], op=ALU.subtract), reads=[c8, Hf], writes=[R1])
            S.op("dve", lambda e, oc=oc: e.tensor_copy(out=oc[:, 1, :], in_=R1[:]), reads=[R1], writes=[oc])
            S.op("dve", lambda e, oc=oc: e.tensor_copy(out=Hf[:], in_=oc[:, 1, :]), reads=[oc], writes=[Hf])
            S.op("dve", lambda e: e.tensor_tensor(out=R1[:], in0=R1[:], in1=Hf[:], op=ALU.subtract), reads=[R1, Hf], writes=[R1])
            S.op("dve", lambda e, oc=oc: e.tensor_copy(out=oc[:, 2, :], in_=R1[:]), reads=[R1], writes=[oc])
            S.op("dve", lambda e, oc=oc: e.tensor_scalar(out=oc[:, 3:6, :], in0=oc[:, 0:3, :], scalar1=-1.0, scalar2=None, op0=ALU.mult), reads=[oc], writes=[oc])
            S.dma("sp", csplit[:, :, ci * CH:(ci + 1) * CH], oc[:], reads=[oc])
        W1F = S.sb([128, 2, 16, 256], F32, "W1F"); W1B = S.sb([128, 2, 16, 256], BF16, "W1B")
        W2F = S.sb([128, 2, 2, 64], F32, "W2F"); W2B = S.sb([128, 2, 2, 64], BF16, "W2B")
        PEF = S.sb([128, 2, 16], F32, "PEF"); PEB = S.sb([128, 2, 16], BF16, "PEB")
        FL = [S.sb([128, 16, 128], BF16, "FLK"), S.sb([128, 16, 128], BF16, "FLV")]
        KG = S.sb([128, 64], F32, "KG"); CSC = S.sb([128, 2, 8], F32, "CSC")
        S.dma("sp", W1F[:], w1, writes=[W1F]); S.dma("sp", W2F[:], w2, writes=[W2F]); S.dma("sp", PEF[:], pe, writes=[PEF])
        S.dma("sp", FL[0][:], flk, writes=[FL[0]]); S.dma("sp", FL[1][:], flv, writes=[FL[1]])
        S.dma("sp", KG[:], kgain.partition_broadcast(128), writes=[KG]); S.dma("sp", CSC[:], csc, writes=[CSC])
        S.op("dve", lambda e: e.tensor_copy(out=W1B[:], in_=W1F[:]), reads=[W1F], writes=[W1B])
        S.op("dve", lambda e: e.tensor_copy(out=W2B[:], in_=W2F[:]), reads=[W2F], writes=[W2B])
        S.op("dve", lambda e: e.tensor_copy(out=PEB[:], in_=PEF[:]), reads=[PEF], writes=[PEB])
        B1 = S.sb([128, 4], F32, "B1")
        HS = S.sb([128, 2, 128], BF16, "HS")
        PB = S.ps([128, 8], F32, "PB")
        PH = Ring([S.ps([128, 128], F32, "PH") for _ in range(2)])
        PO = S.ps([128, 64], F32, "PO")
        YC = S.sb([128, 64], F32, "YC"); SQC = S.sb([128, 64], F32, "SQC"); RSC = S.sb([128, 1], F32, "RSC")
        RTC = S.sb([128, 4, 8], F32, "RTC"); OKC = S.sb([128, 64], BF16, "OKC"); OVC = S.sb([128, 64], BF16, "OVC")
        for kv in range(2):
            for half in range(2):
                idx = kv * 2 + half
                for j in range(16):
                    S.op("pe", lambda e, kv=kv, half=half, j=j, idx=idx: e.matmul(PB[:, idx:idx + 1], lhsT=W1B[:, kv, j, half * 128:(half + 1) * 128], rhs=PEB[:, kv, j:j + 1],
                                                                                   start=(j == 0), stop=(j == 15)), reads=[W1B, PEB], writes=[PB], signal=(j == 15))
                S.op("dve", lambda e, idx=idx: e.tensor_copy(out=B1[:, idx:idx + 1], in_=PB[:, idx:idx + 1]), reads=[PB], writes=[B1])
                ph = PH.next()
                for j in range(16):
                    S.op("pe", lambda e, ph=ph, kv=kv, half=half, j=j: e.matmul(ph[:], lhsT=W1B[:, kv, j, half * 128:(half + 1) * 128], rhs=FL[kv][:, j, :],
                                                                                start=(j == 0), stop=(j == 15)), reads=[W1B, FL[kv]], writes=[ph], signal=(j == 15))
                S.op("act", lambda e, ph=ph, half=half, idx=idx: e.activation(out=HS[:, half, :], in_=ph[:], func=AF.Silu, bias=B1[:, idx:idx + 1]),
                     reads=[ph, B1], writes=[HS])
            for half in range(2):
                S.op("pe", lambda e, kv=kv, half=half: e.matmul(PO[:], lhsT=HS[:, half, :], rhs=W2B[:, kv, half, :], start=(half == 0), stop=(half == 1)),
                     reads=[HS, W2B], writes=[PO], signal=(half == 1))
            if kv == 1:
                S.op("act", lambda e: e.activation(out=OVC[:], in_=PO[:], func=AF.Copy), reads=[PO], writes=[OVC])
                S.dma("sp", vcmp, OVC[:], reads=[OVC])
            else:
                S.op("act", lambda e: e.activation(out=YC[:], in_=PO[:], func=AF.Copy), reads=[PO], writes=[YC])
                S.op("dve", lambda e: e.tensor_tensor(out=SQC[:], in0=YC[:], in1=YC[:], op=ALU.mult), reads=[YC], writes=[SQC])
                S.op("dve", lambda e: e.tensor_reduce(out=RSC[:], in_=SQC[:], axis=AX.X, op=ALU.add), reads=[SQC], writes=[RSC])
                S.op("act", lambda e: e.activation(out=RSC[:], in_=RSC[:], func=AF.Sqrt, bias=EPSC[:], scale=1.0 / 64), reads=[RSC, EPSC], writes=[RSC])
                S.op("dve", lambda e: e.reciprocal(out=RSC[:], in_=RSC[:]), reads=[RSC], writes=[RSC])
                S.op("dve", lambda e: e.scalar_tensor_tensor(out=YC[:], in0=YC[:], scalar=RSC[:, 0:1], in1=KG[:], op0=ALU.mult, op1=ALU.mult), reads=[YC, RSC, KG], writes=[YC])
                S.op("act", lambda e: e.activation(out=OKC[:], in_=YC[:], func=AF.Copy), reads=[YC], writes=[OKC])
                S.op("dve", lambda e: e.tensor_tensor(out=RTC[:, 0, :], in0=YC[:, 0:8], in1=CSC[:, 0, :], op=ALU.mult), reads=[YC, CSC], writes=[RTC])
                S.op("dve", lambda e: e.tensor_tensor(out=RTC[:, 1, :], in0=YC[:, 8:16], in1=CSC[:, 1, :], op=ALU.mult), reads=[YC, CSC], writes=[RTC])
                S.op("dve", lambda e: e.tensor_tensor(out=RTC[:, 2, :], in0=YC[:, 8:16], in1=CSC[:, 0, :], op=ALU.mult), reads=[YC, CSC], writes=[RTC])
                S.op("dve", lambda e: e.tensor_tensor(out=RTC[:, 3, :], in0=YC[:, 0:8], in1=CSC[:, 1, :], op=ALU.mult), reads=[YC, CSC], writes=[RTC])
                S.op("dve", lambda e: e.tensor_tensor(out=OKC[:, 0:8], in0=RTC[:, 0, :], in1=RTC[:, 1, :], op=ALU.subtract), reads=[RTC], writes=[OKC])
                S.op("dve", lambda e: e.tensor_tensor(out=OKC[:, 8:16], in0=RTC[:, 2, :], in1=RTC[:, 3, :], op=ALU.add), reads=[RTC], writes=[OKC])
                S.dma("sp", kcmp, OKC[:], reads=[OKC])
        KMF = S.sb([64, 384], F32, "KMF")
        KMB = S.sb([64, 384], BF16, "KMB")
        S.dma("sp", KMF[:], kmT, writes=[KMF])
        S.op("act", lambda e: e.activation(out=KMB[:], in_=KMF[:], func=AF.Copy), reads=[KMF], writes=[KMB])
        QT = S.sb([64, 6 * TOK], BF16, "QT")
        S.dma("sp", QT[:], qT, writes=[QT])
        GM = S.sb([128, 3, NTT, 64], F32, "GM")
        S.dma("sp", GM[:], gm, writes=[GM])
        PG = Ring([S.ps([128, 64], F32, "PG") for _ in range(2)])
        GS = Ring([S.sb([128, 64], F32, "GS") for _ in range(2)])
        M8 = Ring([S.sb([128, 8], F32, "M8") for _ in range(2)])
        T1 = Ring([S.sb([128, 64], F32, "T1") for _ in range(2)])
        MBO = Ring([S.sb([128, 384], BF16, "MBO") for _ in range(2)])
        for lt in range(NTT):
            mbo = MBO.next()
            for h in range(6):
                pg = PG.next(); gs = GS.next(); m8 = M8.next(); t1 = T1.next()
                S.op("pe", lambda e, pg=pg, h=h, lt=lt: e.matmul(pg[:], lhsT=QT[:, h * TOK + lt * 128:h * TOK + (lt + 1) * 128], rhs=KMB[:, h * 64:(h + 1) * 64], start=True, stop=True),
                     reads=[QT, KMB], writes=[pg])
                S.op("dve", lambda e, pg=pg, gs=gs, lt=lt: e.tensor_tensor(out=gs[:], in0=pg[:], in1=GM[:, 0, lt, :], op=ALU.add), reads=[pg, GM], writes=[gs])
                S.op("dve", lambda e, gs=gs, m8=m8: e.max(out=m8[:], in_=gs[:]), reads=[gs], writes=[m8])
                S.op("dve", lambda e, gs=gs, m8=m8, t1=t1, lt=lt: e.scalar_tensor_tensor(out=t1[:], in0=gs[:], scalar=m8[:, 2:3], in1=GM[:, 1, lt, :], op0=ALU.is_ge, op1=ALU.mult),
                     reads=[gs, m8, GM], writes=[t1])
                S.op("dve", lambda e, t1=t1, lt=lt: e.tensor_tensor(out=t1[:], in0=t1[:], in1=GM[:, 2, lt, :], op=ALU.add), reads=[t1, GM], writes=[t1])
                S.op("dve", lambda e, t1=t1, mbo=mbo, h=h: e.tensor_scalar(out=mbo[:, h * 64:(h + 1) * 64], in0=t1[:], scalar1=-NEGM, scalar2=NEGM, op0=ALU.mult, op1=ALU.add),
                     reads=[t1], writes=[mbo])
            S.dma("sp", mb[lt * 128:(lt + 1) * 128, :], mbo[:], reads=[mbo])
        S.finish()
    return nc


def build_p4():
    nc = bass.Bass("TRN2", target_bir_lowering=False)
    oc = dram_in(nc, "oc", [4, TOK, 257], F32)
    cm = dram_in(nc, "cm", [128, 3, NTT, 256], F32)
    mbs = dram_out(nc, "mbs", [TOK, 256], BF16)
    with ExitStack() as st:
        S = Sched(nc, st)
        CM = S.sb([128, 3, NTT, 256], F32, "CM")
        S.dma("sp", CM[:], cm, writes=[CM])
        OC = Ring([S.sb([128, 4, 257], F32, "OC") for _ in range(2)])
        RD = Ring([S.sb([128, 4], F32, "RD") for _ in range(2)])
        IMP = Ring([S.sb([128, 256], F32, "IMP") for _ in range(2)])
        RR = Ring([S.sb([128, 256], F32, "RR") for _ in range(2)])
        M8 = Ring([S.sb([128, 16], F32, "M8") for _ in range(2)])
        MBO = Ring([S.sb([128, 256], BF16, "MBO") for _ in range(2)])
        for lt in range(NTT):
            o = OC.next(); rd = RD.next(); imp = IMP.next(); rr = RR.next(); m8 = M8.next(); mbo = MBO.next()
            S.dma("sp", o[:], oc[:, lt * 128:(lt + 1) * 128, :].rearrange("h q w -> q h w"), writes=[o])
            S.op("dve", lambda e, o=o, rd=rd: e.tensor_scalar(out=rd[:], in0=o[:, :, 0], scalar1=1e-30, scalar2=None, op0=ALU.max), reads=[o], writes=[rd])
            S.op("dve", lambda e, rd=rd: e.reciprocal(out=rd[:], in_=rd[:]), reads=[rd], writes=[rd])
            S.op("dve", lambda e, o=o, rd=rd, imp=imp: e.tensor_scalar(out=imp[:], in0=o[:, 0, 1:257], scalar1=rd[:, 0:1], scalar2=None, op0=ALU.mult), reads=[o, rd], writes=[imp])
            for h in range(1, 4):
                S.op("dve", lambda e, o=o, rd=rd, imp=imp, h=h: e.scalar_tensor_tensor(out=imp[:], in0=o[:, h, 1:257], scalar=rd[:, h:h + 1], in1=imp[:], op0=ALU.mult, op1=ALU.add),
                     reads=[o, rd, imp], writes=[imp])
            S.op("dve", lambda e, imp=imp, lt=lt: e.tensor_tensor(out=imp[:], in0=imp[:], in1=CM[:, 0, lt, :], op=ALU.add), reads=[imp, CM], writes=[imp])
            S.op("dve", lambda e, imp=imp, m8=m8: e.max(out=m8[:, 0:8], in_=imp[:]), reads=[imp], writes=[m8])
            S.op("dve", lambda e, imp=imp, m8=m8, rr=rr: e.match_replace(out=rr[:], in_to_replace=m8[:, 0:8], in_values=imp[:], imm_value=-1e30), reads=[imp, m8], writes=[rr])
            S.op("dve", lambda e, rr=rr, m8=m8: e.max(out=m8[:, 8:16], in_=rr[:]), reads=[rr], writes=[m8])
            S.op("dve", lambda e, imp=imp, m8=m8, rr=rr, lt=lt: e.scalar_tensor_tensor(out=rr[:], in0=imp[:], scalar=m8[:, 12:13], in1=CM[:, 1, lt, :], op0=ALU.is_ge, op1=ALU.mult),
                 reads=[imp, m8, CM], writes=[rr])
            S.op("dve", lambda e, rr=rr, lt=lt: e.tensor_tensor(out=rr[:], in0=rr[:], in1=CM[:, 2, lt, :], op=ALU.add), reads=[rr, CM], writes=[rr])
            S.op("dve", lambda e, rr=rr, mbo=mbo: e.tensor_scalar(out=mbo[:], in0=rr[:], scalar1=-NEGM, scalar2=NEGM, op0=ALU.mult, op1=ALU.add), reads=[rr], writes=[mbo])
            S.dma("sp", mbs[lt * 128:(lt + 1) * 128, :], mbo[:], reads=[mbo])
        S.finish()
    return nc


def build_p5():
    nc = bass.Bass("TRN2", target_bir_lowering=False)
    xs = dram_in(nc, "xs", [TOK, DM], F32)
    gnorm = dram_in(nc, "gnorm", [1, DM], F32)
    wz = dram_in(nc, "wz", [DM, P5_NCOLS], F32)
    bgate = dram_in(nc, "bgate", [1, 3072], F32)
    ng = dram_in(nc, "ng", [TOK, 18], F32)
    om = dram_in(nc, "om", [TOK, 6 * 65], F32)
    ofx = dram_in(nc, "ofx", [TOK, 6 * 65], F32)
    on = dram_in(nc, "on", [TOK, 3, 4 * 65], F32)
    wup = dram_in(nc, "wup", [DM, DM], F32)
    wout = dram_in(nc, "wout", [DM, DM], F32)
    ident = dram_in(nc, "ident", [128, 128], BF16)
    out = dram_out(nc, "out", [TOK, DM], F32)
    zz = nc.dram_tensor("zz_scr", [TOK, 1024], F32).ap()
    gg = nc.dram_tensor("gg_scr", [TOK, 3072], F32).ap()
    with ExitStack() as st:
        S = Sched(nc, st)
        ID = S.sb([128, 128], BF16, "ID")
        S.dma("sp", ID[:], ident, writes=[ID])
        EPSC = S.sb([128, 1], F32, "EPSC")
        S.op("dve", lambda e: e.memset(EPSC[:], EPS), writes=[EPSC])
        PST = Ring([S.ps([128, 128], F32, "PST") for _ in range(3)])
        with ExitStack() as stA:
            SA = S
            old_stack = S.stack
            S.stack = stA
            BIAS = S.sb([128, 3072], F32, "BIAS")
            S.dma("pool", BIAS[:], bgate.partition_broadcast(128), writes=[BIAS])
            XG, RSTD = emit_xg(S, xs, gnorm, ID, EPSC, PST)
            proj_loop(S, XG, RSTD, wz, P5_TILES, {"D": zz, "E": gg}, EPSC, BIAS=BIAS, nwbuf=2)
            S.stack = old_stack
            S.barrier()
            S.finish_part()
        WUP = S.sb([128, 8, DM], BF16, "WUP"); WOUT = S.sb([128, 8, DM], BF16, "WOUT")
        WF = Ring([S.sb([128, 2, DM], F32, "WF") for _ in range(2)])
        for wi, (src, dst) in enumerate(((wup, WUP), (wout, WOUT))):
            for q4 in range(4):
                wf = WF.next()
                S.dma("sp", wf[:], src[q4 * 256:(q4 + 1) * 256, :].rearrange("(c p) n -> p c n", p=128), writes=[wf])
                S.op("pool", lambda e, wf=wf, dst=dst, q4=q4: e.tensor_copy(out=dst[:, 2 * q4:2 * q4 + 2, :], in_=wf[:]), reads=[wf], writes=[dst])
        X = Ring([S.sb([128, DM], F32, "X") for _ in range(2)])
        Z = Ring([S.sb([128, DM], F32, "Z") for _ in range(2)])
        G = Ring([S.sb([128, 3072], F32, "G") for _ in range(2)])
        NG = Ring([S.sb([128, 18], F32, "NG") for _ in range(2)])
        OM = Ring([S.sb([128, 6, 65], F32, "OM") for _ in range(2)])
        OFX = Ring([S.sb([128, 6, 65], F32, "OFX") for _ in range(2)])
        ON = Ring([S.sb([128, 3, 4, 65], F32, "ON") for _ in range(2)])
        RD = Ring([S.sb([128, 32], F32, "RD") for _ in range(2)])
        T32 = Ring([S.sb([128, DM], F32, "T32") for _ in range(2)])
        TN = Ring([S.sb([128, 256], F32, "TN") for _ in range(2)])
        TB = Ring([S.sb([128, DM], BF16, "TB") for _ in range(2)])
        TT = Ring([S.sb([128, 8, 128], BF16, "TT") for _ in range(2)])
        MG = Ring([S.sb([128, DM], F32, "MG") for _ in range(2)])
        TM = Ring([S.sb([128, 512], F32, "TM") for _ in range(2)])
        MGB = Ring([S.sb([128, DM], BF16, "MGB") for _ in range(2)])
        MT = Ring([S.sb([128, 8, 128], BF16, "MT") for _ in range(2)])
        OUT = Ring([S.sb([128, DM], F32, "OUT") for _ in range(2)])
        PSY = Ring([S.ps([128, 512], F32, "PSY") for _ in range(3)])
        PSO = Ring([S.ps([128, 512], F32, "PSO") for _ in range(2)])
        ctxs = [dict() for _ in range(NTT)]

        def ldA(tt):
            c = ctxs[tt]
            rows_ = slice(tt * 128, (tt + 1) * 128)
            z = Z.next(); ngt = NG.next(); o_m = OM.next(); o_f = OFX.next(); o_n = ON.next()
            S.dma("sp", z[:], zz[rows_, :], writes=[z])
            S.dma("sp", ngt[:], ng[rows_, :], writes=[ngt])
            S.dma("sp", o_m[:], om[rows_, :].rearrange("q (h w) -> q h w", w=65), writes=[o_m])
            S.dma("sp", o_f[:], ofx[rows_, :].rearrange("q (h w) -> q h w", w=65), writes=[o_f])
            S.dma("sp", o_n[:], on[rows_, :, :].rearrange("q j (h w) -> q j h w", w=65), writes=[o_n])
            c.update(z=z, ngt=ngt, o_m=o_m, o_f=o_f, o_n=o_n)

        def ldC(tt):
            g = G.next()
            S.dma("sp", g[:], gg[tt * 128:(tt + 1) * 128, :], writes=[g])
            ctxs[tt]["g"] = g

        def ldE(tt):
            x = X.next()
            S.dma("sp", x[:], xs[tt * 128:(tt + 1) * 128, :], writes=[x])
            ctxs[tt]["x"] = x

        def stA(tt):
            c = ctxs[tt]
            z, ngt, o_m, o_f, o_n = c["z"], c["ngt"], c["o_m"], c["o_f"], c["o_n"]
            rd = RD.next(); t32 = T32.next(); tn = TN.next(); tb = TB.next()
            S.op("dve", lambda e, rd=rd, o_m=o_m: e.reciprocal(out=rd[:, 0:6], in_=o_m[:, :, 64]), reads=[o_m], writes=[rd])
            S.op("dve", lambda e, rd=rd, o_f=o_f: e.reciprocal(out=rd[:, 6:12], in_=o_f[:, :, 64]), reads=[o_f], writes=[rd])
            S.op("dve", lambda e, rd=rd, o_m=o_m, t32=t32: e.tensor_tensor(out=t32[:, 0:384].rearrange("p (h d) -> p h d", d=64), in0=o_m[:, :, 0:64],
                                                                         in1=rd[:, 0:6].unsqueeze(2).to_broadcast([128, 6, 64]), op=ALU.mult), reads=[o_m, rd], writes=[t32])
            S.op("dve", lambda e, rd=rd, o_f=o_f, t32=t32: e.tensor_tensor(out=t32[:, 640:1024].rearrange("p (h d) -> p h d", d=64), in0=o_f[:, :, 0:64],
                                                                         in1=rd[:, 6:12].unsqueeze(2).to_broadcast([128, 6, 64]), op=ALU.mult), reads=[o_f, rd], writes=[t32])
            S.op("dve", lambda e, rd=rd, o_n=o_n: e.tensor_scalar(out=rd[:, 12:24].rearrange("p (j h) -> p j h", h=4), in0=o_n[:, :, :, 64], scalar1=1e-30, scalar2=None, op0=ALU.max),
                 reads=[o_n], writes=[rd])
            S.op("dve", lambda e, rd=rd: e.reciprocal(out=rd[:, 12:24], in_=rd[:, 12:24]), reads=[rd], writes=[rd])
            S.op("dve", lambda e, rd=rd, ngt=ngt: e.tensor_tensor(out=rd[:, 12:24].rearrange("p (j h) -> p j h", h=4), in0=rd[:, 12:24].rearrange("p (j h) -> p j h", h=4),
                                                                  in1=ngt[:, 0:12].rearrange("p (h j) -> p j h", j=3), op=ALU.mult), reads=[rd, ngt], writes=[rd])
            for j in range(3):
                dst = t32 if j == 0 else tn
                dv = (t32[:, 384:640] if j == 0 else tn[:, 0:256]).rearrange("p (h d) -> p h d", d=64)
                S.op("dve", lambda e, rd=rd, o_n=o_n, dv=dv, j=j: e.tensor_tensor(out=dv, in0=o_n[:, j, :, 0:64],
                                                                                  in1=rd[:, 12 + 4 * j:16 + 4 * j].unsqueeze(2).to_broadcast([128, 4, 64]), op=ALU.mult),
                     reads=[o_n, rd], writes=[dst])
                if j > 0:
                    S.op("dve", lambda e, t32=t32, tn=tn: e.tensor_tensor(out=t32[:, 384:640], in0=t32[:, 384:640], in1=tn[:, 0:256], op=ALU.add), reads=[t32, tn], writes=[t32])
            S.op("pool", lambda e, t32=t32, z=z, tb=tb: e.tensor_tensor(out=tb[:], in0=t32[:], in1=z[:], op=ALU.mult), reads=[t32, z], writes=[tb])
            c["tb"] = tb

        def transp(src, dst):
            for cc in range(8):
                pt = PST.next()
                S.op("pe", lambda e, pt=pt, src=src, cc=cc: e.matmul(pt[:], lhsT=src[:, cc * 128:(cc + 1) * 128], rhs=ID[:], start=True, stop=True), reads=[src, ID], writes=[pt])
                S.op("act", lambda e, pt=pt, dst=dst, cc=cc: e.activation(out=dst[:, cc, :], in_=pt[:], func=AF.Copy), reads=[pt], writes=[dst])

        def stB(tt):
            c = ctxs[tt]
            ttt = TT.next()
            transp(c["tb"], ttt)
            c["ttt"] = ttt

        def stC(tt):
            c = ctxs[tt]
            ttt, g = c["ttt"], c["g"]
            mg = MG.next(); mgb = MGB.next()
            for ct in range(2):
                cols = slice(ct * 512, (ct + 1) * 512)
                for bb, chs in enumerate(((0, 1, 2), (3, 4), (5, 6, 7))):
                    py = PSY.next()
                    for k, ch in enumerate(chs):
                        S.op("pe", lambda e, py=py, ttt=ttt, ch=ch, cols=cols, k=k, n=len(chs): e.matmul(py[:], lhsT=ttt[:, ch, :], rhs=WUP[:, ch, cols], start=(k == 0), stop=(k == n - 1)),
                             reads=[ttt, WUP], writes=[py], signal=(k == len(chs) - 1))
                    gsl = slice(bb * 1024 + ct * 512, bb * 1024 + (ct + 1) * 512)
                    if bb == 0:
                        S.op("dve", lambda e, py=py, g=g, mg=mg, cols=cols, gsl=gsl: e.tensor_tensor(out=mg[:, cols], in0=py[:], in1=g[:, gsl], op=ALU.mult), reads=[py, g], writes=[mg])
                    else:
                        tm = TM.next()
                        S.op("dve", lambda e, py=py, g=g, tm=tm, gsl=gsl: e.tensor_tensor(out=tm[:], in0=py[:], in1=g[:, gsl], op=ALU.mult), reads=[py, g], writes=[tm])
                        S.op("pool", lambda e, mg=mg, tm=tm, cols=cols: e.tensor_tensor(out=mg[:, cols], in0=mg[:, cols], in1=tm[:], op=ALU.add), reads=[mg, tm], writes=[mg])
            S.op("pool", lambda e, mg=mg, mgb=mgb: e.tensor_copy(out=mgb[:], in_=mg[:]), reads=[mg], writes=[mgb])
            c["mgb"] = mgb

        def stD(tt):
            c = ctxs[tt]
            mt = MT.next()
            transp(c["mgb"], mt)
            c["mt"] = mt

        def stE(tt):
            c = ctxs[tt]
            mt, x = c["mt"], c["x"]
            ot = OUT.next()
            for ct in range(2):
                cols = slice(ct * 512, (ct + 1) * 512)
                po = PSO.next()
                for cc in range(8):
                    S.op("pe", lambda e, po=po, mt=mt, cc=cc, cols=cols: e.matmul(po[:], lhsT=mt[:, cc, :], rhs=WOUT[:, cc, cols], start=(cc == 0), stop=(cc == 7)),
                         reads=[mt, WOUT], writes=[po], signal=(cc == 7))
                S.op("dve", lambda e, po=po, x=x, ot=ot, cols=cols: e.tensor_tensor(out=ot[:, cols], in0=po[:], in1=x[:, cols], op=ALU.add), reads=[po, x], writes=[ot])
            S.dma("sp", out[tt * 128:(tt + 1) * 128, :], ot[:], reads=[ot])

        ldA(0)
        for s_ in range(NTT + 4):
            if s_ + 1 < NTT:
                ldA(s_ + 1)
            if 0 <= s_ - 1 < NTT:
                ldC(s_ - 1)
            if 0 <= s_ - 3 < NTT:
                ldE(s_ - 3)
            for fn_, tt_ in ((stA, s_), (stB, s_ - 1), (stC, s_ - 2), (stD, s_ - 3), (stE, s_ - 4)):
                if 0 <= tt_ < NTT:
                    fn_(tt_)
        S.finish()
    return nc


def bf(a):
    return np.ascontiguousarray(a).astype(NPBF) if a.dtype != NPBF else np.ascontiguousarray(a)


def mask_tile(fn):
    k = np.arange(128)[:, None]
    q = np.arange(512)[None, :]
    return np.where(fn(k, q), 0.0, NEGM).astype(np.float32)


def pack_masks(tiles):
    return bf(np.concatenate(tiles, axis=1))


M_CAUSAL = [mask_tile(lambda k, q, r=r: 128 * r + k <= q) for r in range(4)]
M_ZERO = np.zeros((128, 512), np.float32)
M_FULL = np.full((128, 512), NEGM, np.float32)
MASKS_PAR = [pack_masks(M_CAUSAL + [M_FULL] * 4), pack_masks([M_ZERO] * 4 + M_CAUSAL)]
MASKS_WIN = pack_masks([mask_tile(lambda k, q, r=r: (q - (128 * r + k) >= 0) & (q - (128 * r + k) < 512)) for r in range(-4, 4)])
MASKS_CMP = [pack_masks([mask_tile(lambda k, q, r=r0 - par: 16 * k + 31 + 512 * r <= q) for r0 in range(-4, 1)]) for par in range(2)]
IDENT = np.eye(128, dtype=np.float32).astype(NPBF)


def dense_sched(nvar):
    sched = []
    for i in range(16):
        sched.append([(kt, (kt // 32) if nvar == 4 else 0, (kt - 8 * i) if kt >= 8 * i else None) for kt in range(8 * i + 8)])
    return sched


def win_sched():
    return [[(8 * i + j, 0, j) for j in range(8)] for i in range(16)]


def cmp_sched():
    sched = []
    for i in range(16):
        prs = []
        for j in range(8):
            r0 = 4 * j - 2 * i
            if r0 >= 2:
                continue
            prs.append((j, 0, None if r0 <= -5 else r0 + 4))
        sched.append(prs)
    return sched


def va_pack(v, extra=None):
    nk = v.shape[0]
    cols = [v.astype(NPBF), np.ones((nk, 1), NPBF)]
    if extra is not None:
        cols.append(extra.astype(NPBF))
    va = np.concatenate(cols, axis=1)
    W = va.shape[1]
    return np.ascontiguousarray(va.reshape(nk // 128, 128, W).transpose(1, 0, 2).reshape(128, (nk // 128) * W))


def par_qidx(par):
    return np.concatenate([np.arange(1024 * i + 512 * par, 1024 * i + 512 * par + 512) for i in range(16)])


_NC_CACHE = {}


def get_nc(key, fn):
    if key not in _NC_CACHE:
        _NC_CACHE[key] = fn()
    return _NC_CACHE[key]


def run(nc, in_maps):
    res = run_bass_kernel_spmd(nc, in_maps, core_ids=list(range(NCORES)))
    return res.results


def layer_forward(xl, p):
    S = S_LEN
    o1 = run_p1(xl, p["norm_g"], p["w_in"], p["b_f"], p["b_gate"], p["moba_qk_g"], p["nsa_q_g"], p["nsa_k_g"], p["fox_qk_g"])
    oA, oB, oC, oF = (o1[k] for k in ("oA", "oB", "oC", "oF"))
    mq, mk, nq, ksl, kw = oA[:, 0:384], oA[:, 384:768], oA[:, 768:1024], oA[:, 1024:1088], oA[:, 1088:1152]
    fq, fk = oB[:, 0:384], oB[:, 384:768]
    mv, fv, kc, vc, vsl, vw = oC[:, 0:384], oC[:, 384:768], oC[:, 768:832], oC[:, 832:896], oC[:, 896:960], oC[:, 960:1024]
    lf = np.ascontiguousarray(oF[:, 12:18].T)
    gidx = (np.arange(1023) * 16)[:, None] + np.arange(32)[None, :]

    def flat_of(t):
        blocks = np.zeros((1024, 32, 64), NPBF)
        blocks[:1023] = t[gidx]
        return blocks.reshape(1024, 2048).T.reshape(16, 128, 1024)
    flk_all, flv_all = flat_of(kc), flat_of(vc)
    w1 = np.ascontiguousarray(p["cmp_w1"].reshape(2, 16, 128, 256).transpose(2, 0, 1, 3))
    w2 = np.ascontiguousarray(p["cmp_w2"].reshape(2, 2, 128, 64).transpose(2, 0, 1, 3))
    pe = np.ascontiguousarray(p["cmp_pe"].reshape(2, 16, 128).transpose(2, 0, 1))
    kgain = np.ascontiguousarray(p["nsa_k_g"][0][None, :])
    in_maps = []
    for c in range(NCORES):
        cosc, sinc = rope_cs(np.arange(128 * c, 128 * c + 128) * 16 + 31)
        qs = mq[c * TOK:(c + 1) * TOK]
        qT = np.ascontiguousarray(qs.reshape(TOK, 6, 64).transpose(2, 1, 0).reshape(64, 6 * TOK))
        gm = np.zeros((128, 3, NTT, 64), np.float32)
        for lt in range(NTT):
            cur = (16 * c + lt) // 2
            gm[:, 0, lt, cur:] = -1e30
            gm[:, 1, lt, :cur] = 1.0
            gm[:, 2, lt, cur] = 1.0
        in_maps.append({"lf": lf, "flk": np.ascontiguousarray(flk_all[:, :, 128 * c:128 * c + 128].transpose(1, 0, 2)),
                        "flv": np.ascontiguousarray(flv_all[:, :, 128 * c:128 * c + 128].transpose(1, 0, 2)),
                        "w1": w1, "w2": w2, "pe": pe, "kgain": kgain, "csc": np.ascontiguousarray(np.stack([cosc, sinc], axis=1)),
                        "qT": qT, "kmT": o1["kmT"], "gm": gm})
    r2 = run(get_nc("p2", build_p2), in_maps)
    csplit = r2[0]["csplit"]
    kcmp = np.concatenate([r["kcmp"] for r in r2], axis=0)
    vcmp = np.concatenate([r["vcmp"] for r in r2], axis=0)
    mb = np.concatenate([r["mb"] for r in r2], axis=0)
    keys = np.arange(S)
    E_moba = (keys[None, :] // 256 == np.arange(64)[:, None]).astype(NPBF)
    E_slc = ((keys[None, :] // 64) % 64 == np.arange(64)[:, None]).astype(NPBF)
    ones3 = np.ones((3, S), NPBF)

    def dense_job_inputs(branch, h, par):
        qi = par_qidx(par)
        KA = np.zeros((128, S), NPBF)
        QA = np.zeros((128, 8192), NPBF)
        if branch == "fox":
            KA[0:64] = fk[:, h * 64:(h + 1) * 64].T
            KA[64:67] = csplit[h, 3:6]
            KA[67:70] = ones3
            QA[0:64] = fq[qi, h * 64:(h + 1) * 64].T
            QA[64:67] = ones3[:, qi]
            QA[67:70] = csplit[h, 0:3][:, qi]
            VA = va_pack(fv[:, h * 64:(h + 1) * 64])
        else:
            KA[0:64] = mk[:, h * 64:(h + 1) * 64].T
            KA[64:128] = E_moba
            QA[0:64] = mq[qi, h * 64:(h + 1) * 64].T
            QA[64:128] = mb[qi, h * 64:(h + 1) * 64].T
            VA = va_pack(mv[:, h * 64:(h + 1) * 64])
        return KA, VA, QA, MASKS_PAR[par]
    djobs = [(br, h, par) for br in ("fox", "moba") for h in range(6) for par in range(2)]
    overlap = np.zeros((1024, 256), np.float32)
    for m in range(256):
        lo_, hi_ = max(4 * m - 1, 0), min(4 * m + 3, 1022)
        overlap[lo_:hi_ + 1, m] = 1.0
    kcmpT = np.ascontiguousarray(kcmp.T)
    va_cmp = va_pack(vcmp, overlap)
    kwT = np.ascontiguousarray(kw.T)
    va_win = va_pack(vw)
    spec_d = dict(Kc=128, NK=S, W=65, nvar=1, NQ=8192, nmask=8, sched=dense_sched(1))
    jobs3 = [spec_d, spec_d, spec_d, dict(Kc=64, NK=S, W=65, nvar=1, NQ=8192, nmask=8, sched=win_sched()),
             dict(Kc=64, NK=1024, W=321, nvar=1, NQ=8192, nmask=5, sched=cmp_sched())]
    kw_sh = [np.concatenate([np.zeros((64, 512), NPBF), kwT[:, :S - 512]], axis=1), kwT]
    vw_aug = np.concatenate([vw.astype(NPBF), np.ones((S, 1), NPBF)], axis=1)
    vw_sh = [np.concatenate([np.zeros((512, 65), NPBF), vw_aug[:S - 512]], axis=0), vw_aug]
    va_win = [np.ascontiguousarray(v.reshape(S // 128, 128, 65).transpose(1, 0, 2).reshape(128, (S // 128) * 65)) for v in vw_sh]
    in_maps = []
    for c in range(NCORES):
        m = {"ident": IDENT}
        for s_ in range(3):
            KA, VA, QA, MK = dense_job_inputs(*djobs[3 * c + s_])
            m[f"ka{s_}"], m[f"va{s_}"], m[f"qa{s_}"], m[f"mk{s_}"] = KA, VA, QA, MK
        hh, par = c % 4, c // 4
        nqT = np.ascontiguousarray(nq[par_qidx(par), hh * 64:(hh + 1) * 64].T)
        m["ka3"], m["va3"], m["qa3"], m["mk3"] = np.ascontiguousarray(kw_sh[par]), va_win[par], nqT, MASKS_WIN
        m["ka4"], m["va4"], m["qa4"], m["mk4"] = kcmpT, va_cmp, nqT, MASKS_CMP[par]
        in_maps.append(m)
    r3 = run(get_nc("attn3", lambda: build_attn(jobs3, "a3")), in_maps)
    o_fox = np.zeros((6, S, 65), np.float32)
    o_moba = np.zeros((6, S, 65), np.float32)
    for jid, (br, h, par) in enumerate(djobs):
        (o_fox if br == "fox" else o_moba)[h, par_qidx(par)] = r3[jid // 3][f"o{jid % 3}"].T
    o_win = np.zeros((4, S, 65), np.float32)
    o_cmp = np.zeros((4, S, 321), np.float32)
    for c in range(NCORES):
        o_win[c % 4, par_qidx(c // 4)] = r3[c]["o3"].T
        o_cmp[c % 4, par_qidx(c // 4)] = r3[c]["o4"].T
    in_maps = []
    for c in range(NCORES):
        cm = np.zeros((128, 3, NTT, 256), np.float32)
        t = (c * TOK + np.arange(TOK)).reshape(NTT, 128).T
        cur = t // 64
        mm = np.arange(256)[None, None, :]
        cand = (mm >= 1) & (mm <= cur[:, :, None] - 2)
        forced = (mm == 0) | (mm == cur[:, :, None]) | (mm == cur[:, :, None] - 1)
        cm[:, 0] = np.where(cand, 0.0, -1e30)
        cm[:, 1] = cand
        cm[:, 2] = forced
        in_maps.append({"oc": np.ascontiguousarray(o_cmp[:, c * TOK:(c + 1) * TOK, 64:321]), "cm": cm})
    r4 = run(get_nc("p4", build_p4), in_maps)
    mbs = np.concatenate([r["mbs"] for r in r4], axis=0)
    spec_s = dict(Kc=128, NK=S, W=65, nvar=4, NQ=8192, nmask=8, sched=dense_sched(4))
    KAs = np.zeros((128, S), NPBF)
    KAs[0:64] = ksl.T
    KAs[64:128] = E_slc
    va_s = va_pack(vsl)
    in_maps = []
    for c in range(NCORES):
        h, par = c // 2, c % 2
        qi = par_qidx(par)
        QA = np.zeros((128, 4, 8192), NPBF)
        QA[0:64] = nq[qi, h * 64:(h + 1) * 64].T[:, None, :]
        QA[64:128] = mbs[qi].T.reshape(4, 64, 8192).transpose(1, 0, 2)
        in_maps.append({"ident": IDENT, "ka0": KAs, "va0": va_s, "qa0": QA.reshape(128, 4 * 8192), "mk0": MASKS_PAR[par]})
    r5 = run(get_nc("attn5", lambda: build_attn([spec_s], "a5")), in_maps)
    o_slc = np.zeros((4, S, 65), np.float32)
    for c in range(NCORES):
        o_slc[c // 2, par_qidx(c % 2)] = r5[c]["o0"].T
    wz = np.ascontiguousarray(p["w_in"][:, p5_col_perm()])
    wup = np.ascontiguousarray(np.concatenate([p["w_up_moba"], p["w_up_nsa"], p["w_up_fox"]], axis=0))
    om = np.ascontiguousarray(o_moba.transpose(1, 0, 2).reshape(S, 6 * 65))
    ofx = np.ascontiguousarray(o_fox.transpose(1, 0, 2).reshape(S, 6 * 65))
    on = np.ascontiguousarray(np.stack([o_cmp[:, :, 0:65], o_slc, o_win], axis=0).transpose(2, 0, 1, 3).reshape(S, 3, 4 * 65))
    in_maps = []
    for c in range(NCORES):
        sl = slice(c * TOK, (c + 1) * TOK)
        in_maps.append({"xs": np.ascontiguousarray(xl[sl]), "gnorm": np.ascontiguousarray(p["norm_g"][None, :]), "wz": wz,
                        "bgate": np.ascontiguousarray(p["b_gate"][None, :]),
                        "ng": np.ascontiguousarray(oF[sl]), "om": om[sl], "ofx": ofx[sl], "on": on[sl], "wup": wup, "wout": p["w_out"], "ident": IDENT})
    r6 = run(get_nc("p5", build_p5), in_maps)
    dbg = dict(o_fox=o_fox, o_moba=o_moba, o_win=o_win, o_cmp=o_cmp, o_slc=o_slc, mb=mb, mbs=mbs, kcmp=kcmp, vcmp=vcmp, csplit=csplit)
    return np.concatenate([r["out"] for r in r6], axis=0), dbg


PARAM_KEYS = ["norm_g", "w_in", "b_f", "b_gate", "moba_qk_g", "nsa_q_g", "nsa_k_g", "fox_qk_g", "cmp_pe", "cmp_w1", "cmp_w2",
              "w_up_moba", "w_up_nsa", "w_up_fox", "w_out"]


def kernel(**inputs):
    x = np.asarray(inputs["x"], np.float32)
    xl = np.ascontiguousarray(x[0])
    for l in range(2):
        p = {k: np.ascontiguousarray(np.asarray(inputs[k], np.float32)[l]) for k in PARAM_KEYS}
        xl, _ = layer_forward(xl, p)
    return xl[None].astype(np.float32)
```

```python
import numpy as np
import ml_dtypes
from contextlib import ExitStack
import concourse.bass as bass
import concourse.mybir as mybir
from concourse.bass_utils import run_bass_kernel_spmd

F32 = mybir.dt.float32
BF16 = mybir.dt.bfloat16
AF = mybir.ActivationFunctionType
ALU = mybir.AluOpType
AX = mybir.AxisListType
NPBF = ml_dtypes.bfloat16

NCORES = 8
S_LEN = 16384
DM = 1024
HD = 64
EPS = 1e-6
SCALE = 0.125
NEGM = -30000.0


class Buf:
    def __init__(self, t, name):
        self.t = t
        self.name = name
        self.w = None
        self.r = []
        self.dsem = None
        self.dcnt = 0

    def __getitem__(self, k):
        return self.t[k]


class Sched:
    ENG = ("pe", "act", "dve", "pool", "sp")

    def __init__(self, nc, stack):
        self.nc = nc
        self.stack = stack
        self.sem_stack = stack
        self.ops = {e: [] for e in self.ENG}
        self.sem = {e: stack.enter_context(nc.semaphore("S_" + e)) for e in self.ENG}
        self.seq = {e: 0 for e in self.ENG}
        self.known = {e: {} for e in self.ENG}
        self.semobjs = {}
        self.dma_tokens = []
        self.nb = 0

    def sb(self, shape, dt, name=None):
        self.nb += 1
        name = (name or "sb") + f"_{self.nb}"
        t = self.stack.enter_context(self.nc.sbuf_tensor(name, list(shape), dt))
        return Buf(t, name)

    def ps(self, shape, dt, name=None):
        self.nb += 1
        name = (name or "ps") + f"_{self.nb}"
        t = self.stack.enter_context(self.nc.psum_tensor(name, list(shape), dt))
        return Buf(t, name)

    def _dsem(self, b):
        if b.dsem is None:
            b.dsem = self.sem_stack.enter_context(self.nc.semaphore("D_" + b.name))
        return b.dsem

    def _waits(self, eng, reads, writes):
        toks = []
        for b in list(reads) + list(writes):
            if b.w is not None:
                toks.append(b.w)
        for b in writes:
            toks.extend(b.r)
        best = {}
        for (s, v) in toks:
            k = id(s)
            self.semobjs[k] = s
            if v > best.get(k, 0):
                best[k] = v
        kn = self.known[eng]
        for k, v in best.items():
            if eng == "pe" and self.semobjs[k] is self.sem["pe"]:
                continue
            if kn.get(k, 0) >= v:
                continue
            kn[k] = v
            self.ops[eng].append(("wait", self.semobjs[k], v))

    def op(self, eng, fn, reads=(), writes=(), signal=True):
        self._waits(eng, reads, writes)
        tok = (self.sem[eng], self.seq[eng] + 1)
        if signal:
            self.seq[eng] += 1
        self.ops[eng].append(("op", fn, self.sem[eng] if signal else None, 1))
        for b in reads:
            b.r.append(tok)
        for b in writes:
            b.w = tok
            b.r = []
        return tok

    def dma(self, q, out_ap, in_ap, reads=(), writes=(), **kw):
        self._waits(q, reads, writes)
        owner = (list(writes) + list(reads))[0]
        s = self._dsem(owner)
        owner.dcnt += 16
        tok = (s, owner.dcnt)
        self.ops[q].append(("op", (lambda e, o=out_ap, i=in_ap, kw=kw: e.dma_start(out=o, in_=i, **kw)), s, 16))
        for b in reads:
            b.r.append(tok)
        for b in writes:
            b.w = tok
            b.r = []
        self.dma_tokens.append(tok)
        return tok

    def barrier(self):
        best = {}
        for (s_, v) in self.dma_tokens:
            k = id(s_)
            self.semobjs[k] = s_
            best[k] = max(best.get(k, 0), v)
        for e in self.ENG:
            if self.seq[e] > 0:
                k = id(self.sem[e])
                self.semobjs[k] = self.sem[e]
                best[k] = self.seq[e]
        for e in self.ENG:
            kn = self.known[e]
            for k, v in best.items():
                if self.semobjs[k] is self.sem[e]:
                    continue
                if kn.get(k, 0) >= v:
                    continue
                kn[k] = v
                self.ops[e].append(("wait", self.semobjs[k], v))

    def finish(self):
        nc = self.nc
        best = {}
        for (s, v) in self.dma_tokens:
            k = id(s)
            self.semobjs[k] = s
            best[k] = max(best.get(k, 0), v)
        for e in self.ENG:
            if e != "sp" and self.seq[e] > 0:
                k = id(self.sem[e])
                self.semobjs[k] = self.sem[e]
                best[k] = self.seq[e]
        for k, v in best.items():
            self.ops["sp"].append(("wait", self.semobjs[k], v))
        self._emit_block()

    def finish_part(self):
        self._emit_block()
        self.ops = {e: [] for e in self.ENG}

    def _emit_block(self):
        nc = self.nc
        ops = self.ops

        def replay(e, lst):
            for it in lst:
                if it[0] == "wait":
                    e.wait_ge(it[1], it[2])
                else:
                    ins = it[1](e)
                    if it[2] is not None:
                        ins.then_inc(it[2], it[3])

        with nc.Block() as block:
            @block.tensor
            def _(e):
                replay(e, ops["pe"])

            @block.scalar
            def _(e):
                replay(e, ops["act"])

            @block.vector
            def _(e):
                replay(e, ops["dve"])

            @block.gpsimd
            def _(e):
                replay(e, ops["pool"])

            @block.sync
            def _(e):
                replay(e, ops["sp"])


class Ring:
    def __init__(self, bufs):
        self.bufs = bufs
        self.i = 0

    def next(self):
        b = self.bufs[self.i % len(self.bufs)]
        self.i += 1
        return b


def dram_in(nc, name, shape, dt):
    return nc.dram_tensor(name, list(shape), dt, kind="ExternalInput").ap()


def dram_out(nc, name, shape, dt):
    return nc.dram_tensor(name, list(shape), dt, kind="ExternalOutput").ap()


TOK = S_LEN // NCORES
NTT = TOK // 128
P1_TILES = [("A", 512), ("A", 512), ("A", 128), ("B", 512), ("B", 256), ("C", 512), ("C", 512), ("F", 18)]
P1_NCOLS = sum(w for _, w in P1_TILES)
P5_TILES = [("D", 512), ("D", 512)] + [("E", 512)] * 6
P5_NCOLS = 4096


def col_ranges():
    sp = [384] * 4 + [256] + [64] * 6 + [12, 256] + [384] * 3 + [6, 384, 3072]
    names = ["mq", "mk", "mv", "mz", "nq", "kc", "vc", "ksl", "vsl", "kw", "vw", "ng", "nz", "fq", "fk", "fv", "ff", "fz", "gl"]
    off = np.concatenate([[0], np.cumsum(sp)])
    return {n: np.arange(off[i], off[i + 1]) for i, n in enumerate(names)}


def p1_col_perm():
    rng = col_ranges()
    return np.concatenate([rng[n] for n in ["mq", "mk", "nq", "ksl", "kw", "fq", "fk", "mv", "fv", "kc", "vc", "vsl", "vw", "ng", "ff"]])


def p5_col_perm():
    rng = col_ranges()
    return np.concatenate([rng[n] for n in ["mz", "nz", "fz", "gl"]])


def emit_xg(S, x, gnorm, ID, EPSC, PST):
    XG = S.sb([128, 8, TOK], BF16, "XG")
    RSTD = S.sb([128, NTT], F32, "RSTD")
    GREP = S.sb([128, DM], F32, "GREP")
    S.dma("pool", GREP[:], gnorm.partition_broadcast(128), writes=[GREP])
    XS = Ring([S.sb([128, DM], F32, "XS") for _ in range(2)])
    XB = Ring([S.sb([128, DM], BF16, "XB") for _ in range(2)])
    JUNK = S.sb([128, DM], BF16, "JUNK")
    SSQ = S.sb([128, NTT], F32, "SSQ")
    for tt in range(NTT):
        b = XS.next(); xb = XB.next()
        S.dma("sp", b[:], x[tt * 128:(tt + 1) * 128, :], writes=[b])
        S.op("act", lambda e, b=b, tt=tt: e.activation(out=JUNK[:], in_=b[:], func=AF.Square, accum_out=SSQ[:, tt:tt + 1]), reads=[b], writes=[JUNK, SSQ])
        S.op("pool", lambda e, b=b, xb=xb: e.tensor_tensor(out=xb[:], in0=b[:], in1=GREP[:], op=ALU.mult), reads=[b, GREP], writes=[xb])
        for c in range(8):
            pt = PST.next()
            S.op("pe", lambda e, pt=pt, xb=xb, c=c: e.matmul(pt[:], lhsT=xb[:, c * 128:(c + 1) * 128], rhs=ID[:], start=True, stop=True), reads=[xb, ID], writes=[pt])
            if c % 2 == 0:
                S.op("act", lambda e, pt=pt, c=c, tt=tt: e.activation(out=XG[:, c, tt * 128:(tt + 1) * 128], in_=pt[:], func=AF.Copy), reads=[pt], writes=[XG])
            else:
                S.op("dve", lambda e, pt=pt, c=c, tt=tt: e.tensor_copy(out=XG[:, c, tt * 128:(tt + 1) * 128], in_=pt[:]), reads=[pt], writes=[XG])
    S.op("act", lambda e: e.activation(out=SSQ[:], in_=SSQ[:], func=AF.Sqrt, bias=EPSC[:], scale=1.0 / DM), reads=[SSQ, EPSC], writes=[SSQ])
    S.op("dve", lambda e: e.reciprocal(out=RSTD[:], in_=SSQ[:]), reads=[SSQ], writes=[RSTD])
    return XG, RSTD


def proj_loop(S, XG, RSTD, w, tiles, sinks, EPSC, BIAS=None, GAINS=None, CS=None, kmean=None, nwbuf=2):
    WF = Ring([S.sb([128, 8, 512], F32, "WF") for _ in range(nwbuf)])
    WB = Ring([S.sb([128, 8, 512], BF16, "WB") for _ in range(2)])
    PS = Ring([S.ps([128, 512], F32, "PS") for _ in range(4)])
    kinds = set(k for k, _ in tiles)
    OF = Ring([S.sb([128, 512], F32, "OF") for _ in range(4)]) if kinds & {"D", "E", "F"} else None
    TMP = Ring([S.sb([128, 512], F32, "TMP") for _ in range(3)]) if kinds & {"E", "F"} else None
    if kinds & {"A", "B", "C"}:
        Y = Ring([S.sb([128, 512], F32, "Y") for _ in range(4)])
        SQ = Ring([S.sb([128, 512], F32, "SQ") for _ in range(3)])
        RS = Ring([S.sb([128, 16], F32, "RS") for _ in range(4)])
        RT = Ring([S.sb([128, 4, 8, 8], F32, "RT") for _ in range(4)])
        OBF = Ring([S.sb([128, 512], BF16, "OBF") for _ in range(4)])
    c0 = 0
    kcol = {k: 0 for k in "ABCDEF"}
    offs = np.concatenate([[0], np.cumsum([wd_ for _, wd_ in tiles])]).tolist()

    def load_w(ti_):
        wd_ = tiles[ti_][1]
        wf_ = WF.bufs[ti_ % len(WF.bufs)]
        S.dma("sp" if ti_ % 2 == 0 else "pool", wf_[:, :, 0:wd_], w[:, offs[ti_]:offs[ti_] + wd_].rearrange("(c p) n -> p c n", p=128), writes=[wf_])

    pend1 = [None]
    pend2 = [None]
    load_w(0)
    for ti, (kind, wd) in enumerate(tiles):
        if kind not in ("A", "B"):
            if pend1[0] is not None:
                pf1, pf2 = pend1[0]
                pf1()
                if pend2[0] is not None:
                    pend2[0]()
                pf2()
                pend1[0] = None
                pend2[0] = None
        wf = WF.bufs[ti % len(WF.bufs)]
        wb = WB.next()
        if ti + 1 < len(tiles):
            load_w(ti + 1)
        for half in range(2):
            eng = "pool" if half == 0 else "dve"
            S.op(eng, lambda e, wf=wf, wb=wb, half=half, wd=wd: e.tensor_copy(out=wb[:, 4 * half:4 * half + 4, 0:wd], in_=wf[:, 4 * half:4 * half + 4, 0:wd]),
                 reads=[wf], writes=[wb])
        k0 = kcol[kind]
        for tt in range(NTT):
            ps = PS.next()
            for c in range(8):
                S.op("pe", lambda e, ps=ps, wb=wb, c=c, tt=tt, wd=wd: e.matmul(ps[:, 0:wd], lhsT=XG[:, c, tt * 128:(tt + 1) * 128], rhs=wb[:, c, 0:wd],
                                                                               start=(c == 0), stop=(c == 7)),
                     reads=[XG, wb], writes=[ps], signal=(c == 7))
            rs_t = RSTD[:, tt:tt + 1]
            rows = slice(tt * 128, (tt + 1) * 128)
            if kind == "C":
                o = OBF.next()
                S.op("act", lambda e, o=o, ps=ps, rs_t=rs_t, wd=wd: e.activation(out=o[:, 0:wd], in_=ps[:, 0:wd], func=AF.Copy, scale=rs_t),
                     reads=[ps, RSTD], writes=[o])
                S.dma("sp", sinks["C"][rows, k0:k0 + wd], o[:, 0:wd], reads=[o])
            elif kind == "D":
                o = OF.next()
                S.op("act", lambda e, o=o, ps=ps, rs_t=rs_t, wd=wd: e.activation(out=o[:, 0:wd], in_=ps[:, 0:wd], func=AF.Silu, scale=rs_t),
                     reads=[ps, RSTD], writes=[o])
                S.dma("sp", sinks["D"][rows, k0:k0 + wd], o[:, 0:wd], reads=[o])
            elif kind == "E":
                t = TMP.next()
                o = OF.next()
                S.op("dve", lambda e, t=t, ps=ps, rs_t=rs_t, k0=k0, wd=wd: e.scalar_tensor_tensor(out=t[:, 0:wd], in0=ps[:, 0:wd], scalar=rs_t, in1=BIAS[:, k0:k0 + wd],
                                                                                               op0=ALU.mult, op1=ALU.add),
                     reads=[ps, RSTD, BIAS], writes=[t])
                S.op("act", lambda e, o=o, t=t, wd=wd: e.activation(out=o[:, 0:wd], in_=t[:, 0:wd], func=AF.Sigmoid), reads=[t], writes=[o])
                S.dma("sp", sinks["E"][rows, k0:k0 + wd], o[:, 0:wd], reads=[o])
            elif kind == "F":
                t = TMP.next()
                o = OF.next()
                S.op("act", lambda e, o=o, ps=ps, rs_t=rs_t: e.activation(out=o[:, 0:12], in_=ps[:, 0:12], func=AF.Sigmoid, scale=rs_t),
                     reads=[ps, RSTD], writes=[o])
                S.op("dve", lambda e, t=t, ps=ps, rs_t=rs_t: e.scalar_tensor_tensor(out=t[:, 0:6], in0=ps[:, 12:18], scalar=rs_t, in1=BIAS[:, 0:6],
                                                                                 op0=ALU.mult, op1=ALU.add),
                     reads=[ps, RSTD, BIAS], writes=[t])
                S.op("act", lambda e, t=t: e.activation(out=t[:, 8:14], in_=t[:, 0:6], func=AF.Exp, scale=-1.0), reads=[t], writes=[t])
                S.op("act", lambda e, t=t: e.activation(out=t[:, 16:22], in_=t[:, 8:14], func=AF.Ln, bias=1.0), reads=[t], writes=[t])
                S.op("dve", lambda e, t=t, o=o: e.tensor_scalar(out=o[:, 12:18], in0=t[:, 16:22], scalar1=-1.0, scalar2=None, op0=ALU.mult),
                     reads=[t], writes=[o])
                S.dma("sp", sinks["F"][rows, :], o[:, 0:18], reads=[o])
            else:
                nh = wd // 64
                y = Y.next(); sq = SQ.next(); rs = RS.next(); o = OBF.next()
                goff = k0 if kind == "A" else 1152 + k0
                rt = RT.next() if kind == "A" else None

                def f0(y=y, sq=sq, rs=rs, ps=ps, rs_t=rs_t, wd=wd, nh=nh):
                    S.op("act", lambda e: e.activation(out=y[:, 0:wd], in_=ps[:, 0:wd], func=AF.Copy, scale=rs_t), reads=[ps, RSTD], writes=[y])
                    S.op("pool", lambda e: e.tensor_tensor(out=sq[:, 0:wd], in0=y[:, 0:wd], in1=y[:, 0:wd], op=ALU.mult), reads=[y], writes=[sq])
                    S.op("dve", lambda e: e.tensor_reduce(out=rs[:, 0:nh], in_=sq[:, 0:wd].rearrange("p (h d) -> p h d", d=64), axis=AX.X, op=ALU.add), reads=[sq], writes=[rs])

                def f1(y=y, rs=rs, o=o, wd=wd, nh=nh, goff=goff, kind=kind, rows=rows, k0=k0):
                    S.op("act", lambda e: e.activation(out=rs[:, 0:nh], in_=rs[:, 0:nh], func=AF.Sqrt, bias=EPSC[:], scale=1.0 / 64), reads=[rs, EPSC], writes=[rs])
                    S.op("dve", lambda e: e.reciprocal(out=rs[:, 0:nh], in_=rs[:, 0:nh]), reads=[rs], writes=[rs])
                    S.op("dve", lambda e: e.tensor_tensor(out=y[:, 0:wd].rearrange("p (h d) -> p h d", d=64), in0=y[:, 0:wd].rearrange("p (h d) -> p h d", d=64),
                                                          in1=rs[:, 0:nh].unsqueeze(2).to_broadcast([128, nh, 64]), op=ALU.mult), reads=[y, rs], writes=[y])
                    if kind == "B":
                        S.op("pool", lambda e: e.tensor_tensor(out=o[:, 0:wd], in0=y[:, 0:wd], in1=GAINS[:, goff:goff + wd], op=ALU.mult), reads=[y, GAINS], writes=[o])
                        S.dma("sp", sinks["B"][rows, k0:k0 + wd], o[:, 0:wd], reads=[o])
                    else:
                        S.op("pool", lambda e: e.tensor_tensor(out=y[:, 0:wd], in0=y[:, 0:wd], in1=GAINS[:, goff:goff + wd], op=ALU.mult), reads=[y, GAINS], writes=[y])

                def f2(y=y, o=o, rt=rt, wd=wd, nh=nh, kind=kind, rows=rows, k0=k0, tt=tt):
                    if kind == "B":
                        return
                    yv = y[:, 0:wd].rearrange("p (h d) -> p h d", d=64)
                    ov = o[:, 0:wd].rearrange("p (h d) -> p h d", d=64)
                    cosb = CS[:, 0, tt, :].unsqueeze(1).to_broadcast([128, nh, 8])
                    sinb = CS[:, 1, tt, :].unsqueeze(1).to_broadcast([128, nh, 8])
                    S.op("act", lambda e: e.activation(out=o[:, 0:wd], in_=y[:, 0:wd], func=AF.Copy), reads=[y], writes=[o])
                    S.op("dve", lambda e: e.tensor_tensor(out=rt[:, 0, 0:nh, :], in0=yv[:, :, 0:8], in1=cosb, op=ALU.mult), reads=[y, CS], writes=[rt])
                    S.op("dve", lambda e: e.tensor_tensor(out=rt[:, 1, 0:nh, :], in0=yv[:, :, 8:16], in1=sinb, op=ALU.mult), reads=[y, CS], writes=[rt])
                    S.op("dve", lambda e: e.tensor_tensor(out=rt[:, 2, 0:nh, :], in0=yv[:, :, 8:16], in1=cosb, op=ALU.mult), reads=[y, CS], writes=[rt])
                    S.op("dve", lambda e: e.tensor_tensor(out=rt[:, 3, 0:nh, :], in0=yv[:, :, 0:8], in1=sinb, op=ALU.mult), reads=[y, CS], writes=[rt])
                    S.op("dve", lambda e: e.tensor_tensor(out=ov[:, :, 0:8], in0=rt[:, 0, 0:nh, :], in1=rt[:, 1, 0:nh, :], op=ALU.subtract), reads=[rt], writes=[o])
                    S.op("dve", lambda e: e.tensor_tensor(out=ov[:, :, 8:16], in0=rt[:, 2, 0:nh, :], in1=rt[:, 3, 0:nh, :], op=ALU.add), reads=[rt], writes=[o])
                    if kmean is not None:
                        KMP, C256 = kmean
                        for hc in range(0, wd, 64):
                            gc = k0 + hc
                            if 384 <= gc < 768:
                                h = (gc - 384) // 64
                                S.op("pe", lambda e, hc=hc, h=h: e.matmul(KMP[:, h * NTT + tt:h * NTT + tt + 1], lhsT=o[:, hc:hc + 64], rhs=C256[:, 0:1], start=True, stop=True),
                                     reads=[o, C256], writes=[KMP])
                    S.dma("sp", sinks["A"][rows, k0:k0 + wd], o[:, 0:wd], reads=[o])

                f0()
                if pend1[0] is not None:
                    pf1, pf2 = pend1[0]
                    pf1()
                    if pend2[0] is not None:
                        pend2[0]()
                    pend2[0] = pf2
                pend1[0] = (f1, f2)
        kcol[kind] += wd
        c0 += wd
    if pend1[0] is not None:
        pf1, pf2 = pend1[0]
        pf1()
        if pend2[0] is not None:
            pend2[0]()
        pf2()
    elif pend2[0] is not None:
        pend2[0]()


def build_p1():
    nc = bass.Bass("TRN2", target_bir_lowering=False)
    x = dram_in(nc, "x", [TOK, DM], F32)
    gnorm = dram_in(nc, "gnorm", [1, DM], F32)
    w = dram_in(nc, "w", [DM, P1_NCOLS], F32)
    cs = dram_in(nc, "cs", [128, 2, NTT, 8], F32)
    gains = dram_in(nc, "gains", [1, 1920], F32)
    bias = dram_in(nc, "bias", [1, 6], F32)
    ident = dram_in(nc, "ident", [128, 128], BF16)
    oA = dram_out(nc, "oA", [TOK, 1152], BF16)
    oB = dram_out(nc, "oB", [TOK, 768], BF16)
    oC = dram_out(nc, "oC", [TOK, 1024], BF16)
    oF = dram_out(nc, "oF", [TOK, 18], F32)
    okm = dram_out(nc, "okm", [64, 48], F32)
    with ExitStack() as st:
        S = Sched(nc, st)
        EPSC = S.sb([128, 1], F32, "EPSC")
        S.op("dve", lambda e: e.memset(EPSC[:], EPS), writes=[EPSC])
        C256 = S.sb([128, 1], BF16, "C256")
        S.op("dve", lambda e: e.memset(C256[:], 1.0 / 256), writes=[C256])
        ID = S.sb([128, 128], BF16, "ID")
        S.dma("sp", ID[:], ident, writes=[ID])
        CS = S.sb([128, 2, NTT, 8], F32, "CS")
        GAINS = S.sb([128, 1920], F32, "GAINS")
        BIAS = S.sb([128, 6], F32, "BIAS")
        S.dma("sp", CS[:], cs, writes=[CS])
        S.dma("pool", GAINS[:], gains.partition_broadcast(128), writes=[GAINS])
        S.dma("pool", BIAS[:], bias.partition_broadcast(128), writes=[BIAS])
        PST = Ring([S.ps([128, 128], F32, "PST") for _ in range(3)])
        KMP = S.ps([64, 96], F32, "KMP")
        XG, RSTD = emit_xg(S, x, gnorm, ID, EPSC, PST)
        proj_loop(S, XG, RSTD, w, P1_TILES, {"A": oA, "B": oB, "C": oC, "F": oF}, EPSC, BIAS=BIAS, GAINS=GAINS, CS=CS, kmean=(KMP, C256))
        KMS = S.sb([64, 96], F32, "KMS")
        KM8 = S.sb([64, 48], F32, "KM8")
        S.op("act", lambda e: e.activation(out=KMS[:], in_=KMP[:], func=AF.Copy), reads=[KMP], writes=[KMS])
        kv = KMS[:].rearrange("p (h b t) -> p h b t", h=6, t=2)
        S.op("dve", lambda e: e.tensor_tensor(out=KM8[:].rearrange("p (h b) -> p h b", h=6), in0=kv[:, :, :, 0], in1=kv[:, :, :, 1], op=ALU.add), reads=[KMS], writes=[KM8])
        S.dma("sp", okm, KM8[:], reads=[KM8])
        S.finish()
    return nc


def rope_cs(pos):
    inv = (500000.0 ** (-np.arange(0, 16, 2, dtype=np.float32) / np.float32(16))).astype(np.float32)
    ang = pos.astype(np.float32)[:, None] * inv[None, :]
    return np.cos(ang).astype(np.float32), np.sin(ang).astype(np.float32)


def run_p1(xl, norm_g, w_in, b_f, b_gate, moba_qk_g, nsa_q_g, nsa_k_g, fox_qk_g):
    nc = get_nc("p1", build_p1)
    wr = np.ascontiguousarray(w_in[:, p1_col_perm()])
    gains = np.concatenate([np.tile(moba_qk_g[0], 6), np.tile(moba_qk_g[1], 6), np.tile(nsa_q_g, 4), nsa_k_g[1], nsa_k_g[2],
                            np.tile(fox_qk_g[0], 6), np.tile(fox_qk_g[1], 6)])[None, :].astype(np.float32)
    in_maps = []
    for c in range(NCORES):
        xs = xl[c * TOK:(c + 1) * TOK]
        cos, sin = rope_cs(np.arange(c * TOK, (c + 1) * TOK))
        cs = np.stack([cos.reshape(NTT, 128, 8).transpose(1, 0, 2), sin.reshape(NTT, 128, 8).transpose(1, 0, 2)], axis=1)
        in_maps.append({"x": np.ascontiguousarray(xs), "gnorm": np.ascontiguousarray(norm_g[None, :]), "w": wr,
                        "cs": np.ascontiguousarray(cs), "gains": gains, "bias": np.ascontiguousarray(b_f[None, :]), "ident": IDENT})
    res = run(nc, in_maps)
    out = {}
    for k in ["oA", "oB", "oC", "oF"]:
        out[k] = np.concatenate([r[k] for r in res], axis=0)
    out["kmT"] = np.ascontiguousarray(np.concatenate([r["okm"].reshape(64, 6, 8) for r in res], axis=2).reshape(64, 384))
    return out


def build_attn(jobs, tag):
    nc = bass.Bass("TRN2", target_bir_lowering=False)
    ident = dram_in(nc, "ident", [128, 128], BF16)
    ins = []
    for j, jb in enumerate(jobs):
        ins.append(dict(
            ka=dram_in(nc, f"ka{j}", [jb["Kc"], jb["NK"]], BF16),
            va=dram_in(nc, f"va{j}", [128, (jb["NK"] // 128) * jb["W"]], BF16),
            qa=dram_in(nc, f"qa{j}", [jb["Kc"], jb["nvar"] * jb["NQ"]], BF16),
            mk=dram_in(nc, f"mk{j}", [128, jb["nmask"] * 512], BF16),
            o=dram_out(nc, f"o{j}", [jb["W"], jb["NQ"]], F32)))
    mNK = max(jb["NK"] for jb in jobs)
    mVA = max((jb["NK"] // 128) * jb["W"] for jb in jobs)
    mQA = max(jb["nvar"] * jb["NQ"] for jb in jobs)
    mMK = max(jb["nmask"] for jb in jobs)
    nset = min(2, len(jobs))
    with ExitStack() as st:
        S = Sched(nc, st)
        ID = S.sb([128, 128], BF16, "ID")
        S.dma("sp", ID[:], ident, writes=[ID])
        sets = [dict(KA=S.sb([128, mNK], BF16, "KA"), VA=S.sb([128, mVA], BF16, "VA"), QA=S.sb([128, mQA], BF16, "QA"),
                     MK=S.sb([128, mMK * 512], BF16, "MK")) for _ in range(nset)]
        PSS = Ring([S.ps([128, 512], F32, "PSS") for _ in range(4)])
        LOOK = 3
        ACCB = [S.ps([128, 512], F32, "ACC") for _ in range(4)]
        acc_i = [0]
        PT = Ring([S.sb([128, 512], BF16, "PT") for _ in range(LOOK + 2)])
        OB = Ring([S.sb([128, 512], F32, "OB") for _ in range(3)])
        for j, jb in enumerate(jobs):
            sset = sets[j % nset]
            KA, VA, QA, MK = sset["KA"], sset["VA"], sset["QA"], sset["MK"]
            Kc, W, NQ = jb["Kc"], jb["W"], jb["NQ"]
            io = ins[j]
            S.dma("sp", KA[0:Kc, 0:jb["NK"]], io["ka"], writes=[KA])
            S.dma("sp", QA[0:Kc, 0:jb["nvar"] * NQ], io["qa"], writes=[QA])
            S.dma("sp", VA[:, 0:(jb["NK"] // 128) * W], io["va"], writes=[VA])
            S.dma("sp", MK[:, 0:jb["nmask"] * 512], io["mk"], writes=[MK])
            chunks = [(c0_, min(128, W - c0_)) for c0_ in range(0, W, 128)]
            pendq = []
            for lg, pairs in enumerate(jb["sched"]):
                npairs = len(pairs)
                if len(chunks) == 1:
                    accs = [ACCB[acc_i[0] % 4]]
                    acc_i[0] += 1
                else:
                    accs = ACCB[0:len(chunks)]

                def emit_pv(pt, kt, first, last, lg=lg, accs=accs):
                    for ci, (cc, wc) in enumerate(chunks):
                        acc = accs[ci]
                        S.op("pe", lambda e, pt=pt, kt=kt, first=first, last=last, W=W, VA=VA, acc=acc, cc=cc, wc=wc: e.matmul(
                            acc[0:wc, :], lhsT=VA[:, kt * W + cc:kt * W + cc + wc], rhs=pt[:], start=first, stop=last),
                            reads=[pt, VA], writes=[acc], signal=(ci == len(chunks) - 1))
                    if last:
                        for ci, (cc, wc) in enumerate(chunks):
                            acc = accs[ci]
                            ob = OB.next()
                            S.op("dve", lambda e, ob=ob, acc=acc, wc=wc: e.tensor_copy(out=ob[0:wc, :], in_=acc[0:wc, :]), reads=[acc], writes=[ob])
                            S.dma("pool", io["o"][cc:cc + wc, lg * 512:(lg + 1) * 512], ob[0:wc, :], reads=[ob])

                for pi, (kt, var, midx) in enumerate(pairs):
                    ps = PSS.next()
                    q0 = var * NQ + lg * 512
                    S.op("pe", lambda e, ps=ps, kt=kt, q0=q0, midx=midx, KA=KA, QA=QA, Kc=Kc: e.matmul(
                        ps[:], lhsT=KA[0:Kc, kt * 128:(kt + 1) * 128], rhs=QA[0:Kc, q0:q0 + 512], start=True, stop=(midx is None)),
                        reads=[KA, QA], writes=[ps], signal=(midx is None))
                    if midx is not None:
                        S.op("pe", lambda e, ps=ps, midx=midx, MK=MK: e.matmul(
                            ps[:], lhsT=ID[:], rhs=MK[:, midx * 512:(midx + 1) * 512], start=False, stop=True),
                            reads=[ID, MK], writes=[ps], signal=True)
                    pt = PT.next()
                    S.op("act", lambda e, pt=pt, ps=ps: e.activation(out=pt[:], in_=ps[:], func=AF.Exp, scale=SCALE), reads=[ps], writes=[pt])
                    pendq.append((emit_pv, (pt, kt, pi == 0, pi == npairs - 1)))
                    if len(pendq) > LOOK:
                        f_, a_ = pendq.pop(0)
                        f_(*a_)
            for f_, a_ in pendq:
                f_(*a_)
        S.finish()
    return nc


def build_p2():
    nc = bass.Bass("TRN2", target_bir_lowering=False)
    lf = dram_in(nc, "lf", [6, S_LEN], F32)
    flk = dram_in(nc, "flk", [128, 16, 128], BF16)
    flv = dram_in(nc, "flv", [128, 16, 128], BF16)
    w1 = dram_in(nc, "w1", [128, 2, 16, 256], F32)
    w2 = dram_in(nc, "w2", [128, 2, 2, 64], F32)
    pe = dram_in(nc, "pe", [128, 2, 16], F32)
    kgain = dram_in(nc, "kgain", [1, 64], F32)
    csc = dram_in(nc, "csc", [128, 2, 8], F32)
    qT = dram_in(nc, "qT", [64, 6 * TOK], BF16)
    kmT = dram_in(nc, "kmT", [64, 384], F32)
    gm = dram_in(nc, "gm", [128, 3, NTT, 64], F32)
    csplit = dram_out(nc, "csplit", [6, 6, S_LEN], BF16)
    kcmp = dram_out(nc, "kcmp", [128, 64], BF16)
    vcmp = dram_out(nc, "vcmp", [128, 64], BF16)
    mb = dram_out(nc, "mb", [TOK, 384], BF16)
    with ExitStack() as st:
        S = Sched(nc, st)
        EPSC = S.sb([128, 1], F32, "EPSC")
        S.op("dve", lambda e: e.memset(EPSC[:], EPS), writes=[EPSC])
        CH = 1024
        ONES = S.sb([6, CH], F32, "ONES")
        S.op("dve", lambda e: e.memset(ONES[:], 1.0), writes=[ONES])
        LFr = Ring([S.sb([6, CH], F32, "LF") for _ in range(2)])
        Cr = Ring([S.sb([6, CH], F32, "C8") for _ in range(2)])
        OCr = Ring([S.sb([6, 6, CH], BF16, "OC") for _ in range(2)])
        Hf = S.sb([6, CH], F32, "Hf")
        R1 = S.sb([6, CH], F32, "R1")
        prev = None
        for ci in range(S_LEN // CH):
            l = LFr.next(); c8 = Cr.next(); oc = OCr.next()
            S.dma("sp", l[:], lf[:, ci * CH:(ci + 1) * CH], writes=[l])
            S.op("dve", lambda e, l=l: e.tensor_scalar(out=l[:], in0=l[:], scalar1=8.0, scalar2=None, op0=ALU.mult), reads=[l], writes=[l])
            if prev is None:
                S.op("dve", lambda e, l=l, c8=c8: e.tensor_tensor_scan(out=c8[:], data0=ONES[:], data1=l[:], initial=0.0, op0=ALU.mult, op1=ALU.add),
                     reads=[ONES, l], writes=[c8])
            else:
                S.op("dve", lambda e, l=l, c8=c8, prev=prev: e.tensor_tensor_scan(out=c8[:], data0=ONES[:], data1=l[:], initial=prev[:, CH - 1:CH], op0=ALU.mult, op1=ALU.add),
                     reads=[ONES, l, prev], writes=[c8])
            prev = c8
            S.op("dve", lambda e, c8=c8, oc=oc: e.tensor_copy(out=oc[:, 0, :], in_=c8[:]), reads=[c8], writes=[oc])
            S.op("dve", lambda e, oc=oc: e.tensor_copy(out=Hf[:], in_=oc[:, 0, :]), reads=[oc], writes=[Hf])
            S.op("dve", lambda e, c8=c8: e.tensor_tensor(out=R1[:], in0=c8[:], in1=Hf[:], op=ALU.subtract), reads=[c8, Hf], writes=[R1])
            S.op("dve", lambda e, oc=oc: e.tensor_copy(out=oc[:, 1, :], in_=R1[:]), reads=[R1], writes=[oc])
            S.op("dve", lambda e, oc=oc: e.tensor_copy(out=Hf[:], in_=oc[:, 1, :]), reads=[oc], writes=[Hf])
            S.op("dve", lambda e: e.tensor_tensor(out=R1[:], in0=R1[:], in1=Hf[:], op=ALU.subtract), reads=[R1, Hf], writes=[R1])
            S.op("dve", lambda e, oc=oc: e.tensor_copy(out=oc[:, 2, :], in_=R1[:]), reads=[R1], writes=[oc])
            S.op("dve", lambda e, oc=oc: e.tensor_scalar(out=oc[:, 3:6, :], in0=oc[:, 0:3, :], scalar1=-1.0, scalar2=None, op0=ALU.mult), reads=[oc], writes=[oc])
            S.dma("sp", csplit[:, :, ci * CH:(ci + 1) * CH], oc[:], reads=[oc])
        W1F = S.sb([128, 2, 16, 256], F32, "W1F"); W1B = S.sb([128, 2, 16, 256], BF16, "W1B")
        W2F = S.sb([128, 2, 2, 64], F32, "W2F"); W2B = S.sb([128, 2, 2, 64], BF16, "W2B")
        PEF = S.sb([128, 2, 16], F32, "PEF"); PEB = S.sb([128, 2, 16], BF16, "PEB")
        FL = [S.sb([128, 16, 128], BF16, "FLK"), S.sb([128, 16, 128], BF16, "FLV")]
        KG = S.sb([128, 64], F32, "KG"); CSC = S.sb([128, 2, 8], F32, "CSC")
        S.dma("sp", W1F[:], w1, writes=[W1F]); S.dma("sp", W2F[:], w2, writes=[W2F]); S.dma("sp", PEF[:], pe, writes=[PEF])
        S.dma("sp", FL[0][:], flk, writes=[FL[0]]); S.dma("sp", FL[1][:], flv, writes=[FL[1]])
        S.dma("sp", KG[:], kgain.partition_broadcast(128), writes=[KG]); S.dma("sp", CSC[:], csc, writes=[CSC])
        S.op("dve", lambda e: e.tensor_copy(out=W1B[:], in_=W1F[:]), reads=[W1F], writes=[W1B])
        S.op("dve", lambda e: e.tensor_copy(out=W2B[:], in_=W2F[:]), reads=[W2F], writes=[W2B])
        S.op("dve", lambda e: e.tensor_copy(out=PEB[:], in_=PEF[:]), reads=[PEF], writes=[PEB])
        B1 = S.sb([128, 4], F32, "B1")
        HS = S.sb([128, 2, 128], BF16, "HS")
        PB = S.ps([128, 8], F32, "PB")
        PH = Ring([S.ps([128, 128], F32, "PH") for _ in range(2)])
        PO = S.ps([128, 64], F32, "PO")
        YC = S.sb([128, 64], F32, "YC"); SQC = S.sb([128, 64], F32, "SQC"); RSC = S.sb([128, 1], F32, "RSC")
        RTC = S.sb([128, 4, 8], F32, "RTC"); OKC = S.sb([128, 64], BF16, "OKC"); OVC = S.sb([128, 64], BF16, "OVC")
        for kv in range(2):
            for half in range(2):
                idx = kv * 2 + half
                for j in range(16):
                    S.op("pe", lambda e, kv=kv, half=half, j=j, idx=idx: e.matmul(PB[:, idx:idx + 1], lhsT=W1B[:, kv, j, half * 128:(half + 1) * 128], rhs=PEB[:, kv, j:j + 1],
                                                                                   start=(j == 0), stop=(j == 15)), reads=[W1B, PEB], writes=[PB], signal=(j == 15))
                S.op("dve", lambda e, idx=idx: e.tensor_copy(out=B1[:, idx:idx + 1], in_=PB[:, idx:idx + 1]), reads=[PB], writes=[B1])
                ph = PH.next()
                for j in range(16):
                    S.op("pe", lambda e, ph=ph, kv=kv, half=half, j=j: e.matmul(ph[:], lhsT=W1B[:, kv, j, half * 128:(half + 1) * 128], rhs=FL[kv][:, j, :],
                                                                                start=(j == 0), stop=(j == 15)), reads=[W1B, FL[kv]], writes=[ph], signal=(j == 15))
                S.op("act", lambda e, ph=ph, half=half, idx=idx: e.activation(out=HS[:, half, :], in_=ph[:], func=AF.Silu, bias=B1[:, idx:idx + 1]),
                     reads=[ph, B1], writes=[HS])
            for half in range(2):
                S.op("pe", lambda e, kv=kv, half=half: e.matmul(PO[:], lhsT=HS[:, half, :], rhs=W2B[:, kv, half, :], start=(half == 0), stop=(half == 1)),
                     reads=[HS, W2B], writes=[PO], signal=(half == 1))
            if kv == 1:
                S.op("act", lambda e: e.activation(out=OVC[:], in_=PO[:], func=AF.Copy), reads=[PO], writes=[OVC])
                S.dma("sp", vcmp, OVC[:], reads=[OVC])
            else:
                S.op("act", lambda e: e.activation(out=YC[:], in_=PO[:], func=AF.Copy), reads=[PO], writes=[YC])
                S.op("dve", lambda e: e.tensor_tensor(out=SQC[:], in0=YC[:], in1=YC[:], op=ALU.mult), reads=[YC], writes=[SQC])
                S.op("dve", lambda e: e.tensor_reduce(out=RSC[:], in_=SQC[:], axis=AX.X, op=ALU.add), reads=[SQC], writes=[RSC])
                S.op("act", lambda e: e.activation(out=RSC[:], in_=RSC[:], func=AF.Sqrt, bias=EPSC[:], scale=1.0 / 64), reads=[RSC, EPSC], writes=[RSC])
                S.op("dve", lambda e: e.reciprocal(out=RSC[:], in_=RSC[:]), reads=[RSC], writes=[RSC])
                S.op("dve", lambda e: e.scalar_tensor_tensor(out=YC[:], in0=YC[:], scalar=RSC[:, 0:1], in1=KG[:], op0=ALU.mult, op1=ALU.mult), reads=[YC, RSC, KG], writes=[YC])
                S.op("act", lambda e: e.activation(out=OKC[:], in_=YC[:], func=AF.Copy), reads=[YC], writes=[OKC])
                S.op("dve", lambda e: e.tensor_tensor(out=RTC[:, 0, :], in0=YC[:, 0:8], in1=CSC[:, 0, :], op=ALU.mult), reads=[YC, CSC], writes=[RTC])
                S.op("dve", lambda e: e.tensor_tensor(out=RTC[:, 1, :], in0=YC[:, 8:16], in1=CSC[:, 1, :], op=ALU.mult), reads=[YC, CSC], writes=[RTC])
                S.op("dve", lambda e: e.tensor_tensor(out=RTC[:, 2, :], in0=YC[:, 8:16], in1=CSC[:, 0, :], op=ALU.mult), reads=[YC, CSC], writes=[RTC])
                S.op("dve", lambda e: e.tensor_tensor(out=RTC[:, 3, :], in0=YC[:, 0:8], in1=CSC[:, 1, :], op=ALU.mult), reads=[YC, CSC], writes=[RTC])
                S.op("dve", lambda e: e.tensor_tensor(out=OKC[:, 0:8], in0=RTC[:, 0, :], in1=RTC[:, 1, :], op=ALU.subtract), reads=[RTC], writes=[OKC])
                S.op("dve", lambda e: e.tensor_tensor(out=OKC[:, 8:16], in0=RTC[:, 2, :], in1=RTC[:, 3, :], op=ALU.add), reads=[RTC], writes=[OKC])
                S.dma("sp", kcmp, OKC[:], reads=[OKC])
        KMF = S.sb([64, 384], F32, "KMF")
        KMB = S.sb([64, 384], BF16, "KMB")
        S.dma("sp", KMF[:], kmT, writes=[KMF])
        S.op("act", lambda e: e.activation(out=KMB[:], in_=KMF[:], func=AF.Copy), reads=[KMF], writes=[KMB])
        QT = S.sb([64, 6 * TOK], BF16, "QT")
        S.dma("sp", QT[:], qT, writes=[QT])
        GM = S.sb([128, 3, NTT, 64], F32, "GM")
        S.dma("sp", GM[:], gm, writes=[GM])
        PG = Ring([S.ps([128, 384], F32, "PG") for _ in range(2)])
        GS = Ring([S.sb([128, 6, 64], F32, "GS") for _ in range(2)])
        M8 = Ring([S.sb([128, 6, 8], F32, "M8") for _ in range(2)])
        T1 = Ring([S.sb([128, 6, 64], F32, "T1") for _ in range(2)])
        MBO = Ring([S.sb([128, 384], BF16, "MBO") for _ in range(2)])
        for lt in range(NTT):
            mbo = MBO.next(); pg = PG.next(); gs = GS.next(); m8 = M8.next(); t1 = T1.next()
            for h in range(6):
                S.op("pe", lambda e, pg=pg, h=h, lt=lt: e.matmul(pg[:, h * 64:(h + 1) * 64], lhsT=QT[:, h * TOK + lt * 128:h * TOK + (lt + 1) * 128], rhs=KMB[:, h * 64:(h + 1) * 64], start=True, stop=True),
                     reads=[QT, KMB], writes=[pg], signal=(h == 5))
            bc = lambda k, lt=lt: GM[:, k, lt, :].unsqueeze(1).to_broadcast([128, 6, 64])
            S.op("dve", lambda e, pg=pg, gs=gs, bc=bc: e.tensor_tensor(out=gs[:], in0=pg[:].rearrange("p (h b) -> p h b", h=6), in1=bc(0), op=ALU.add), reads=[pg, GM], writes=[gs])
            for h in range(6):
                S.op("dve", lambda e, gs=gs, m8=m8, h=h: e.max(out=m8[:, h, :], in_=gs[:, h, :]), reads=[gs], writes=[m8])
            S.op("dve", lambda e, gs=gs, m8=m8, t1=t1: e.tensor_tensor(out=t1[:], in0=gs[:], in1=m8[:, :, 2:3].to_broadcast([128, 6, 64]), op=ALU.is_ge), reads=[gs, m8], writes=[t1])
            S.op("dve", lambda e, t1=t1, bc=bc: e.tensor_tensor(out=t1[:], in0=t1[:], in1=bc(1), op=ALU.mult), reads=[t1, GM], writes=[t1])
            S.op("dve", lambda e, t1=t1, bc=bc: e.tensor_tensor(out=t1[:], in0=t1[:], in1=bc(2), op=ALU.add), reads=[t1, GM], writes=[t1])
            S.op("dve", lambda e, t1=t1, mbo=mbo: e.tensor_scalar(out=mbo[:], in0=t1[:].rearrange("p h b -> p (h b)"), scalar1=-NEGM, scalar2=NEGM, op0=ALU.mult, op1=ALU.add),
                 reads=[t1], writes=[mbo])
            S.dma("sp", mb[lt * 128:(lt + 1) * 128, :], mbo[:], reads=[mbo])
        S.finish()
    return nc


def build_p4():
    nc = bass.Bass("TRN2", target_bir_lowering=False)
    oc = dram_in(nc, "oc", [4, TOK, 257], F32)
    cm = dram_in(nc, "cm", [128, 3, NTT, 256], F32)
    mbs = dram_out(nc, "mbs", [TOK, 256], BF16)
    with ExitStack() as st:
        S = Sched(nc, st)
        CM = S.sb([128, 3, NTT, 256], F32, "CM")
        S.dma("sp", CM[:], cm, writes=[CM])
        OC = Ring([S.sb([128, 4, 257], F32, "OC") for _ in range(2)])
        RD = Ring([S.sb([128, 4], F32, "RD") for _ in range(2)])
        IMP = Ring([S.sb([128, 256], F32, "IMP") for _ in range(2)])
        RR = Ring([S.sb([128, 256], F32, "RR") for _ in range(2)])
        M8 = Ring([S.sb([128, 16], F32, "M8") for _ in range(2)])
        MBO = Ring([S.sb([128, 256], BF16, "MBO") for _ in range(2)])
        for lt in range(NTT):
            o = OC.next(); rd = RD.next(); imp = IMP.next(); rr = RR.next(); m8 = M8.next(); mbo = MBO.next()
            S.dma("sp", o[:], oc[:, lt * 128:(lt + 1) * 128, :].rearrange("h q w -> q h w"), writes=[o])
            S.op("dve", lambda e, o=o, rd=rd: e.tensor_scalar(out=rd[:], in0=o[:, :, 0], scalar1=1e-30, scalar2=None, op0=ALU.max), reads=[o], writes=[rd])
            S.op("dve", lambda e, rd=rd: e.reciprocal(out=rd[:], in_=rd[:]), reads=[rd], writes=[rd])
            S.op("dve", lambda e, o=o, rd=rd, imp=imp: e.tensor_scalar(out=imp[:], in0=o[:, 0, 1:257], scalar1=rd[:, 0:1], scalar2=None, op0=ALU.mult), reads=[o, rd], writes=[imp])
            for h in range(1, 4):
                S.op("dve", lambda e, o=o, rd=rd, imp=imp, h=h: e.scalar_tensor_tensor(out=imp[:], in0=o[:, h, 1:257], scalar=rd[:, h:h + 1], in1=imp[:], op0=ALU.mult, op1=ALU.add),
                     reads=[o, rd, imp], writes=[imp])
            S.op("dve", lambda e, imp=imp, lt=lt: e.tensor_tensor(out=imp[:], in0=imp[:], in1=CM[:, 0, lt, :], op=ALU.add), reads=[imp, CM], writes=[imp])
            S.op("dve", lambda e, imp=imp, m8=m8: e.max(out=m8[:, 0:8], in_=imp[:]), reads=[imp], writes=[m8])
            S.op("dve", lambda e, imp=imp, m8=m8, rr=rr: e.match_replace(out=rr[:], in_to_replace=m8[:, 0:8], in_values=imp[:], imm_value=-1e30), reads=[imp, m8], writes=[rr])
            S.op("dve", lambda e, rr=rr, m8=m8: e.max(out=m8[:, 8:16], in_=rr[:]), reads=[rr], writes=[m8])
            S.op("dve", lambda e, imp=imp, m8=m8, rr=rr, lt=lt: e.scalar_tensor_tensor(out=rr[:], in0=imp[:], scalar=m8[:, 12:13], in1=CM[:, 1, lt, :], op0=ALU.is_ge, op1=ALU.mult),
                 reads=[imp, m8, CM], writes=[rr])
            S.op("dve", lambda e, rr=rr, lt=lt: e.tensor_tensor(out=rr[:], in0=rr[:], in1=CM[:, 2, lt, :], op=ALU.add), reads=[rr, CM], writes=[rr])
            S.op("dve", lambda e, rr=rr, mbo=mbo: e.tensor_scalar(out=mbo[:], in0=rr[:], scalar1=-NEGM, scalar2=NEGM, op0=ALU.mult, op1=ALU.add), reads=[rr], writes=[mbo])
            S.dma("sp", mbs[lt * 128:(lt + 1) * 128, :], mbo[:], reads=[mbo])
        S.finish()
    return nc


def build_p5():
    nc = bass.Bass("TRN2", target_bir_lowering=False)
    xs = dram_in(nc, "xs", [TOK, DM], F32)
    gnorm = dram_in(nc, "gnorm", [1, DM], F32)
    wz = dram_in(nc, "wz", [DM, P5_NCOLS], F32)
    bgate = dram_in(nc, "bgate", [1, 3072], F32)
    ng = dram_in(nc, "ng", [TOK, 18], F32)
    om = dram_in(nc, "om", [TOK, 6 * 65], F32)
    ofx = dram_in(nc, "ofx", [TOK, 6 * 65], F32)
    on = dram_in(nc, "on", [TOK, 3, 4 * 65], F32)
    wup = dram_in(nc, "wup", [DM, DM], F32)
    wout = dram_in(nc, "wout", [DM, DM], F32)
    ident = dram_in(nc, "ident", [128, 128], BF16)
    out = dram_out(nc, "out", [TOK, DM], F32)
    zz = nc.dram_tensor("zz_scr", [TOK, 1024], F32).ap()
    gg = nc.dram_tensor("gg_scr", [TOK, 3072], F32).ap()
    with ExitStack() as st:
        S = Sched(nc, st)
        ID = S.sb([128, 128], BF16, "ID")
        S.dma("sp", ID[:], ident, writes=[ID])
        EPSC = S.sb([128, 1], F32, "EPSC")
        S.op("dve", lambda e: e.memset(EPSC[:], EPS), writes=[EPSC])
        PST = Ring([S.ps([128, 128], F32, "PST") for _ in range(3)])
        with ExitStack() as stA:
            SA = S
            old_stack = S.stack
            S.stack = stA
            BIAS = S.sb([128, 3072], F32, "BIAS")
            S.dma("pool", BIAS[:], bgate.partition_broadcast(128), writes=[BIAS])
            XG, RSTD = emit_xg(S, xs, gnorm, ID, EPSC, PST)
            proj_loop(S, XG, RSTD, wz, P5_TILES, {"D": zz, "E": gg}, EPSC, BIAS=BIAS, nwbuf=2)
            S.stack = old_stack
            S.barrier()
            S.finish_part()
        WUP = S.sb([128, 8, DM], BF16, "WUP"); WOUT = S.sb([128, 8, DM], BF16, "WOUT")
        WF = Ring([S.sb([128, 2, DM], F32, "WF") for _ in range(2)])
        for wi, (src, dst) in enumerate(((wup, WUP), (wout, WOUT))):
            for q4 in range(4):
                wf = WF.next()
                S.dma("sp", wf[:], src[q4 * 256:(q4 + 1) * 256, :].rearrange("(c p) n -> p c n", p=128), writes=[wf])
                S.op("pool", lambda e, wf=wf, dst=dst, q4=q4: e.tensor_copy(out=dst[:, 2 * q4:2 * q4 + 2, :], in_=wf[:]), reads=[wf], writes=[dst])
        X = Ring([S.sb([128, DM], F32, "X") for _ in range(2)])
        Z = Ring([S.sb([128, DM], F32, "Z") for _ in range(2)])
        G = Ring([S.sb([128, 3072], F32, "G") for _ in range(2)])
        NG = Ring([S.sb([128, 18], F32, "NG") for _ in range(2)])
        OM = Ring([S.sb([128, 6, 65], F32, "OM") for _ in range(2)])
        OFX = Ring([S.sb([128, 6, 65], F32, "OFX") for _ in range(2)])
        ON = Ring([S.sb([128, 3, 4, 65], F32, "ON") for _ in range(2)])
        RD = Ring([S.sb([128, 32], F32, "RD") for _ in range(2)])
        T32 = Ring([S.sb([128, DM], F32, "T32") for _ in range(2)])
        TN = Ring([S.sb([128, 256], F32, "TN") for _ in range(2)])
        TB = Ring([S.sb([128, DM], BF16, "TB") for _ in range(2)])
        TT = Ring([S.sb([128, 8, 128], BF16, "TT") for _ in range(2)])
        MG = Ring([S.sb([128, DM], F32, "MG") for _ in range(2)])
        TM = Ring([S.sb([128, 512], F32, "TM") for _ in range(2)])
        MGB = Ring([S.sb([128, DM], BF16, "MGB") for _ in range(2)])
        MT = Ring([S.sb([128, 8, 128], BF16, "MT") for _ in range(2)])
        OUT = Ring([S.sb([128, DM], F32, "OUT") for _ in range(2)])
        PSY = Ring([S.ps([128, 512], F32, "PSY") for _ in range(3)])
        PSO = Ring([S.ps([128, 512], F32, "PSO") for _ in range(2)])
        ctxs = [dict() for _ in range(NTT)]

        def ldA(tt):
            c = ctxs[tt]
            rows_ = slice(tt * 128, (tt + 1) * 128)
            z = Z.next(); ngt = NG.next(); o_m = OM.next(); o_f = OFX.next(); o_n = ON.next()
            S.dma("sp", z[:], zz[rows_, :], writes=[z])
            S.dma("sp", ngt[:], ng[rows_, :], writes=[ngt])
            S.dma("sp", o_m[:], om[rows_, :].rearrange("q (h w) -> q h w", w=65), writes=[o_m])
            S.dma("sp", o_f[:], ofx[rows_, :].rearrange("q (h w) -> q h w", w=65), writes=[o_f])
            S.dma("sp", o_n[:], on[rows_, :, :].rearrange("q j (h w) -> q j h w", w=65), writes=[o_n])
            c.update(z=z, ngt=ngt, o_m=o_m, o_f=o_f, o_n=o_n)

        def ldC(tt):
            g = G.next()
            S.dma("sp", g[:], gg[tt * 128:(tt + 1) * 128, :], writes=[g])
            ctxs[tt]["g"] = g

        def ldE(tt):
            x = X.next()
            S.dma("sp", x[:], xs[tt * 128:(tt + 1) * 128, :], writes=[x])
            ctxs[tt]["x"] = x

        def stA(tt):
            c = ctxs[tt]
            z, ngt, o_m, o_f, o_n = c["z"], c["ngt"], c["o_m"], c["o_f"], c["o_n"]
            rd = RD.next(); t32 = T32.next(); tn = TN.next(); tb = TB.next()
            S.op("dve", lambda e, rd=rd, o_m=o_m: e.reciprocal(out=rd[:, 0:6], in_=o_m[:, :, 64]), reads=[o_m], writes=[rd])
            S.op("dve", lambda e, rd=rd, o_f=o_f: e.reciprocal(out=rd[:, 6:12], in_=o_f[:, :, 64]), reads=[o_f], writes=[rd])
            S.op("dve", lambda e, rd=rd, o_m=o_m, t32=t32: e.tensor_tensor(out=t32[:, 0:384].rearrange("p (h d) -> p h d", d=64), in0=o_m[:, :, 0:64],
                                                                         in1=rd[:, 0:6].unsqueeze(2).to_broadcast([128, 6, 64]), op=ALU.mult), reads=[o_m, rd], writes=[t32])
            S.op("dve", lambda e, rd=rd, o_f=o_f, t32=t32: e.tensor_tensor(out=t32[:, 640:1024].rearrange("p (h d) -> p h d", d=64), in0=o_f[:, :, 0:64],
                                                                         in1=rd[:, 6:12].unsqueeze(2).to_broadcast([128, 6, 64]), op=ALU.mult), reads=[o_f, rd], writes=[t32])
            S.op("dve", lambda e, rd=rd, o_n=o_n: e.tensor_scalar(out=rd[:, 12:24].rearrange("p (j h) -> p j h", h=4), in0=o_n[:, :, :, 64], scalar1=1e-30, scalar2=None, op0=ALU.max),
                 reads=[o_n], writes=[rd])
            S.op("dve", lambda e, rd=rd: e.reciprocal(out=rd[:, 12:24], in_=rd[:, 12:24]), reads=[rd], writes=[rd])
            S.op("dve", lambda e, rd=rd, ngt=ngt: e.tensor_tensor(out=rd[:, 12:24].rearrange("p (j h) -> p j h", h=4), in0=rd[:, 12:24].rearrange("p (j h) -> p j h", h=4),
                                                                  in1=ngt[:, 0:12].rearrange("p (h j) -> p j h", j=3), op=ALU.mult), reads=[rd, ngt], writes=[rd])
            for j in range(3):
                dst = t32 if j == 0 else tn
                dv = (t32[:, 384:640] if j == 0 else tn[:, 0:256]).rearrange("p (h d) -> p h d", d=64)
                S.op("dve", lambda e, rd=rd, o_n=o_n, dv=dv, j=j: e.tensor_tensor(out=dv, in0=o_n[:, j, :, 0:64],
                                                                                  in1=rd[:, 12 + 4 * j:16 + 4 * j].unsqueeze(2).to_broadcast([128, 4, 64]), op=ALU.mult),
                     reads=[o_n, rd], writes=[dst])
                if j > 0:
                    S.op("dve", lambda e, t32=t32, tn=tn: e.tensor_tensor(out=t32[:, 384:640], in0=t32[:, 384:640], in1=tn[:, 0:256], op=ALU.add), reads=[t32, tn], writes=[t32])
            S.op("pool", lambda e, t32=t32, z=z, tb=tb: e.tensor_tensor(out=tb[:], in0=t32[:], in1=z[:], op=ALU.mult), reads=[t32, z], writes=[tb])
            c["tb"] = tb

        def transp(src, dst):
            for cc in range(8):
                pt = PST.next()
                S.op("pe", lambda e, pt=pt, src=src, cc=cc: e.matmul(pt[:], lhsT=src[:, cc * 128:(cc + 1) * 128], rhs=ID[:], start=True, stop=True), reads=[src, ID], writes=[pt])
                S.op("act", lambda e, pt=pt, dst=dst, cc=cc: e.activation(out=dst[:, cc, :], in_=pt[:], func=AF.Copy), reads=[pt], writes=[dst])

        def stB(tt):
            c = ctxs[tt]
            ttt = TT.next()
            transp(c["tb"], ttt)
            c["ttt"] = ttt

        def stC(tt):
            c = ctxs[tt]
            ttt, g = c["ttt"], c["g"]
            mg = MG.next(); mgb = MGB.next()
            for ct in range(2):
                cols = slice(ct * 512, (ct + 1) * 512)
                for bb, chs in enumerate(((0, 1, 2), (3, 4), (5, 6, 7))):
                    py = PSY.next()
                    for k, ch in enumerate(chs):
                        S.op("pe", lambda e, py=py, ttt=ttt, ch=ch, cols=cols, k=k, n=len(chs): e.matmul(py[:], lhsT=ttt[:, ch, :], rhs=WUP[:, ch, cols], start=(k == 0), stop=(k == n - 1)),
                             reads=[ttt, WUP], writes=[py], signal=(k == len(chs) - 1))
                    gsl = slice(bb * 1024 + ct * 512, bb * 1024 + (ct + 1) * 512)
                    if bb == 0:
                        S.op("dve", lambda e, py=py, g=g, mg=mg, cols=cols, gsl=gsl: e.tensor_tensor(out=mg[:, cols], in0=py[:], in1=g[:, gsl], op=ALU.mult), reads=[py, g], writes=[mg])
                    else:
                        tm = TM.next()
                        S.op("dve", lambda e, py=py, g=g, tm=tm, gsl=gsl: e.tensor_tensor(out=tm[:], in0=py[:], in1=g[:, gsl], op=ALU.mult), reads=[py, g], writes=[tm])
                        S.op("pool", lambda e, mg=mg, tm=tm, cols=cols: e.tensor_tensor(out=mg[:, cols], in0=mg[:, cols], in1=tm[:], op=ALU.add), reads=[mg, tm], writes=[mg])
            S.op("pool", lambda e, mg=mg, mgb=mgb: e.tensor_copy(out=mgb[:], in_=mg[:]), reads=[mg], writes=[mgb])
            c["mgb"] = mgb

        def stD(tt):
            c = ctxs[tt]
            mt = MT.next()
            transp(c["mgb"], mt)
            c["mt"] = mt

        def stE(tt):
            c = ctxs[tt]
            mt, x = c["mt"], c["x"]
            ot = OUT.next()
            for ct in range(2):
                cols = slice(ct * 512, (ct + 1) * 512)
                po = PSO.next()
                for cc in range(8):
                    S.op("pe", lambda e, po=po, mt=mt, cc=cc, cols=cols: e.matmul(po[:], lhsT=mt[:, cc, :], rhs=WOUT[:, cc, cols], start=(cc == 0), stop=(cc == 7)),
                         reads=[mt, WOUT], writes=[po], signal=(cc == 7))
                S.op("dve", lambda e, po=po, x=x, ot=ot, cols=cols: e.tensor_tensor(out=ot[:, cols], in0=po[:], in1=x[:, cols], op=ALU.add), reads=[po, x], writes=[ot])
            S.dma("sp", out[tt * 128:(tt + 1) * 128, :], ot[:], reads=[ot])

        ldA(0)
        for s_ in range(NTT + 4):
            if s_ + 1 < NTT:
                ldA(s_ + 1)
            if 0 <= s_ - 1 < NTT:
                ldC(s_ - 1)
            if 0 <= s_ - 3 < NTT:
                ldE(s_ - 3)
            for fn_, tt_ in ((stA, s_), (stB, s_ - 1), (stC, s_ - 2), (stD, s_ - 3), (stE, s_ - 4)):
                if 0 <= tt_ < NTT:
                    fn_(tt_)
        S.finish()
    return nc


def bf(a):
    return np.ascontiguousarray(a).astype(NPBF) if a.dtype != NPBF else np.ascontiguousarray(a)


def mask_tile(fn):
    k = np.arange(128)[:, None]
    q = np.arange(512)[None, :]
    return np.where(fn(k, q), 0.0, NEGM).astype(np.float32)


def pack_masks(tiles):
    return bf(np.concatenate(tiles, axis=1))


M_CAUSAL = [mask_tile(lambda k, q, r=r: 128 * r + k <= q) for r in range(4)]
M_ZERO = np.zeros((128, 512), np.float32)
M_FULL = np.full((128, 512), NEGM, np.float32)
MASKS_PAR = [pack_masks(M_CAUSAL + [M_FULL] * 4), pack_masks([M_ZERO] * 4 + M_CAUSAL)]
MASKS_WIN = pack_masks([mask_tile(lambda k, q, r=r: (q - (128 * r + k) >= 0) & (q - (128 * r + k) < 512)) for r in range(-4, 4)])
MASKS_CMP = [pack_masks([mask_tile(lambda k, q, r=r0 - par: 16 * k + 31 + 512 * r <= q) for r0 in range(-4, 1)]) for par in range(2)]
IDENT = np.eye(128, dtype=np.float32).astype(NPBF)


def dense_sched(nvar):
    sched = []
    for i in range(16):
        sched.append([(kt, (kt // 32) if nvar == 4 else 0, (kt - 8 * i) if kt >= 8 * i else None) for kt in range(8 * i + 8)])
    return sched


def win_sched():
    return [[(8 * i + j, 0, j) for j in range(8)] for i in range(16)]


def cmp_sched():
    sched = []
    for i in range(16):
        prs = []
        for j in range(8):
            r0 = 4 * j - 2 * i
            if r0 >= 2:
                continue
            prs.append((j, 0, None if r0 <= -5 else r0 + 4))
        sched.append(prs)
    return sched


def va_pack(v, extra=None):
    nk = v.shape[0]
    cols = [v.astype(NPBF), np.ones((nk, 1), NPBF)]
    if extra is not None:
        cols.append(extra.astype(NPBF))
    va = np.concatenate(cols, axis=1)
    W = va.shape[1]
    return np.ascontiguousarray(va.reshape(nk // 128, 128, W).transpose(1, 0, 2).reshape(128, (nk // 128) * W))


def par_qidx(par):
    return np.concatenate([np.arange(1024 * i + 512 * par, 1024 * i + 512 * par + 512) for i in range(16)])


_NC_CACHE = {}


def get_nc(key, fn):
    if key not in _NC_CACHE:
        _NC_CACHE[key] = fn()
    return _NC_CACHE[key]


def run(nc, in_maps):
    res = run_bass_kernel_spmd(nc, in_maps, core_ids=list(range(NCORES)))
    return res.results


def layer_forward(xl, p):
    S = S_LEN
    o1 = run_p1(xl, p["norm_g"], p["w_in"], p["b_f"], p["b_gate"], p["moba_qk_g"], p["nsa_q_g"], p["nsa_k_g"], p["fox_qk_g"])
    oA, oB, oC, oF = (o1[k] for k in ("oA", "oB", "oC", "oF"))
    mq, mk, nq, ksl, kw = oA[:, 0:384], oA[:, 384:768], oA[:, 768:1024], oA[:, 1024:1088], oA[:, 1088:1152]
    fq, fk = oB[:, 0:384], oB[:, 384:768]
    mv, fv, kc, vc, vsl, vw = oC[:, 0:384], oC[:, 384:768], oC[:, 768:832], oC[:, 832:896], oC[:, 896:960], oC[:, 960:1024]
    lf = np.ascontiguousarray(oF[:, 12:18].T)
    gidx = (np.arange(1023) * 16)[:, None] + np.arange(32)[None, :]

    def flat_of(t):
        blocks = np.zeros((1024, 32, 64), NPBF)
        blocks[:1023] = t[gidx]
        return blocks.reshape(1024, 2048).T.reshape(16, 128, 1024)
    flk_all, flv_all = flat_of(kc), flat_of(vc)
    w1 = np.ascontiguousarray(p["cmp_w1"].reshape(2, 16, 128, 256).transpose(2, 0, 1, 3))
    w2 = np.ascontiguousarray(p["cmp_w2"].reshape(2, 2, 128, 64).transpose(2, 0, 1, 3))
    pe = np.ascontiguousarray(p["cmp_pe"].reshape(2, 16, 128).transpose(2, 0, 1))
    kgain = np.ascontiguousarray(p["nsa_k_g"][0][None, :])
    in_maps = []
    for c in range(NCORES):
        cosc, sinc = rope_cs(np.arange(128 * c, 128 * c + 128) * 16 + 31)
        qs = mq[c * TOK:(c + 1) * TOK]
        qT = np.ascontiguousarray(qs.reshape(TOK, 6, 64).transpose(2, 1, 0).reshape(64, 6 * TOK))
        gm = np.zeros((128, 3, NTT, 64), np.float32)
        for lt in range(NTT):
            cur = (16 * c + lt) // 2
            gm[:, 0, lt, cur:] = -1e30
            gm[:, 1, lt, :cur] = 1.0
            gm[:, 2, lt, cur] = 1.0
        in_maps.append({"lf": lf, "flk": np.ascontiguousarray(flk_all[:, :, 128 * c:128 * c + 128].transpose(1, 0, 2)),
                        "flv": np.ascontiguousarray(flv_all[:, :, 128 * c:128 * c + 128].transpose(1, 0, 2)),
                        "w1": w1, "w2": w2, "pe": pe, "kgain": kgain, "csc": np.ascontiguousarray(np.stack([cosc, sinc], axis=1)),
                        "qT": qT, "kmT": o1["kmT"], "gm": gm})
    r2 = run(get_nc("p2", build_p2), in_maps)
    csplit = r2[0]["csplit"]
    kcmp = np.concatenate([r["kcmp"] for r in r2], axis=0)
    vcmp = np.concatenate([r["vcmp"] for r in r2], axis=0)
    mb = np.concatenate([r["mb"] for r in r2], axis=0)
    keys = np.arange(S)
    E_moba = (keys[None, :] // 256 == np.arange(64)[:, None]).astype(NPBF)
    E_slc = ((keys[None, :] // 64) % 64 == np.arange(64)[:, None]).astype(NPBF)
    ones3 = np.ones((3, S), NPBF)

    def dense_job_inputs(branch, h, par):
        qi = par_qidx(par)
        KA = np.zeros((128, S), NPBF)
        QA = np.zeros((128, 8192), NPBF)
        if branch == "fox":
            KA[0:64] = fk[:, h * 64:(h + 1) * 64].T
            KA[64:67] = csplit[h, 3:6]
            KA[67:70] = ones3
            QA[0:64] = fq[qi, h * 64:(h + 1) * 64].T
            QA[64:67] = ones3[:, qi]
            QA[67:70] = csplit[h, 0:3][:, qi]
            VA = va_pack(fv[:, h * 64:(h + 1) * 64])
        else:
            KA[0:64] = mk[:, h * 64:(h + 1) * 64].T
            KA[64:128] = E_moba
            QA[0:64] = mq[qi, h * 64:(h + 1) * 64].T
            QA[64:128] = mb[qi, h * 64:(h + 1) * 64].T
            VA = va_pack(mv[:, h * 64:(h + 1) * 64])
        return KA, VA, QA, MASKS_PAR[par]
    djobs = [(br, h, par) for br in ("fox", "moba") for h in range(6) for par in range(2)]
    overlap = np.zeros((1024, 256), np.float32)
    for m in range(256):
        lo_, hi_ = max(4 * m - 1, 0), min(4 * m + 3, 1022)
        overlap[lo_:hi_ + 1, m] = 1.0
    kcmpT = np.ascontiguousarray(kcmp.T)
    va_cmp = va_pack(vcmp, overlap)
    kwT = np.ascontiguousarray(kw.T)
    va_win = va_pack(vw)
    spec_d = dict(Kc=128, NK=S, W=65, nvar=1, NQ=8192, nmask=8, sched=dense_sched(1))
    jobs3 = [spec_d, spec_d, spec_d, dict(Kc=64, NK=S, W=65, nvar=1, NQ=8192, nmask=8, sched=win_sched()),
             dict(Kc=64, NK=1024, W=321, nvar=1, NQ=8192, nmask=5, sched=cmp_sched())]
    kw_sh = [np.concatenate([np.zeros((64, 512), NPBF), kwT[:, :S - 512]], axis=1), kwT]
    vw_aug = np.concatenate([vw.astype(NPBF), np.ones((S, 1), NPBF)], axis=1)
    vw_sh = [np.concatenate([np.zeros((512, 65), NPBF), vw_aug[:S - 512]], axis=0), vw_aug]
    va_win = [np.ascontiguousarray(v.reshape(S // 128, 128, 65).transpose(1, 0, 2).reshape(128, (S // 128) * 65)) for v in vw_sh]
    in_maps = []
    for c in range(NCORES):
        m = {"ident": IDENT}
        for s_ in range(3):
            KA, VA, QA, MK = dense_job_inputs(*djobs[3 * c + s_])
            m[f"ka{s_}"], m[f"va{s_}"], m[f"qa{s_}"], m[f"mk{s_}"] = KA, VA, QA, MK
        hh, par = c % 4, c // 4
        nqT = np.ascontiguousarray(nq[par_qidx(par), hh * 64:(hh + 1) * 64].T)
        m["ka3"], m["va3"], m["qa3"], m["mk3"] = np.ascontiguousarray(kw_sh[par]), va_win[par], nqT, MASKS_WIN
        m["ka4"], m["va4"], m["qa4"], m["mk4"] = kcmpT, va_cmp, nqT, MASKS_CMP[par]
        in_maps.append(m)
    r3 = run(get_nc("attn3", lambda: build_attn(jobs3, "a3")), in_maps)
    o_fox = np.zeros((6, S, 65), np.float32)
    o_moba = np.zeros((6, S, 65), np.float32)
    for jid, (br, h, par) in enumerate(djobs):
        (o_fox if br == "fox" else o_moba)[h, par_qidx(par)] = r3[jid // 3][f"o{jid % 3}"].T
    o_win = np.zeros((4, S, 65), np.float32)
    o_cmp = np.zeros((4, S, 321), np.float32)
    for c in range(NCORES):
        o_win[c % 4, par_qidx(c // 4)] = r3[c]["o3"].T
        o_cmp[c % 4, par_qidx(c // 4)] = r3[c]["o4"].T
    in_maps = []
    for c in range(NCORES):
        cm = np.zeros((128, 3, NTT, 256), np.float32)
        t = (c * TOK + np.arange(TOK)).reshape(NTT, 128).T
        cur = t // 64
        mm = np.arange(256)[None, None, :]
        cand = (mm >= 1) & (mm <= cur[:, :, None] - 2)
        forced = (mm == 0) | (mm == cur[:, :, None]) | (mm == cur[:, :, None] - 1)
        cm[:, 0] = np.where(cand, 0.0, -1e30)
        cm[:, 1] = cand
        cm[:, 2] = forced
        in_maps.append({"oc": np.ascontiguousarray(o_cmp[:, c * TOK:(c + 1) * TOK, 64:321]), "cm": cm})
    r4 = run(get_nc("p4", build_p4), in_maps)
    mbs = np.concatenate([r["mbs"] for r in r4], axis=0)
    spec_s = dict(Kc=128, NK=S, W=65, nvar=4, NQ=8192, nmask=8, sched=dense_sched(4))
    KAs = np.zeros((128, S), NPBF)
    KAs[0:64] = ksl.T
    KAs[64:128] = E_slc
    va_s = va_pack(vsl)
    in_maps = []
    for c in range(NCORES):
        h, par = c // 2, c % 2
        qi = par_qidx(par)
        QA = np.zeros((128, 4, 8192), NPBF)
        QA[0:64] = nq[qi, h * 64:(h + 1) * 64].T[:, None, :]
        QA[64:128] = mbs[qi].T.reshape(4, 64, 8192).transpose(1, 0, 2)
        in_maps.append({"ident": IDENT, "ka0": KAs, "va0": va_s, "qa0": QA.reshape(128, 4 * 8192), "mk0": MASKS_PAR[par]})
    r5 = run(get_nc("attn5", lambda: build_attn([spec_s], "a5")), in_maps)
    o_slc = np.zeros((4, S, 65), np.float32)
    for c in range(NCORES):
        o_slc[c // 2, par_qidx(c % 2)] = r5[c]["o0"].T
    wz = np.ascontiguousarray(p["w_in"][:, p5_col_perm()])
    wup = np.ascontiguousarray(np.concatenate([p["w_up_moba"], p["w_up_nsa"], p["w_up_fox"]], axis=0))
    om = np.ascontiguousarray(o_moba.transpose(1, 0, 2).reshape(S, 6 * 65))
    ofx = np.ascontiguousarray(o_fox.transpose(1, 0, 2).reshape(S, 6 * 65))
    on = np.ascontiguousarray(np.stack([o_cmp[:, :, 0:65], o_slc, o_win], axis=0).transpose(2, 0, 1, 3).reshape(S, 3, 4 * 65))
    in_maps = []
    for c in range(NCORES):
        sl = slice(c * TOK, (c + 1) * TOK)
        in_maps.append({"xs": np.ascontiguousarray(xl[sl]), "gnorm": np.ascontiguousarray(p["norm_g"][None, :]), "wz": wz,
                        "bgate": np.ascontiguousarray(p["b_gate"][None, :]),
                        "ng": np.ascontiguousarray(oF[sl]), "om": om[sl], "ofx": ofx[sl], "on": on[sl], "wup": wup, "wout": p["w_out"], "ident": IDENT})
    r6 = run(get_nc("p5", build_p5), in_maps)
    dbg = dict(o_fox=o_fox, o_moba=o_moba, o_win=o_win, o_cmp=o_cmp, o_slc=o_slc, mb=mb, mbs=mbs, kcmp=kcmp, vcmp=vcmp, csplit=csplit)
    return np.concatenate([r["out"] for r in r6], axis=0), dbg


PARAM_KEYS = ["norm_g", "w_in", "b_f", "b_gate", "moba_qk_g", "nsa_q_g", "nsa_k_g", "fox_qk_g", "cmp_pe", "cmp_w1", "cmp_w2",
              "w_up_moba", "w_up_nsa", "w_up_fox", "w_out"]


def kernel(**inputs):
    x = np.asarray(inputs["x"], np.float32)
    xl = np.ascontiguousarray(x[0])
    for l in range(2):
        p = {k: np.ascontiguousarray(np.asarray(inputs[k], np.float32)[l]) for k in PARAM_KEYS}
        xl, _ = layer_forward(xl, p)
    return xl[None].astype(np.float32)
```
